# Optimizing a Trainium2 kernel written in Bass

```python
import math
import jax
import jax.numpy as jnp
from jax import lax
import numpy as np

D_MODEL = 2048
BATCH = 2
SEQ = 4096
DEPTH = 2
DEC_BATCH = 128
DEC_SEQ = 1
PAST_LEN = 2048
PAGE_SIZE = 128

BRANCH = D_MODEL // 4
MIX = 4 * BRANCH
H_DIFF = 4
DQK = BRANCH // (2 * H_DIFF)
DV = 2 * DQK
Q_BLOCK = 128
R_HEAD = 64
H_RWKV = BRANCH // R_HEAD
W_LORA = 64
A_LORA = 64
V_LORA = 32
N_SHIFT = 3 * BRANCH + W_LORA + A_LORA
DECAY_SCALE = 0.6065306597
LNX_EPS = 64e-5
M_HEADDIM = 64
H_MAMBA = BRANCH // M_HEADDIM
M_GROUPS = 2
M_STATE = 128
M_CONV = 4
CONV_CH = BRANCH + 2 * M_GROUPS * M_STATE
SSD_CHUNK = 128
H_X = 4
D_X = BRANCH // H_X
N_MEM = 256
RMS_EPS = 1e-5
IN_SIZES = (BRANCH, BRANCH, BRANCH, BRANCH, N_SHIFT, BRANCH, BRANCH, CONV_CH, H_MAMBA, BRANCH, BRANCH)
N_IN = 8 * BRANCH + N_SHIFT + CONV_CH + H_MAMBA

kernel_name = 'hymba_diff_rwkv7_mamba2_memory_step'


def rmsnorm(x, g, eps=RMS_EPS):
    xf = x.astype(jnp.float32)
    y = xf * lax.rsqrt(jnp.mean(xf * xf, axis=-1, keepdims=True) + eps)
    return (y * g.astype(jnp.float32)).astype(x.dtype)


def split_in(p):
    offs = [int(o) for o in np.cumsum(IN_SIZES)[:-1]]
    return jnp.split(p, offs, axis=-1)


def alibi_slopes():
    return 2.0 ** (-8.0 * jnp.arange(1, H_DIFF + 1, dtype=jnp.float32) / H_DIFF)


def diff_attn(q, k, v, q_pos, k_pos, lam):
    s = jnp.einsum('bqhcd,bkhcd->bhcqk', q, k, preferred_element_type=jnp.float32) * (DQK ** -0.5)
    dist = (q_pos[:, None] - k_pos[None, :]).astype(jnp.float32)
    bias = jnp.where(dist >= 0, -alibi_slopes()[:, None, None] * dist, -jnp.inf)
    pr = jax.nn.softmax(s + bias[None, :, None], axis=-1)
    w = pr[:, :, 0] - lam * pr[:, :, 1]
    return jnp.einsum('bhqk,bkhv->bqhv', w.astype(v.dtype), v, preferred_element_type=jnp.float32)


def diff_attn_prompt(q, k, v, lam):
    b, t = q.shape[:2]
    nb = t // Q_BLOCK
    qb = jnp.moveaxis(q.reshape(b, nb, Q_BLOCK, H_DIFF, 2, DQK), 1, 0)
    k_pos = jnp.arange(t)

    def one_block(args):
        q_blk, i = args
        return diff_attn(q_blk, k, v, i * Q_BLOCK + jnp.arange(Q_BLOCK), k_pos, lam)

    o = lax.map(one_block, (qb, jnp.arange(nb)))
    return jnp.moveaxis(o, 0, 1).reshape(b, t, H_DIFF, DV)


def rwkv7_scan(r, w, k, v, a, bb, s0):
    def step(s, inp):
        r_t, w_t, k_t, v_t, a_t, b_t = inp
        sa = jnp.einsum('bhvk,bhk->bhv', s, a_t)
        s = s * w_t[:, :, None, :] + sa[..., None] * b_t[:, :, None, :] + v_t[..., None] * k_t[:, :, None, :]
        return s, jnp.einsum('bhvk,bhk->bhv', s, r_t)

    seq = tuple(jnp.moveaxis(z, 1, 0) for z in (r, w, k, v, a, bb))
    s, y = lax.scan(step, s0, seq)
    return jnp.moveaxis(y, 0, 1), s


def segsum(a):
    t = a.shape[-1]
    x = jnp.broadcast_to(a[..., :, None], a.shape + (t,))
    x = jnp.where(jnp.tril(jnp.ones((t, t), bool), -1), x, 0.0)
    cs = jnp.cumsum(x, axis=-2)
    return jnp.where(jnp.tril(jnp.ones((t, t), bool)), cs, -jnp.inf)


def ssd_scan(xdt, adt, bh, ch, s0, chunk):
    b, t, h, p = xdt.shape
    n = bh.shape[-1]
    nc = t // chunk
    X = xdt.reshape(b, nc, chunk, h, p)
    Bc = bh.reshape(b, nc, chunk, h, n)
    Cc = ch.reshape(b, nc, chunk, h, n)
    A = jnp.moveaxis(adt.reshape(b, nc, chunk, h), 3, 1)
    a_cs = jnp.cumsum(A, axis=-1)
    L = jnp.exp(segsum(A))
    y_diag = jnp.einsum('bhcls,bcshp->bclhp', jnp.einsum('bclhn,bcshn->bhcls', Cc, Bc) * L, X)
    decay_states = jnp.exp(a_cs[..., -1:] - a_cs)
    states = jnp.einsum('bclhn,bhcl,bclhp->bchpn', Bc, decay_states, X)
    states = jnp.concatenate([s0[:, None], states], axis=1)
    chunk_decay = jnp.exp(segsum(jnp.pad(a_cs[..., -1], ((0, 0), (0, 0), (1, 0)))))
    new_states = jnp.einsum('bhzc,bchpn->bzhpn', chunk_decay, states)
    y_off = jnp.einsum('bclhn,bchpn,bhcl->bclhp', Cc, new_states[:, :-1], jnp.exp(a_cs))
    return (y_diag + y_off).reshape(b, t, h, p), new_states[:, -1]


def memory_kv(mem, g, wk, wv):
    b = mem.shape[0]
    m = rmsnorm(mem, g)
    return (m @ wk).reshape(b, N_MEM, H_X, D_X), (m @ wv).reshape(b, N_MEM, H_X, D_X)


def hybrid_layer(x, l, lp, past_kv, rw_prev, rw_state, conv_buf, ssm_state, mem_k, mem_v, v_first):
    f32 = jnp.float32
    b, t, _ = x.shape
    xdtype = x.dtype
    h = rmsnorm(x, lp['norm_g'])
    p = h @ lp['w_in']
    dq, dk, dvv, dg, rs, rg, mz, mxbc, mdt, xq, xg = split_in(p)

    q = dq.reshape(b, t, H_DIFF, 2, DQK)
    k = dk.reshape(b, t, H_DIFF, 2, DQK)
    v = dvv.reshape(b, t, H_DIFF, DV)
    lam_init = 0.8 - 0.6 * math.exp(-0.3 * l)
    lam = (jnp.exp(jnp.sum(lp['lam_q1'] * lp['lam_k1'])) - jnp.exp(jnp.sum(lp['lam_q2'] * lp['lam_k2'])) + lam_init)
    if past_kv is None:
        o = diff_attn_prompt(q, k, v, lam)
    else:
        k_past, v_past = past_kv
        pos0 = k_past.shape[1]
        o = diff_attn(q, jnp.concatenate([k_past.astype(k.dtype), k], axis=1),
                      jnp.concatenate([v_past.astype(v.dtype), v], axis=1),
                      pos0 + jnp.arange(t), jnp.arange(pos0 + t), lam)
    o = rmsnorm(o, lp['subln_g']) * (1.0 - lam_init)
    out_a = o.reshape(b, t, BRANCH) * jax.nn.silu(dg.astype(f32))

    rs32 = rs.astype(f32)
    prev = jnp.concatenate([rw_prev[:, None].astype(f32), rs32[:, :-1]], axis=1)
    xs = rs32 + (prev - rs32) * lp['shift_mu']
    r, kr, vr, wlo, alo = jnp.split(xs, [BRANCH, 2 * BRANCH, 3 * BRANCH, 3 * BRANCH + W_LORA], axis=-1)
    logw = -DECAY_SCALE * jax.nn.sigmoid(lp['w0'] + jnp.tanh(wlo) @ lp['w_up'])
    a = jax.nn.sigmoid(lp['a0'] + alo @ lp['a_up'])
    if l == 0:
        v_first = vr
    else:
        vr = vr + (v_first - vr) * jax.nn.sigmoid(lp['vres_0'] + (vr @ lp['vres_a']) @ lp['vres_b'])
    hd = lambda z: z.reshape(b, t, H_RWKV, R_HEAD)
    kk = hd(kr * lp['k_k'])
    kk = kk / jnp.maximum(jnp.sqrt(jnp.sum(kk * kk, axis=-1, keepdims=True)), 1e-12)
    kr = kr * (1.0 + (a - 1.0) * lp['k_a'])
    rh, kh, vh, ah = hd(r), hd(kr), hd(vr), hd(a)
    y, rw_state_new = rwkv7_scan(rh, jnp.exp(hd(logw)), kh, vh, -kk, kk * ah, rw_state.astype(f32))
    mu = jnp.mean(y, axis=-1, keepdims=True)
    var = jnp.mean(jnp.square(y - mu), axis=-1, keepdims=True)
    y = ((y - mu) * lax.rsqrt(var + LNX_EPS)).reshape(b, t, BRANCH) * lp['lnx_g'] + lp['lnx_b']
    y = y + (jnp.sum(rh * kh * lp['r_k'], axis=-1, keepdims=True) * vh).reshape(b, t, BRANCH)
    out_b = y * jax.nn.silu(rg.astype(f32))
    new_rw_prev = rs[:, -1]

    buf = jnp.concatenate([conv_buf.astype(mxbc.dtype), mxbc], axis=1)
    xbc = lp['conv_b']
    for j in range(M_CONV):
        xbc = xbc + buf[:, j:j + t].astype(f32) * lp['conv_w'][j]
    xbc = jax.nn.silu(xbc)
    xm, bm, cm = jnp.split(xbc, [BRANCH, BRANCH + M_GROUPS * M_STATE], axis=-1)
    dt = jax.nn.softplus(mdt.astype(f32) + lp['dt_bias'])
    A = -jnp.exp(lp['a_log'])
    xh = xm.reshape(b, t, H_MAMBA, M_HEADDIM)
    rep = H_MAMBA // M_GROUPS
    bh = jnp.repeat(bm.reshape(b, t, M_GROUPS, M_STATE), rep, axis=2)
    chh = jnp.repeat(cm.reshape(b, t, M_GROUPS, M_STATE), rep, axis=2)
    chunk = SSD_CHUNK if t % SSD_CHUNK == 0 else t
    ym, ssm_new = ssd_scan(xh * dt[..., None], A * dt, bh, chh, ssm_state.astype(f32), chunk)
    ym = ym + lp['d_skip'][:, None] * xh
    out_m = rmsnorm(ym.reshape(b, t, BRANCH) * jax.nn.silu(mz.astype(f32)), lp['gnorm_g'])
    new_conv = buf[:, -(M_CONV - 1):]

    qx = xq.reshape(b, t, H_X, D_X)
    s = jnp.einsum('bqhd,bmhd->bhqm', qx, mem_k.astype(qx.dtype), preferred_element_type=f32) * (D_X ** -0.5)
    pm = jax.nn.softmax(s, axis=-1)
    ox = jnp.einsum('bhqm,bmhd->bqhd', pm, mem_v.astype(f32))
    out_x = ox.reshape(b, t, BRANCH) * jax.nn.silu(xg.astype(f32))

    mixed = jnp.concatenate([out_a, out_b, out_m, out_x], axis=-1).astype(xdtype)
    x = x + mixed @ lp['w_out']
    return x, (k, v, new_rw_prev, rw_state_new, new_conv, ssm_new), v_first


def setup_inputs(seed: int = 0) -> dict:
    key = jax.random.key(seed)
    ks = iter(jax.random.split(key, 64))
    f32 = jnp.float32

    def nrm(shape, scale):
        return jax.random.normal(next(ks), shape, f32) * scale

    def gain(shape):
        return 1.0 + nrm(shape, 0.02)

    n_pages = PAST_LEN // PAGE_SIZE
    n_pool = (DEC_BATCH * n_pages * 5) // 4
    perm = jax.random.permutation(next(ks), n_pool)[: DEC_BATCH * n_pages]
    page_table = perm.reshape(DEC_BATCH, n_pages).astype(jnp.int32)

    dt0 = jnp.exp(jax.random.uniform(next(ks), (DEPTH, H_MAMBA), f32) * (math.log(0.1) - math.log(0.001)) + math.log(0.001))
    dt_bias = dt0 + jnp.log(-jnp.expm1(-dt0))
    a_log = jnp.log(jax.random.uniform(next(ks), (DEPTH, H_MAMBA), f32, 1.0, 16.0))

    return {
        'x_prompt': nrm((BATCH, SEQ, D_MODEL), 1.0),
        'x_sample': nrm((DEC_BATCH, DEC_SEQ, D_MODEL), 1.0),
        'cache_diff_k': nrm((DEPTH, n_pool, PAGE_SIZE, H_DIFF, 2, DQK), 1.0),
        'cache_diff_v': nrm((DEPTH, n_pool, PAGE_SIZE, H_DIFF, DV), 1.0),
        'cache_mem_k': nrm((DEPTH, DEC_BATCH, N_MEM, H_X, D_X), 1.0),
        'cache_mem_v': nrm((DEPTH, DEC_BATCH, N_MEM, H_X, D_X), 1.0),
        'state_rwkv_shift': nrm((DEPTH, DEC_BATCH, N_SHIFT), 1.0),
        'state_rwkv': nrm((DEPTH, DEC_BATCH, H_RWKV, R_HEAD, R_HEAD), 0.3),
        'state_conv': nrm((DEPTH, DEC_BATCH, M_CONV - 1, CONV_CH), 1.0),
        'state_ssm': nrm((DEPTH, DEC_BATCH, H_MAMBA, M_HEADDIM, M_STATE), 0.3),
        'page_table': page_table,
        'mem_prompt': nrm((BATCH, N_MEM, D_MODEL), 1.0),
        'norm_g': gain((DEPTH, D_MODEL)),
        'w_in': nrm((DEPTH, D_MODEL, N_IN), D_MODEL ** -0.5),
        'w_out': nrm((DEPTH, MIX, D_MODEL), MIX ** -0.5),
        'lam_q1': nrm((DEPTH, DQK), 0.1),
        'lam_k1': nrm((DEPTH, DQK), 0.1),
        'lam_q2': nrm((DEPTH, DQK), 0.1),
        'lam_k2': nrm((DEPTH, DQK), 0.1),
        'subln_g': gain((DEPTH, DV)),
        'shift_mu': jax.random.uniform(next(ks), (DEPTH, N_SHIFT), f32),
        'w0': nrm((DEPTH, BRANCH), 0.5),
        'w_up': nrm((DEPTH, W_LORA, BRANCH), W_LORA ** -0.5),
        'a0': nrm((DEPTH, BRANCH), 0.5),
        'a_up': nrm((DEPTH, A_LORA, BRANCH), A_LORA ** -0.5),
        'k_k': 0.85 + nrm((DEPTH, BRANCH), 0.1),
        'k_a': 1.0 + nrm((DEPTH, BRANCH), 0.1),
        'r_k': nrm((DEPTH, H_RWKV, R_HEAD), 0.1),
        'lnx_g': gain((DEPTH, BRANCH)),
        'lnx_b': nrm((DEPTH, BRANCH), 0.02),
        'vres_0': nrm((DEPTH - 1, BRANCH), 0.5),
        'vres_a': nrm((DEPTH - 1, BRANCH, V_LORA), BRANCH ** -0.5),
        'vres_b': nrm((DEPTH - 1, V_LORA, BRANCH), V_LORA ** -0.5),
        'conv_w': nrm((DEPTH, M_CONV, CONV_CH), M_CONV ** -0.5),
        'conv_b': nrm((DEPTH, CONV_CH), 0.02),
        'dt_bias': dt_bias,
        'a_log': a_log,
        'd_skip': 1.0 + nrm((DEPTH, H_MAMBA), 0.1),
        'gnorm_g': gain((DEPTH, BRANCH)),
        'mem_norm_g': gain((DEPTH, D_MODEL)),
        'w_mk': nrm((DEPTH, D_MODEL, BRANCH), D_MODEL ** -0.5),
        'w_mv': nrm((DEPTH, D_MODEL, BRANCH), D_MODEL ** -0.5),
        'final_g': gain((D_MODEL,)),
    }


def reference(x_prompt, x_sample, cache_diff_k, cache_diff_v, cache_mem_k, cache_mem_v, state_rwkv_shift,
              state_rwkv, state_conv, state_ssm, page_table, mem_prompt, norm_g, w_in, w_out, lam_q1, lam_k1,
              lam_q2, lam_k2, subln_g, shift_mu, w0, w_up, a0, a_up, k_k, k_a, r_k, lnx_g, lnx_b, vres_0,
              vres_a, vres_b, conv_w, conv_b, dt_bias, a_log, d_skip, gnorm_g, mem_norm_g, w_mk, w_mv, final_g):
    f32 = jnp.float32
    bp = x_prompt.shape[0]
    db = x_sample.shape[0]
    n_pages = page_table.shape[1]
    past_len = n_pages * PAGE_SIZE
    xp, xs = x_prompt, x_sample
    vf_p, vf_s = None, None
    pk, pv, pmk, pmv, prs, prw, pcv, pss = [], [], [], [], [], [], [], []
    sk, sv, srs, srw, scv, sss = [], [], [], [], [], []
    for l in range(DEPTH):
        lp = {
            'norm_g': norm_g[l], 'w_in': w_in[l], 'w_out': w_out[l],
            'lam_q1': lam_q1[l].astype(f32), 'lam_k1': lam_k1[l].astype(f32),
            'lam_q2': lam_q2[l].astype(f32), 'lam_k2': lam_k2[l].astype(f32),
            'subln_g': subln_g[l], 'shift_mu': shift_mu[l].astype(f32),
            'w0': w0[l].astype(f32), 'w_up': w_up[l].astype(f32),
            'a0': a0[l].astype(f32), 'a_up': a_up[l].astype(f32),
            'k_k': k_k[l].astype(f32), 'k_a': k_a[l].astype(f32), 'r_k': r_k[l].astype(f32),
            'lnx_g': lnx_g[l].astype(f32), 'lnx_b': lnx_b[l].astype(f32),
            'conv_w': conv_w[l].astype(f32), 'conv_b': conv_b[l].astype(f32),
            'dt_bias': dt_bias[l].astype(f32), 'a_log': a_log[l].astype(f32),
            'd_skip': d_skip[l].astype(f32), 'gnorm_g': gnorm_g[l],
        }
        if l > 0:
            lp['vres_0'] = vres_0[l - 1].astype(f32)
            lp['vres_a'] = vres_a[l - 1].astype(f32)
            lp['vres_b'] = vres_b[l - 1].astype(f32)

        mk, mv = memory_kv(mem_prompt, mem_norm_g[l], w_mk[l], w_mv[l])
        xp, st, vf_p = hybrid_layer(
            xp, l, lp, None,
            jnp.zeros((bp, N_SHIFT), xp.dtype),
            jnp.zeros((bp, H_RWKV, R_HEAD, R_HEAD), f32),
            jnp.zeros((bp, M_CONV - 1, CONV_CH), xp.dtype),
            jnp.zeros((bp, H_MAMBA, M_HEADDIM, M_STATE), f32),
            mk, mv, vf_p)
        pk.append(st[0]); pv.append(st[1]); pmk.append(mk); pmv.append(mv)
        prs.append(st[2].astype(state_rwkv_shift.dtype)); prw.append(st[3].astype(state_rwkv.dtype))
        pcv.append(st[4].astype(state_conv.dtype)); pss.append(st[5].astype(state_ssm.dtype))

        k_past = cache_diff_k[l][page_table].reshape(db, past_len, H_DIFF, 2, DQK)
        v_past = cache_diff_v[l][page_table].reshape(db, past_len, H_DIFF, DV)
        xs, ss, vf_s = hybrid_layer(
            xs, l, lp, (k_past, v_past), state_rwkv_shift[l], state_rwkv[l], state_conv[l], state_ssm[l],
            cache_mem_k[l], cache_mem_v[l], vf_s)
        sk.append(ss[0]); sv.append(ss[1])
        srs.append(ss[2].astype(state_rwkv_shift.dtype)); srw.append(ss[3].astype(state_rwkv.dtype))
        scv.append(ss[4].astype(state_conv.dtype)); sss.append(ss[5].astype(state_ssm.dtype))

    y_prompt = rmsnorm(xp, final_g)
    y_sample = rmsnorm(xs, final_g)
    return (y_prompt, y_sample,
            jnp.stack(pk), jnp.stack(pv), jnp.stack(pmk), jnp.stack(pmv),
            jnp.stack(prs), jnp.stack(prw), jnp.stack(pcv), jnp.stack(pss),
            jnp.stack(sk), jnp.stack(sv), jnp.stack(srs), jnp.stack(srw), jnp.stack(scv), jnp.stack(sss))
```

```python
import contextlib
import numpy as np
import ml_dtypes
import concourse.bass as bass
import concourse.mybir as mybir
from concourse.bass_utils import run_bass_kernel_spmd

F32 = mybir.dt.float32
BF16 = mybir.dt.bfloat16
I32 = mybir.dt.int32
AF = mybir.ActivationFunctionType
ALU = mybir.AluOpType
AX = mybir.AxisListType
NPBF = ml_dtypes.bfloat16

D = 2048
NDC = 16
BR = 512
N_IN = 6792
NMEM = 256
RMS_EPS = 1e-5
LNX_EPS = 64e-5
DECAY = 0.6065306597
NEG = -30000.0

C_DQ, C_TM1, C_XQ, NCOL_AX = 0, 128, 640, 768
C_TM2, C_R, C_K, C_V, C_WA, C_XM, C_BM, C_CM, C_VF = 0, 258, 386, 514, 642, 770, 898, 1026, 1154
NCOL_BC0, NCOL_BC1 = 1154, 1666
TM_DK, TM_DV, TM_DG, TM_XG, TM_RG, TM_MZ, TM_DT = 0, 128, 256, 384, 512, 640, 768
NTM = 770
NMIX = 514


class Cfg:
    def __init__(self, T, NSH, NPG, NPOOL):
        self.T, self.NSH, self.NPG, self.NPOOL = T, NSH, NPG, NPOOL
        self.TB = 256
        self.NBLK = T // 256
        self.NTOK = T + NSH
        self.PAST = NPG * 128
        self.CHR = 640 if T >= 1024 else 256


ENG = ("pe", "act", "dve", "pool", "sp")
SEM_EPOCH = 20000
N_DMA_SLOTS = 32


class _Buf:
    __slots__ = ("last_w", "readers")

    def __init__(self):
        self.last_w = None
        self.readers = []


class _Op:
    __slots__ = ("eng", "idx", "fn", "waits", "is_dma", "slot", "slot_cnt", "inc")


class Sched:
    def __init__(self, nc):
        self.nc = nc
        self.ops = {e: [] for e in ENG}
        self.seen = {e: {} for e in ENG}
        self.ccount = {e: 0 for e in ENG}
        self.dma_cnt = [0] * N_DMA_SLOTS
        self.dma_rr = 0
        self.dma_rr_pool = 0
        self.bufs = {}

    def _b(self, name):
        b = self.bufs.get(name)
        if b is None:
            b = self.bufs[name] = _Buf()
        return b

    def _deps(self, reads, writes):
        deps = []
        for n in reads:
            b = self._b(n)
            if b.last_w is not None:
                deps.append(b.last_w)
        for n in writes:
            b = self._b(n)
            if b.last_w is not None:
                deps.append(b.last_w)
            deps.extend(b.readers)
        return deps

    def _commit(self, op, reads, writes):
        for n in reads:
            self._b(n).readers.append(op)
        for n in writes:
            b = self._b(n)
            b.last_w = op
            b.readers = []

    def _waits(self, eng, deps):
        need = {}
        for d in deps:
            if d.is_dma:
                key = ("d", d.slot)
                val = d.inc * d.slot_cnt
            else:
                if d.eng == "pe" and eng == "pe":
                    continue
                ep = d.idx // SEM_EPOCH
                key = ("e", d.eng, ep)
                val = d.idx - ep * SEM_EPOCH + 1
            if need.get(key, 0) < val:
                need[key] = val
        out = []
        seen = self.seen[eng]
        for key, val in need.items():
            if seen.get(key, 0) >= val:
                continue
            seen[key] = val
            out.append((key, val))
        return out

    def op(self, eng, fn, reads=(), writes=()):
        o = _Op()
        o.eng = eng
        o.idx = self.ccount[eng]
        self.ccount[eng] += 1
        o.fn = fn
        o.is_dma = False
        o.waits = self._waits(eng, self._deps(reads, writes))
        self.ops[eng].append(o)
        self._commit(o, reads, writes)
        return o

    def dma(self, q, fn, reads=(), writes=(), inc=16):
        o = _Op()
        o.eng = q
        o.fn = fn
        o.is_dma = True
        o.inc = inc
        o.idx = -1
        if inc == 16:
            if q == "pool":
                s = 20 + self.dma_rr_pool
                self.dma_rr_pool = (self.dma_rr_pool + 1) % 10
            else:
                s = self.dma_rr
                self.dma_rr = (self.dma_rr + 1) % 20
        else:
            s = N_DMA_SLOTS - 1
        o.slot = s
        waits = self._waits(q, self._deps(reads, writes))
        if self.dma_cnt[s] > 0:
            key = ("d", s)
            val = inc * self.dma_cnt[s]
            if self.seen[q].get(key, 0) < val:
                self.seen[q][key] = val
                waits.append((key, val))
        self.dma_cnt[s] += 1
        o.slot_cnt = self.dma_cnt[s]
        o.waits = waits
        self.ops[q].append(o)
        self._commit(o, reads, writes)
        return o

    def emit(self):
        nc = self.nc
        n_ep = {e: (self.ccount[e] // SEM_EPOCH + 1) for e in ENG}
        with contextlib.ExitStack() as st:
            esem = {}
            for e in ENG:
                for ep in range(n_ep[e]):
                    esem[(e, ep)] = st.enter_context(nc.semaphore(f"s_{e}_{ep}"))
            dsem = [st.enter_context(nc.semaphore(f"d_{i}")) for i in range(N_DMA_SLOTS)]
            block = st.enter_context(nc.Block())

            def semof(key):
                return dsem[key[1]] if key[0] == "d" else esem[(key[1], key[2])]

            def run(e, final=False):
                def body(engobj):
                    for o in self.ops[e]:
                        for key, val in o.waits:
                            engobj.wait_ge(semof(key), val)
                        ins = o.fn(engobj)
                        if o.is_dma:
                            if o.inc == 16:
                                ins.then_inc(dsem[o.slot], 16)
                            else:
                                ins.then_inc(dsem[o.slot])
                        else:
                            ins.then_inc(esem[(e, o.idx // SEM_EPOCH)], 1)
                    if final:
                        for s in range(N_DMA_SLOTS):
                            if self.dma_cnt[s] > 0:
                                inc = 16 if s != N_DMA_SLOTS - 1 else 1
                                engobj.wait_ge(dsem[s], inc * self.dma_cnt[s])
                return body

            block.tensor(run("pe"))
            block.scalar(run("act"))
            block.vector(run("dve"))
            block.gpsimd(run("pool"))
            block.sync(run("sp", final=True))


_REGIONS = {}
_ESZ = {F32: 4, BF16: 2, I32: 4}


def _names(*aps):
    out = []
    for a in aps:
        if a is None or isinstance(a, (int, float)):
            continue
        nm = a.tensor.name
        regs = _REGIONS.get(nm)
        if regs is None:
            out.append(nm)
            continue
        apl = a.ap
        esz = _ESZ[a.dtype]
        pstride = apl[0][0]
        off = int(a.offset) % pstride if pstride else int(a.offset)
        ext = sum(st_ * (cn_ - 1) for st_, cn_ in apl[1:]) + 1
        b0, b1 = off * esz, (off + ext) * esz
        inside = False
        for key, r0, r1 in regs:
            if b0 < r1 and b1 > r0:
                out.append(key)
            if r0 <= b0 and b1 <= r1:
                inside = True
        if not inside:
            out.append(nm)
    return out


class Prog:
    def __init__(self, cfg, dbg=False):
        self.cfg = cfg
        self.dbg = dbg
        self.nc = bass.Bass("TRN2", target_bir_lowering=False)
        self.S = Sched(self.nc)
        self.st = contextlib.ExitStack()
        self.ins = {}
        self.outs = {}
        self.ps_rr = 0
        self.act_rr = 0

    def inp(self, name, shape, dt=F32):
        t = self.nc.dram_tensor(name, list(shape), dt, kind="ExternalInput").ap()
        self.ins[name] = (tuple(shape), dt)
        return t

    def outp(self, name, shape, dt=F32):
        t = self.nc.dram_tensor(name, list(shape), dt, kind="ExternalOutput").ap()
        self.outs[name] = (tuple(shape), dt)
        return t

    def dram(self, name, shape, dt):
        return self.nc.dram_tensor(name, list(shape), dt).ap()

    def sb(self, name, shape, dt=F32):
        return self.st.enter_context(self.nc.sbuf_tensor(name, list(shape), dt))

    def mm(self, out, lhsT, rhs, start=True, stop=True):
        rd = _names(lhsT, rhs) + ([] if start else _names(out))
        self.S.op("pe", lambda e: e.matmul(out, lhsT=lhsT, rhs=rhs, start=start, stop=stop),
                  reads=rd, writes=_names(out))

    def tr(self, out, in_, ident):
        self.S.op("pe", lambda e: e.transpose(out=out, in_=in_, identity=ident),
                  reads=_names(in_, ident), writes=_names(out))

    def act(self, out, in_, func, bias=0.0, scale=1.0, accum_out=None):
        kw = {}
        if accum_out is not None:
            kw["accum_out"] = accum_out
        self.S.op("act", lambda e: e.activation(out=out, in_=in_, func=func, bias=bias, scale=scale, **kw),
                  reads=_names(in_, bias, scale), writes=_names(out, accum_out))

    def tt(self, out, in0, in1, op, eng="dve"):
        self.S.op(eng, lambda e: e.tensor_tensor(out=out, in0=in0, in1=in1, op=op),
                  reads=_names(in0, in1), writes=_names(out))

    def ts(self, out, in0, s1, op0, s2=None, op1=None, eng="dve", accum_out=None):
        kw = {}
        if accum_out is not None:
            kw["accum_out"] = accum_out
        if op1 is None:
            self.S.op(eng, lambda e: e.tensor_scalar(out=out, in0=in0, scalar1=s1, scalar2=None, op0=op0, **kw),
                      reads=_names(in0, s1), writes=_names(out, accum_out))
        else:
            self.S.op(eng, lambda e: e.tensor_scalar(out=out, in0=in0, scalar1=s1, scalar2=s2, op0=op0, op1=op1, **kw),
                      reads=_names(in0, s1, s2), writes=_names(out, accum_out))

    def stt(self, out, in0, scalar, in1, op0, op1):
        self.S.op("dve", lambda e: e.scalar_tensor_tensor(out=out, in0=in0, scalar=scalar, in1=in1, op0=op0, op1=op1),
                  reads=_names(in0, scalar, in1), writes=_names(out))

    def cp(self, out, in_, eng="dve"):
        if eng == "act":
            self.act(out, in_, AF.Copy)
        else:
            self.S.op(eng, lambda e: e.tensor_copy(out=out, in_=in_), reads=_names(in_), writes=_names(out))

    def red(self, out, in_, op=ALU.add, axis=AX.X):
        self.S.op("dve", lambda e: e.tensor_reduce(out=out, in_=in_, axis=axis, op=op),
                  reads=_names(in_), writes=_names(out))

    def recip(self, out, in_):
        self.S.op("dve", lambda e: e.reciprocal(out=out, in_=in_), reads=_names(in_), writes=_names(out))

    def memset(self, ap, val, eng="pool"):
        self.S.op(eng, lambda e: e.memset(ap, val), writes=_names(ap))

    def scan(self, out, d0, d1, init, op0, op1):
        self.S.op("dve", lambda e: e.tensor_tensor_scan(out=out, data0=d0, data1=d1, initial=init, op0=op0, op1=op1),
                  reads=_names(d0, d1, init), writes=_names(out))

    def ld(self, out, in_, q="sp", slow=False):
        kw = {"allow_slow_non_contiguous": True} if slow else {}
        self.S.dma(q, lambda e: e.dma_start(out=out, in_=in_, **kw), reads=_names(in_), writes=_names(out))

    def stq(self, out, in_, slow=False):
        self.ld(out, in_, q="pool", slow=slow)

    def gather(self, out, in_, idx_ap):
        self.S.dma("pool", lambda e: e.indirect_dma_start(
            out=out, out_offset=None, in_=in_, in_offset=bass.IndirectOffsetOnAxis(ap=idx_ap, axis=0)),
            reads=_names(in_, idx_ap), writes=_names(out))

    def allgather(self, out, in_, groups):
        self.S.dma("pool", lambda e: e.collective_compute(
            "AllGather", ALU.bypass, replica_groups=groups, ins=[in_.opt()], outs=[out.opt()]),
            reads=_names(in_), writes=_names(out), inc=1)

    def bank(self):
        b = self.PS[self.ps_rr]
        self.ps_rr = (self.ps_rr + 1) % 4
        return b

    def rsqrt(self, out, in_, mul, add):
        self.ts(out, in_, mul, ALU.mult, add, ALU.add)
        self.act(out, out, AF.Sqrt)
        self.recip(out, out)


PP_NG, PP_MG, PP_MU, PP_MUVF, PP_W0, PP_A0, PP_KK, PP_KA, PP_RK, PP_VR0, PP_CW, PP_CB, NPP = 0, 16, 32, 36, 40, 41, 42, 43, 44, 45, 46, 58, 64
PB_SUB, PB_LNG, PB_LNB, PB_GN, PB_DTB, PB_ALOG, PB_DSK, PB_LAM, PB_FG, NPB = 0, 128, 256, 384, 512, 514, 516, 518, 774, 1286


def build_program(cfg, stage=99, dbg=False):
    P = Prog(cfg, dbg)
    T, NSH, NTOK, NBLK, NPG = cfg.T, cfg.NSH, cfg.NTOK, cfg.NBLK, cfg.NPG
    NT = T // 128
    groups = [[0, 1, 2, 3], [4, 5, 6, 7]]
    sb, mm, tr, act, tt, ts, stt, cp, red = P.sb, P.mm, P.tr, P.act, P.tt, P.ts, P.stt, P.cp, P.red

    xp = P.inp("xp", [T, D])
    xs = P.inp("xs", [NSH, D])
    xres = P.inp("xres", [NTOK, 512])
    mem = P.inp("mem", [NMEM, D])
    pt = P.inp("pt", [NSH, NPG], I32)
    c_identb = P.inp("c_identb", [128, 128], BF16)
    c_identf = P.inp("c_identf", [128, 128])
    c_negtri = P.inp("c_negtri", [128, 128])
    c_negmask = P.inp("c_negmask", [128, 1024], BF16)
    c_alibl = P.inp("c_alibl", [3, 128], BF16)
    c_alibr = P.inp("c_alibr", [3, 512], BF16)
    c_mask4 = P.inp("c_mask4", [128, 512])
    c_maskl = P.inp("c_maskl", [128, 128])
    c_utri = P.inp("c_utri", [128, 128])
    c_bones = P.inp("c_bones", [128, 128])
    c_bind = P.inp("c_bind", [128, 2])
    c_ind2 = P.inp("c_ind2", [2, 128])
    c_sbias = P.inp("c_sbias", [128, 17])
    c_oiota = P.inp("c_oiota", [128, 1])
    c_abias = P.inp("c_abias", [128, 64])
    CHR = cfg.CHR
    chunks = [(c0, min(CHR, NTOK - c0)) for c0 in range(0, NTOK, CHR)]
    L = []
    for l in range(2):
        d = {}
        d["ncol_bc"] = NCOL_BC0 if l == 0 else NCOL_BC1
        d["winax"] = P.inp(f"winax{l}", [D, NCOL_AX])
        d["winbc"] = P.inp(f"winbc{l}", [D, d["ncol_bc"]])
        d["wout"] = P.inp(f"wout{l}", [D, 512])
        d["wmk"] = P.inp(f"wmk{l}", [D, 128])
        d["wmv"] = P.inp(f"wmv{l}", [D, 128])
        d["wua"] = P.inp(f"wua{l}", [128, 128])
        if l == 1:
            d["vra"] = P.inp("vra", [512, 32])
            d["vrb"] = P.inp("vrb", [32, 128])
        d["pp"] = P.inp(f"pp{l}", [128, NPP])
        d["pb"] = P.inp(f"pb{l}", [1, NPB])
        d["ck"] = P.inp(f"ck{l}", [cfg.NPOOL * (128 // NPG), NPG * 128])
        d["cv"] = P.inp(f"cv{l}", [cfg.NPOOL * (128 // NPG), NPG * 128])
        d["cmk"] = P.inp(f"cmk{l}", [NSH, 256, 128])
        d["cmv"] = P.inp(f"cmv{l}", [NSH, 256, 128])
        d["srs"] = P.inp(f"srs{l}", [NSH, 1024 if l == 1 else 512])
        d["srw"] = P.inp(f"srw{l}", [NSH, 2, 64, 64])
        d["scv"] = P.inp(f"scv{l}", [NSH, 3, 384])
        d["ssm"] = P.inp(f"ssm{l}", [NSH, 2, 64, 128])
        d["o_k"] = P.outp(f"o_k{l}", [NTOK, 128])
        d["o_v"] = P.outp(f"o_v{l}", [NTOK, 128])
        d["o_mk"] = P.outp(f"o_mk{l}", [NMEM, 128])
        d["o_mv"] = P.outp(f"o_mv{l}", [NMEM, 128])
        d["o_rs"] = P.outp(f"o_rs{l}", [4, 128, 1 + NSH])
        d["o_rw"] = P.outp(f"o_rw{l}", [1 + NSH, 2, 64, 64])
        d["o_cv"] = P.outp(f"o_cv{l}", [1 + NSH, 3, 384])
        d["o_ss"] = P.outp(f"o_ss{l}", [1 + NSH, 2, 64, 128])
        d["mixb"] = P.dram(f"mixb{l}", [NTOK, NMIX], BF16)
        d["mixg"] = [P.dram(f"mixg{l}_{ci}", [4 * cr, NMIX], BF16) for ci, (c0, cr) in enumerate(chunks)]
        L.append(d)
    o_y = P.outp("o_y", [NTOK, 512])
    x1s = P.dram("x1s", [NTOK, 512], F32)
    x1b = P.dram("x1b", [NTOK, 512], BF16)
    x1g = [P.dram(f"x1g_{ci}", [4 * cr, 512], BF16) for ci, (c0, cr) in enumerate(chunks)]
    ssqb = P.dram("ssqb", [NTOK, 2], F32)
    ssqg = P.dram("ssqg", [4 * NTOK, 2], F32)
    vfirst = P.dram("vfirst", [128, NTOK], F32)
    hTs = P.dram("hTs", [NBLK + 1, 128, NDC * 256], BF16)
    scr = P.dram("scr", [8, NSH, 128], F32)
    scrq = P.dram("scrq", [2, NSH, 128], F32)
    if P.dbg:
        dbg_mix = [P.outp(f"dbg_mix{l}", [NTOK, NMIX], BF16) for l in range(2)]

    P.PS = [P.st.enter_context(P.nc.psum_tensor(f"ps{i}", [128, 512], F32)) for i in range(8)]
    PS = P.PS
    identb = sb("identb", [128, 128], BF16)
    identf = sb("identf", [128, 128])
    negtri = sb("negtri", [128, 128])
    negmask = sb("negmask", [128, 1024], BF16)
    alibl = sb("alibl", [3, 128], BF16)
    alibr = sb("alibr", [3, 512], BF16)
    mask4 = sb("mask4", [128, 512])
    maskl = sb("maskl", [128, 128])
    utri = sb("utri", [128, 128])
    bones = sb("bones", [128, 128])
    bind = sb("bind", [128, 2])
    ind2 = sb("ind2", [2, 128])
    sbias = sb("sbias", [128, 17])
    oiota = sb("oiota", [128, 1])
    abias = sb("abias", [128, 64])
    onesf = sb("onesf", [128, 128])
    onesb = sb("onesb", [128, 128], BF16)
    for dst, src in ((identb, c_identb), (identf, c_identf), (negtri, c_negtri), (negmask, c_negmask),
                     (alibl, c_alibl), (alibr, c_alibr), (mask4, c_mask4), (maskl, c_maskl), (utri, c_utri),
                     (bones, c_bones), (bind, c_bind), (ind2, c_ind2), (sbias, c_sbias), (oiota, c_oiota), (abias, c_abias)):
        P.ld(dst[:], src[:, :])
    P.memset(onesf[:], 1.0)
    P.memset(onesb[:], 1.0)

    Wb = sb("Wb", [128, NDC, NCOL_BC1], BF16)
    wst2 = sb("wst2", [128, 1024])
    woutb = Wb[:, :, 0:512]
    wmkb = Wb[:, :, 0:128]
    wmvb = Wb[:, :, 128:256]
    wuab = sb("wuab", [128, 128], BF16)
    vrab = sb("vrab", [128, 4, 32], BF16)
    vrbb = sb("vrbb", [32, 128], BF16)
    pp = sb("pp", [128, NPP])
    pb = sb("pb", [128, NPB])
    ppd = sb("ppd", [128, 16])
    pbd = sb("pbd", [128, 8])
    gsub = sb("gsub", [128, 128])
    hT = sb("hT", [128, NDC, 256], BF16)
    xt = sb("xt", [128, D])
    xtb = sb("xtb", [128, D], BF16)
    xn = xtb
    st1 = sb("st1", [128, 8])
    TM = sb("TM", [128, 2, NTM])
    mixt = sb("mixt", [128, 2, NMIX], BF16)
    P.memset(mixt[:], 0.0)

    def load_w(src, ncol, dst_c0=0):
        k = 0
        for dc in range(NDC):
            c = 0
            while c < ncol:
                n = min(1024, ncol - c)
                stg = xt[:, 0:n] if k % 2 == 0 else wst2[:, 0:n]
                P.ld(stg, src[dc * 128:(dc + 1) * 128, c:c + n])
                cp(Wb[:, dc, dst_c0 + c:dst_c0 + c + n], stg, eng=("dve", "act", "pool")[k % 3])
                k += 1
                c += n

    def load_params(l):
        d = L[l]
        P.ld(wst2[:, :128], d["wua"][:, :])
        cp(wuab[:], wst2[:, :128])
        if l == 1:
            P.ld(wst2[:, 128:256].rearrange("p (c w) -> p c w", w=32), d["vra"].rearrange("(c p) w -> p c w", p=128))
            cp(vrab[:], wst2[:, 128:256].rearrange("p (c w) -> p c w", w=32))
            P.ld(wst2[:32, 256:384], d["vrb"][:, :])
            cp(vrbb[:], wst2[:32, 256:384])
        P.ld(pp[:], d["pp"][:, :])
        P.ld(pb[:], d["pb"][0:1, :].broadcast_to([128, NPB]))
        ts(ppd[:, 0:8], pp[:, PP_MU:PP_MU + 8], -1.0, ALU.mult, 1.0, ALU.add)
        ts(ppd[:, 8:9], pp[:, PP_KA:PP_KA + 1], -1.0, ALU.mult, 1.0, ALU.add)
        import math
        lam_init = 0.8 - 0.6 * math.exp(-0.3 * l)
        tt(xt[:, 0:64], pb[:, PB_LAM:PB_LAM + 64], pb[:, PB_LAM + 64:PB_LAM + 128], ALU.mult)
        tt(xt[:, 64:128], pb[:, PB_LAM + 128:PB_LAM + 192], pb[:, PB_LAM + 192:PB_LAM + 256], ALU.mult)
        red(st1[:, 0:2], xt[:, 0:128].rearrange("p (a b) -> p a b", b=64))
        act(st1[:, 0:2], st1[:, 0:2], AF.Exp)
        tt(pbd[:, 0:1], st1[:, 0:1], st1[:, 1:2], ALU.subtract)
        ts(pbd[:, 0:1], pbd[:, 0:1], lam_init, ALU.add)
        ts(pbd[:, 1:2], pbd[:, 0:1], -1.0, ALU.mult)
        ts(gsub[:], pb[:, PB_SUB:PB_SUB + 128], 1.0 - lam_init, ALU.mult)
        act(pbd[:, 2:4], pb[:, PB_ALOG:PB_ALOG + 2], AF.Exp)
        ts(pbd[:, 2:4], pbd[:, 2:4], -1.0, ALU.mult)

    def norm_transpose(src_f32, nrows, gcol, dstT, c0, from_bf16=None):
        if from_bf16 is not None:
            cp(xt[:nrows, :], from_bf16)
            src = xt
        else:
            src = src_f32
        act(xn[:nrows, :], src[:nrows, :], AF.Square, accum_out=st1[:nrows, 4:5])
        P.rsqrt(st1[:nrows, 5:6], st1[:nrows, 4:5], 1.0 / D, RMS_EPS)
        ts(xn[:nrows, :], src[:nrows, :], st1[:nrows, 5:6], ALU.mult)
        for half in range(2):
            bk = P.bank()
            bkb = bk[:].bitcast(BF16)
            for j in range(8):
                dc = half * 8 + j
                tr(bkb[:, j * 128:j * 128 + nrows], xn[:nrows, dc * 128:(dc + 1) * 128], identb[:nrows, :nrows])
            tt(dstT[:, half * 8:(half + 1) * 8, c0:c0 + nrows],
               bkb[:, :].rearrange("p (j t) -> p j t", t=128)[:, :, :nrows],
               pp[:, gcol + half * 8:gcol + half * 8 + 8].unsqueeze(2).broadcast_to([128, 8, nrows]), ALU.mult)

    kT_all = sb("kT_all", [128, T], BF16)
    Vaug = sb("Vaug", [128, NT, 132], BF16)
    qT = sb("qT", [128, 256], BF16)
    ET = [sb(f"ET{i}", [128, 512], BF16) for i in range(2)]
    mkT = sb("mkT", [128, 256], BF16)
    mvaug = sb("mvaug", [128, 2, 132], BF16)
    xqT = sb("xqT", [128, 256], BF16)
    EX = sb("EX", [128, 512], BF16)
    fin = sb("fin", [128, 6, 128])
    r2 = sb("r2", [128, 8])
    P.memset(Vaug[:], 1.0)
    P.memset(mvaug[:], 1.0)

    def proj_fm(c0, NB, ncols=128):
        bk = P.bank()
        for dc in range(NDC):
            mm(bk[:ncols, :NB], Wb[:, dc, c0:c0 + ncols], hT[:, dc, :NB], start=(dc == 0), stop=(dc == NDC - 1))
        return bk

    def proj_tm(ti, nr, c0, ncols, dst):
        done = 0
        while done < ncols:
            n = min(512, ncols - done)
            bk = P.bank()
            for dc in range(NDC):
                mm(bk[:nr, :n], hT[:, dc, ti * 128:ti * 128 + nr], Wb[:, dc, c0 + done:c0 + done + n],
                   start=(dc == 0), stop=(dc == NDC - 1))
            cp(dst[:nr, done:done + n], bk[:nr, :n], eng=("act" if (done // 512) % 2 == 0 else "dve"))
            done += n

    def attn_finalize(O0, O1, nr, gate, out):
        P.recip(r2[:nr, 0:1], O0[:nr, 128:129])
        P.recip(r2[:nr, 1:2], O1[:nr, 128:129])
        ts(r2[:nr, 1:2], r2[:nr, 1:2], pbd[:nr, 1:2], ALU.mult)
        ts(fin[:nr, 0, :], O0[:nr, 0:128], r2[:nr, 0:1], ALU.mult)
        stt(fin[:nr, 1, :], O1[:nr, 0:128], r2[:nr, 1:2], fin[:nr, 0, :], ALU.mult, ALU.add)
        attn_post(fin[:nr, 1, :], nr, gate, out)

    def attn_post(o, nr, gate, out):
        act(fin[:nr, 2, :], o, AF.Square, accum_out=r2[:nr, 2:3])
        P.rsqrt(r2[:nr, 3:4], r2[:nr, 2:3], 1.0 / 128, RMS_EPS)
        act(fin[:nr, 3, :], gate, AF.Silu)
        stt(fin[:nr, 2, :], o, r2[:nr, 3:4], gsub[:nr, :], ALU.mult, ALU.mult)
        tt(out, fin[:nr, 2, :], fin[:nr, 3, :], ALU.mult)

    def diff_attn_block(qb):
        for kt in range(2 * qb + 2):
            j = kt - 2 * qb
            bks = [P.bank(), P.bank()]
            for c in range(2):
                o_ = bks[c][:, 0:256]
                mm(o_, kT_all[c * 64:(c + 1) * 64, kt * 128:(kt + 1) * 128], qT[c * 64:(c + 1) * 64, :], start=True, stop=(j < 0))
                if j >= 0:
                    mm(o_, identb[:, :], negmask[:, j * 512 + c * 256:j * 512 + (c + 1) * 256], start=False, stop=True)
            E = ET[kt % 2]
            for qt in range(2):
                if j == 1 and qt == 0:
                    continue
                m = 2 * qb + qt - kt
                for c in range(2):
                    act(E[:, c * 256 + qt * 128:c * 256 + (qt + 1) * 128], bks[c][:, qt * 128:(qt + 1) * 128],
                        AF.Exp, bias=abias[:, m:m + 1])
            for qt in range(2):
                if j == 1 and qt == 0:
                    continue
                for c in range(2):
                    mm(PS[4 + 2 * qt + c][:, 0:129], E[:, c * 256 + qt * 128:c * 256 + (qt + 1) * 128],
                       Vaug[:, kt, 0:129], start=(kt == 0), stop=(kt == 2 * qb + qt))
        for qt in range(2):
            attn_finalize(PS[4 + 2 * qt], PS[5 + 2 * qt], 128, TM[:, qt, TM_DG:TM_DG + 128], mixt[:, qt, 0:128])

    def cross_attn_block(NB):
        bk = P.bank()
        for m in range(2):
            mm(bk[:, m * 256:m * 256 + NB], mkT[:, m * 128:(m + 1) * 128], xqT[:, :NB])
        act(EX[:], bk[:], AF.Exp)
        bk2 = P.bank()
        for qt in range(NB // 128):
            for m in range(2):
                mm(bk2[:, qt * 132:qt * 132 + 129], EX[:, m * 256 + qt * 128:m * 256 + (qt + 1) * 128],
                   mvaug[:, m, 0:129], start=(m == 0), stop=(m == 1))
        for qt in range(NB // 128):
            P.recip(r2[:, 4:5], bk2[:, qt * 132 + 128:qt * 132 + 129])
            act(fin[:, 4, :], TM[:, qt, TM_XG:TM_XG + 128], AF.Silu)
            stt(mixt[:, qt, 384:512], bk2[:, qt * 132:qt * 132 + 128], r2[:, 4:5], fin[:, 4, :], ALU.mult, ALU.mult)

    def memory_kv(l):
        d = L[l]
        for ti in range(2):
            P.ld(xt[:, :], mem[ti * 128:(ti + 1) * 128, :])
            norm_transpose(xt, 128, PP_MG, hT, ti * 128)
        for (wb_, o_dram, isk) in ((wmkb, d["o_mk"], True), (wmvb, d["o_mv"], False)):
            for ti in range(2):
                bk = P.bank()
                for dc in range(NDC):
                    mm(bk[:, :128], hT[:, dc, ti * 128:(ti + 1) * 128], wb_[:, dc, :], start=(dc == 0), stop=(dc == NDC - 1))
                cp(fin[:, 0, :], bk[:, :128], eng="act")
                P.stq(o_dram[ti * 128:(ti + 1) * 128, :], fin[:, 0, :])
                if not isk:
                    cp(mvaug[:, ti, 0:128], bk[:, :128])
            if isk:
                bk = P.bank()
                for dc in range(NDC):
                    mm(bk[:, :256], wb_[:, dc, :], hT[:, dc, :256], start=(dc == 0), stop=(dc == NDC - 1))
                act(mkT[:], bk[:, :256], AF.Copy, scale=1.0 / np.sqrt(128.0))
    xbc_pre = sb("xbc_pre", [128, 3, 3 + 256])
    cacc = sb("cacc", [128, 3, 256])
    xcT = sb("xcT", [128, 256])
    BcT = sb("BcT", [128, 256], BF16)
    CcT = sb("CcT", [128, 256], BF16)
    dtt = sb("dtt", [128, 2, 2])
    adt = sb("adt", [128, 2, 2])
    acs = sb("acs", [128, 2])
    adtb = sb("adtb", [128, 2, 128])
    Xtok = sb("Xtok", [128, 128])
    Xdtf = sb("Xdtf", [128, 128])
    Xdtb = sb("Xdtb", [128, 128], BF16)
    Xdec = sb("Xdec", [128, 128], BF16)
    Btok = sb("Btok", [128, 128], BF16)
    Dm = sb("Dm", [128, 2, 128])
    LT = sb("LT", [128, 2, 128])
    EA = sb("EA", [128, 2, 128])
    CE = sb("CE", [128, 2, 128], BF16)
    MT = sb("MT", [128, 2, 128], BF16)
    dcol = sb("dcol", [128, 4])
    ym = sb("ym", [128, 128])
    vv = sb("vv", [128, 128])
    szs = sb("szs", [128, 128])
    SsT = sb("SsT", [128, 2, 64])
    SsTb = sb("SsTb", [128, 2, 64], BF16)
    sso = sb("sso", [64, 2, 128])

    def conv_silu(taps, grp, NB, out):
        a = cacc[:, grp, :NB]
        ts(a, taps[3], pp[:, PP_CW + grp * 4 + 3:PP_CW + grp * 4 + 4], ALU.mult)
        for j in (2, 1, 0):
            stt(a, taps[j], pp[:, PP_CW + grp * 4 + j:PP_CW + grp * 4 + j + 1], a, ALU.mult, ALU.add)
        act(out, a, AF.Silu, bias=pp[:, PP_CB + grp:PP_CB + grp + 1])

    def softplus_dt(dst, src, nr):
        k = src.shape[1]
        tt(dst, src, pb[:nr, PB_DTB:PB_DTB + 2].unsqueeze(1).broadcast_to([nr, k, 2]), ALU.add)
        act(dst, dst, AF.Exp)
        act(dst, dst, AF.Ln, bias=1.0)

    def mamba_post(y_ps, xtok, nr, zgate, out, ssq_out):
        for h in range(2):
            stt(ym[:nr, h * 64:(h + 1) * 64], xtok[:nr, h * 64:(h + 1) * 64], pb[:nr, PB_DSK + h:PB_DSK + h + 1],
                y_ps[:nr, h * 64:(h + 1) * 64], ALU.mult, ALU.add)
        act(szs[:nr, :], zgate, AF.Silu)
        tt(vv[:nr, :], ym[:nr, :], szs[:nr, :], ALU.mult)
        act(szs[:nr, :], vv[:nr, :], AF.Square, accum_out=dcol[:nr, 2:3])
        tt(out, vv[:nr, :], pb[:nr, PB_GN:PB_GN + 128], ALU.mult)
        cp(ssq_out[:, 0:1], dcol[:nr, 2:3])
        cp(dcol[:nr, 3:4], ssq_out[:, 0:1])
        tt(dcol[:nr, 3:4], dcol[:nr, 2:3], dcol[:nr, 3:4], ALU.subtract)
        cp(ssq_out[:, 1:2], dcol[:nr, 3:4])

    def mamba_block(blk, NB):
        for grp, c0 in enumerate((C_XM, C_BM, C_CM)):
            bk = proj_fm(c0, NB)
            cp(xbc_pre[:, grp, 3:3 + NB], bk[:, :NB], eng=("act" if grp == 1 else "dve"))
        for grp, out in enumerate((xcT, BcT, CcT)):
            conv_silu([xbc_pre[:, grp, j:j + NB] for j in range(4)], grp, NB, out[:, :NB])
        cp(xbc_pre[:, :, 0:3], xbc_pre[:, :, NB:NB + 3])
        nch = NB // 128
        softplus_dt(dtt[:, :nch, :], TM[:, :nch, TM_DT:TM_DT + 2], 128)
        tt(adt[:, :nch, :], dtt[:, :nch, :], pbd[:, 2:4].unsqueeze(1).broadcast_to([128, nch, 2]), ALU.mult)
        for ci in range(nch):
            cs_ = slice(ci * 128, (ci + 1) * 128)
            bk = P.bank()
            tr(bk[:, 0:128], xcT[:, cs_], identf[:, :])
            cp(Xtok[:], bk[:, 0:128], eng="act")
            bkb = P.bank()
            bkbb = bkb[:].bitcast(BF16)
            tr(bkbb[:, 0:128], BcT[:, cs_], identb[:, :])
            cp(Btok[:], bkbb[:, 0:128])
            for h in range(2):
                ts(Xdtf[:, h * 64:(h + 1) * 64], Xtok[:, h * 64:(h + 1) * 64], dtt[:, ci, h:h + 1], ALU.mult)
            cp(Xdtb[:], Xdtf[:])
            bk1 = P.bank()
            mm(bk1[:, 0:2], utri[:, :], adt[:, ci, :])
            cp(acs[:], bk1[:, 0:2])
            bkA = P.bank()
            for h in range(2):
                cp(adtb[:, h, :], adt[:, ci, h:h + 1].broadcast_to([128, 128]))
                mm(bkA[:, h * 128:(h + 1) * 128], adtb[:, h, :], utri[:, :])
            bkG = P.bank()
            mm(bkG[:, 0:128], BcT[:, cs_], CcT[:, cs_])
            for h in range(2):
                ar = bkA[:, h * 128:(h + 1) * 128]
                stt(Dm[:, h, :], ar, acs[:, h:h + 1], negtri[:, :], ALU.subtract, ALU.add)
                act(LT[:, h, :], Dm[:, h, :], AF.Exp)
                act(EA[:, h, :], ar, AF.Exp)
                tt(CE[:, h, :], CcT[:, cs_], EA[:, h, :], ALU.mult)
                ts(dcol[:, h:h + 1], bkA[:, h * 128 + 127:h * 128 + 128], acs[:, h:h + 1], ALU.subtract)
                tt(MT[:, h, :], bkG[:, 0:128], LT[:, h, :], ALU.mult)
            act(dcol[:, 0:2], dcol[:, 0:2], AF.Exp)
            for h in range(2):
                ts(Xdec[:, h * 64:(h + 1) * 64], Xdtf[:, h * 64:(h + 1) * 64], dcol[:, h:h + 1], ALU.mult)
            bkY = P.bank()
            for h in range(2):
                mm(bkY[:, h * 64:(h + 1) * 64], MT[:, h, :], Xdtb[:, h * 64:(h + 1) * 64], start=True, stop=False)
                mm(bkY[:, h * 64:(h + 1) * 64], CE[:, h, :], SsTb[:, h, :], start=False, stop=True)
            mamba_post(bkY, Xtok, 128, TM[:, ci, TM_MZ:TM_MZ + 128], mixt[:, ci, 256:384], mixt[:, ci, 512:514])
            bkS = P.bank()
            for h in range(2):
                mm(bkS[:, h * 64:(h + 1) * 64], Btok[:, :], Xdec[:, h * 64:(h + 1) * 64])
            for h in range(2):
                act(dcol[:, 2:3], bkA[:, h * 128 + 127:h * 128 + 128], AF.Exp)
                stt(SsT[:, h, :], SsT[:, h, :], dcol[:, 2:3], bkS[:, h * 64:(h + 1) * 64], ALU.mult, ALU.add)
            cp(SsTb[:], SsT[:])

    def mamba_state_out(dst):
        for h in range(2):
            bk = P.bank()
            tr(bk[:64, 0:128], SsT[:, h, :], identf[:, :])
            cp(sso[:, h, :], bk[:64, 0:128])
        P.stq(dst.rearrange("h p n -> p h n"), sso[:])
    rsb = sb("rsb", [128, 8, 1 + 256])
    xs_ = sb("xs_", [128, 8, 256])
    wab = sb("wab", [128, 256], BF16)
    xsvfb = sb("xsvfb", [128, 4, 256], BF16)
    lo32 = sb("lo32", [32, 256], BF16)
    sgw = sb("sgw", [128, 256])
    asg = sb("asg", [128, 256])
    vsg = sb("vsg", [128, 256])
    vfb = sb("vfb", [128, 256])
    kkn = sb("kkn", [128, 256])
    kp = sb("kp", [128, 256])
    rtmp = sb("rtmp", [128, 256])
    rtmp2 = sb("rtmp2", [128, 256])
    rkf = sb("rkf", [128, 256])
    cs = sb("cs", [128, 256])
    csr = sb("csr", [128, 256])
    Pin = sb("Pin", [128, 256])
    Pex = sb("Pex", [128, 256])
    Pinv = sb("Pinv", [128, 256])
    AR = sb("AR", [128, 2 * 256], BF16)
    btl = sb("btl", [128, 256], BF16)
    ktl = sb("ktl", [128, 256], BF16)
    vbl = sb("vbl", [128, 256], BF16)
    rhoinv = sb("rhoinv", [128, 2])
    BKV = sb("BKV", [128, 3, 128], BF16)
    A4 = sb("A4", [128, 2, 512], BF16)
    PQ = [sb(f"PQ{i}", [128, 2, 2, 128], BF16) for i in range(2)]
    TTt = [sb(f"TTt{i}", [128, 2, 128], BF16) for i in range(2)]
    SrT = sb("SrT", [128, 64])
    S0s = sb("S0s", [128, 64])
    S0sb = sb("S0sb", [128, 64], BF16)
    tmpS = sb("tmpS", [128, 64])
    Wt = sb("Wt", [128, 128], BF16)
    Ut = sb("Ut", [128, 128], BF16)
    Yt = sb("Yt", [128, 128])
    yn = sb("yn", [128, 128])
    sqy = sb("sqy", [128, 128])
    sgr = sb("sgr", [128, 128])
    gst = sb("gst", [128, 12])
    rwo = sb("rwo", [64, 128])

    def rwkv_prep(l, NB, cur, prev, vf_cols):
        ngrp = 8 if l == 1 else 4
        for grp in range(ngrp):
            ts(xs_[:, grp, :NB], cur(grp), ppd[:, grp:grp + 1], ALU.mult)
            stt(xs_[:, grp, :NB], prev(grp), pp[:, PP_MU + grp:PP_MU + grp + 1], xs_[:, grp, :NB], ALU.mult, ALU.add)
        act(wab[0:64, :NB], xs_[0:64, 3, :NB], AF.Tanh)
        cp(wab[64:128, :NB], xs_[64:128, 3, :NB])
        bk = P.bank()
        mm(bk[:, :NB], wuab[0:64, :], wab[0:64, :NB])
        act(sgw[:, :NB], bk[:, :NB], AF.Sigmoid, bias=pp[:, PP_W0:PP_W0 + 1])
        bk = P.bank()
        mm(bk[:, :NB], wuab[64:128, :], wab[64:128, :NB])
        act(asg[:, :NB], bk[:, :NB], AF.Sigmoid, bias=pp[:, PP_A0:PP_A0 + 1])
        xv = xs_[:, 2, :NB]
        if l == 1:
            cp(xsvfb[:, :, :NB], xs_[:, 4:8, :NB])
            bk = P.bank()
            for c4 in range(4):
                mm(bk[:32, :NB], vrab[:, c4, :], xsvfb[:, c4, :NB], start=(c4 == 0), stop=(c4 == 3))
            cp(lo32[:, :NB], bk[:32, :NB])
            bk = P.bank()
            mm(bk[:, :NB], vrbb[:, :], lo32[:, :NB])
            act(vsg[:, :NB], bk[:, :NB], AF.Sigmoid, bias=pp[:, PP_VR0:PP_VR0 + 1])
            P.ld(vfb[:, :NB], vfirst[:, vf_cols])
            tt(rtmp[:, :NB], vfb[:, :NB], xv, ALU.subtract)
            tt(rtmp[:, :NB], rtmp[:, :NB], vsg[:, :NB], ALU.mult)
            tt(xv, xv, rtmp[:, :NB], ALU.add)
        else:
            P.stq(vfirst[:, vf_cols], xv)
        ts(rtmp[:, :NB], xs_[:, 1, :NB], pp[:, PP_KK:PP_KK + 1], ALU.mult)
        tt(rtmp2[:, :NB], rtmp[:, :NB], rtmp[:, :NB], ALU.mult)
        bk = P.bank()
        mm(bk[:, :NB], bones[:, :], rtmp2[:, :NB])
        P.rsqrt(rtmp2[:, :NB], bk[:, :NB], 1.0, 1e-24)
        tt(kkn[:, :NB], rtmp[:, :NB], rtmp2[:, :NB], ALU.mult)
        ts(rtmp[:, :NB], asg[:, :NB], pp[:, PP_KA:PP_KA + 1], ALU.mult, ppd[:, 8:9], ALU.add)
        tt(kp[:, :NB], xs_[:, 1, :NB], rtmp[:, :NB], ALU.mult)
        stt(rkf[:, :NB], xs_[:, 0, :NB], pp[:, PP_RK:PP_RK + 1], kp[:, :NB], ALU.mult, ALU.mult)

    def rwkv_post(y_ps, vtok, rk2, nr, gate, out):
        if y_ps is not None:
            cp(Yt[:nr, :], y_ps, eng="act")
        red(gst[:nr, 0:2], Yt[:nr, :].rearrange("p (h n) -> p h n", n=64))
        tt(sqy[:nr, :], Yt[:nr, :], Yt[:nr, :], ALU.mult)
        red(gst[:nr, 2:4], sqy[:nr, :].rearrange("p (h n) -> p h n", n=64))
        ts(gst[:nr, 4:6], gst[:nr, 0:2], 1.0 / 64, ALU.mult)
        tt(gst[:nr, 6:8], gst[:nr, 4:6], gst[:nr, 4:6], ALU.mult)
        stt(gst[:nr, 8:10], gst[:nr, 2:4], 1.0 / 64, gst[:nr, 6:8], ALU.mult, ALU.subtract)
        P.rsqrt(gst[:nr, 8:10], gst[:nr, 8:10], 1.0, LNX_EPS)
        for h in range(2):
            ts(yn[:nr, h * 64:(h + 1) * 64], Yt[:nr, h * 64:(h + 1) * 64], gst[:nr, 4 + h:5 + h], ALU.subtract,
               gst[:nr, 8 + h:9 + h], ALU.mult)
        tt(yn[:nr, :], yn[:nr, :], pb[:nr, PB_LNG:PB_LNG + 128], ALU.mult)
        tt(yn[:nr, :], yn[:nr, :], pb[:nr, PB_LNB:PB_LNB + 128], ALU.add)
        for h in range(2):
            stt(yn[:nr, h * 64:(h + 1) * 64], vtok[:nr, h * 64:(h + 1) * 64], rk2[:nr, h:h + 1],
                yn[:nr, h * 64:(h + 1) * 64], ALU.mult, ALU.add)
        act(sgr[:nr, :], gate, AF.Silu)
        tt(out, yn[:nr, :], sgr[:nr, :], ALU.mult)

    def rwkv_block(l, blk, NB):
        ngrp = 8 if l == 1 else 4
        cols = [C_R, C_K, C_V, C_WA] + ([C_VF + i * 128 for i in range(4)] if l == 1 else [])
        for grp in range(ngrp):
            bk = proj_fm(cols[grp], NB)
            cp(rsb[:, grp, 1:1 + NB], bk[:, :NB], eng=("act" if grp % 2 == 0 else "dve"))
        rwkv_prep(l, NB, lambda g_: rsb[:, g_, 1:1 + NB], lambda g_: rsb[:, g_, 0:NB], slice(blk * 256, blk * 256 + NB))
        cp(rsb[:, :ngrp, 0:1], rsb[:, :ngrp, NB:NB + 1])
        nch = NB // 128
        if stage < 3.3:
            return
        for ci in range(nch):
            c_ = slice(ci * 128, (ci + 1) * 128)
            P.scan(cs[:, c_], onesf[:, :128], sgw[:, c_], 0.0, ALU.mult, ALU.add)
            ts(csr[:, c_], cs[:, c_], cs[:, ci * 128 + 63:ci * 128 + 64], ALU.subtract)
        act(Pin[:, :NB], csr[:, :NB], AF.Exp, scale=-DECAY)
        act(Pinv[:, :NB], csr[:, :NB], AF.Exp, scale=DECAY)
        tt(rtmp[:, :NB], csr[:, :NB], sgw[:, :NB], ALU.subtract)
        act(Pex[:, :NB], rtmp[:, :NB], AF.Exp, scale=-DECAY)
        for ci in range(nch):
            c_ = slice(ci * 128, (ci + 1) * 128)
            stt(AR[:, ci * 256:ci * 256 + 128], kkn[:, c_], -1.0, Pex[:, c_], ALU.mult, ALU.mult)
            tt(AR[:, ci * 256 + 128:ci * 256 + 256], xs_[:, 0, c_], Pin[:, c_], ALU.mult)
            act(rhoinv[:, ci:ci + 1], cs[:, ci * 128 + 63:ci * 128 + 64], AF.Exp, scale=-DECAY)
        tt(rtmp[:, :NB], kkn[:, :NB], asg[:, :NB], ALU.mult)
        tt(btl[:, :NB], rtmp[:, :NB], Pinv[:, :NB], ALU.mult)
        tt(ktl[:, :NB], kp[:, :NB], Pinv[:, :NB], ALU.mult)
        cp(vbl[:, :NB], xs_[:, 2, :NB])
        if stage < 3.5:
            return
        for ci in range(nch):
            c_ = slice(ci * 128, (ci + 1) * 128)
            ar0 = ci * 256
            bkb = P.bank()
            bkbb = bkb[:].bitcast(BF16)
            tr(bkbb[:, 0:128], btl[:, c_], identb[:, :])
            tr(bkbb[:, 128:256], ktl[:, c_], identb[:, :])
            tr(bkbb[:, 256:384], vbl[:, c_], identb[:, :])
            cp(BKV[:].rearrange("p a b -> p (a b)"), bkbb[:, 0:384], eng="act")
            if stage < 3.52:
                continue
            for h in range(2):
                hs = slice(h * 64, (h + 1) * 64)
                bkA = P.bank()
                mm(bkA[:, 0:256], btl[hs, c_], AR[hs, ar0:ar0 + 256])
                mm(bkA[:, 256:512], ktl[hs, c_], AR[hs, ar0:ar0 + 256])
                tt(A4[:, h, :], bkA[:, :], mask4[:, :], ALU.mult)
            if stage < 3.54:
                continue
            for h in range(2):
                hs = slice(h * 64, (h + 1) * 64)
                bkL = P.bank()
                mm(bkL[:, 0:128], AR[hs, ar0:ar0 + 128], btl[hs, c_])
                tt(PQ[0][:, h, 1, :], bkL[:, 0:128], maskl[:, :], ALU.mult)
            cp(PQ[0][:, :, 0, :], A4[:, :, 0:128])
            tt(TTt[0][:], A4[:, :, 0:128], identb[:, :].unsqueeze(1).broadcast_to([128, 2, 128]), ALU.add)
            cur = 0
            if stage < 3.56:
                continue
            for k in range(6 if stage >= 3.6 else 1):
                nxt = 1 - cur
                bkQ = P.bank()
                for h in range(2):
                    mm(bkQ[:, h * 256:h * 256 + 128], PQ[cur][:, h, 1, :], PQ[cur][:, h, 0, :])
                    mm(bkQ[:, h * 256 + 128:h * 256 + 256], PQ[cur][:, h, 0, :], PQ[cur][:, h, 1, :])
                cp(PQ[nxt][:].rearrange("p h a t -> p (h a t)"), bkQ[:, :], eng="act")
                bkT = P.bank()
                for h in range(2):
                    mm(bkT[:, h * 128:(h + 1) * 128], PQ[nxt][:, h, 1, :], TTt[cur][:, h, :])
                tt(TTt[nxt][:].rearrange("p h t -> p (h t)"), bkT[:, 0:256], TTt[cur][:].rearrange("p h t -> p (h t)"), ALU.add)
                cur = nxt
            TTf = TTt[cur]
            if stage < 3.7:
                continue
            ts(S0s[:], SrT[:], rhoinv[:, ci:ci + 1], ALU.mult)
            cp(S0sb[:], S0s[:])
            for h in range(2):
                hs = slice(h * 64, (h + 1) * 64)
                bkW = P.bank()
                mm(bkW[:, 0:64], AR[hs, ar0:ar0 + 128], S0sb[hs, :], start=True, stop=False)
                mm(bkW[:, 0:64], A4[:, h, 256:384], BKV[:, 2, hs], start=False, stop=True)
                cp(Wt[:, hs], bkW[:, 0:64], eng=("act" if h == 0 else "dve"))
            bkU = P.bank()
            for h in range(2):
                hs = slice(h * 64, (h + 1) * 64)
                mm(bkU[:, hs], TTf[:, h, :], Wt[:, hs])
            cp(Ut[:], bkU[:, 0:128])
            for h in range(2):
                hs = slice(h * 64, (h + 1) * 64)
                bkY = P.bank()
                mm(bkY[:, 0:64], AR[hs, ar0 + 128:ar0 + 256], S0sb[hs, :], start=True, stop=False)
                mm(bkY[:, 0:64], A4[:, h, 128:256], Ut[:, hs], start=False, stop=False)
                mm(bkY[:, 0:64], A4[:, h, 384:512], BKV[:, 2, hs], start=False, stop=True)
                cp(Yt[:, hs], bkY[:, 0:64], eng=("act" if h == 0 else "dve"))
            bkS = P.bank()
            mm(bkS[:, 0:128], BKV[:, 0, :], Ut[:, :], start=True, stop=False)
            mm(bkS[:, 0:128], BKV[:, 1, :], BKV[:, 2, :], start=False, stop=True)
            for h in range(2):
                hs = slice(h * 64, (h + 1) * 64)
                tt(tmpS[hs, :], bkS[hs, hs], S0s[hs, :], ALU.add)
                ts(SrT[hs, :], tmpS[hs, :], Pin[hs, ci * 128 + 127:ci * 128 + 128], ALU.mult)
            if stage < 3.9:
                continue
            bkR = P.bank()
            mm(bkR[:, 0:2], rkf[:, c_], bind[:, :])
            cp(gst[:, 10:12], bkR[:, 0:2])
            rwkv_post(None, BKV[:, 2, :], gst[:, 10:12], 128, TM[:, ci, TM_RG:TM_RG + 128], mixt[:, ci, 128:256])

    def rwkv_state_out(dst):
        bk = P.bank()
        tr(bk[:64, 0:128], SrT[:, :], identf[:, :])
        cp(rwo[:], bk[:64, 0:128])
        P.stq(dst.rearrange("h v k -> v h k"), rwo[:].rearrange("v (h k) -> v h k", k=64))
    TPP = NPG
    NSL = TPP + 1
    BPP = 128 // TPP
    c_indj = P.inp("c_indj", [NPG, 128])
    indj = sb("indj", [NPG, 128])
    P.ld(indj[:], c_indj[:, :])
    pti = sb("pti", [NPG, NSH], I32)
    ptf = sb("ptf", [NPG, NSH])
    idxf = sb("idxf", [128, NSH])
    idx = sb("idx", [128, NSH], I32)
    P.ld(pti[:], pt.rearrange("s j -> j s"), slow=True)
    cp(ptf[:], pti[:])
    bk = P.bank()
    mm(bk[:, :NSH], indj[:, :], ptf[:, :])
    ts(idxf[:], bk[:, :NSH], float(BPP), ALU.mult, oiota[:, 0:1], ALU.add)
    cp(idx[:], idxf[:])

    wbf = Wb[:].rearrange("p a b -> p (a b)").bitcast(F32)
    _REGIONS.clear()
    _REGIONS["Wb"] = [("Wb#K", 8192 * 4, (8192 + NSL * 128) * 4), ("Wb#V", 10496 * 4, (10496 + NSL * 128) * 4)]
    Kt = wbf[:, 8192:8192 + NSL * 128].rearrange("p (s d) -> p s d", d=128)
    Vt = wbf[:, 10496:10496 + NSL * 128].rearrange("p (s d) -> p s d", d=128)
    Kx = sb("Kx", [128, 2, 128])
    Vx = sb("Vx", [128, 2, 128])
    qb_ = sb("qb_", [128, 2, 128])
    scs = sb("scs", [128, NSL, 2])
    Es = sb("Es", [128, NSL, 2])
    wg = sb("wg", [128, NSL + 2])
    sst = sb("sst", [128, 8])
    qtok = sb("qtok", [128, 2, 128])
    xbcT = sb("xbcT", [128, 384])
    tokS = xt[:, 1024:2048].rearrange("p (a b) -> p a b", b=128)

    def sample_attn(l):
        d = L[l]
        proj_tm(0, NSH, C_DQ, 128, qtok[:, 0, :])
        proj_tm(0, NSH, C_XQ, 128, qtok[:, 1, :])
        P.memset(Kt[:, TPP, :], 0.0)
        P.memset(Vt[:, TPP, :], 0.0)
        ts(qtok[:NSH, 0, :], qtok[:NSH, 0, :], 0.125, ALU.mult)
        ts(qtok[:NSH, 1, :], qtok[:NSH, 1, :], float(1.0 / np.sqrt(128.0)), ALU.mult)
        P.stq(scrq.rearrange("q s d -> s q d"), qtok[:NSH, :, :])
        oA, oX = PS[6], PS[7]
        Wbb = Wb[:].rearrange("p a b -> p (a b)")
        Vb = Wbb[:, 0:NSL * 128].rearrange("p (s d) -> p s d", d=128)
        Vxb = Wbb[:, 4096:4096 + 256].rearrange("p (m d) -> p m d", d=128)
        wgb = Wbb[:, 8192:8192 + NSL + 2]
        for s in range(NSH):
            P.ld(qb_[:], scrq[:, s:s + 1, :].rearrange("q o d -> o q d").broadcast_to([128, 2, 128]))
            P.gather(Kt[:, 0:TPP, :].rearrange("p t d -> p (t d)"), d["ck"][:, :], idx[:, s:s + 1])
            P.ld(Kt[0:1, TPP, :], TM[s:s + 1, 0, TM_DK:TM_DK + 128])
            P.gather(Vt[:, 0:TPP, :].rearrange("p t d -> p (t d)"), d["cv"][:, :], idx[:, s:s + 1])
            P.ld(Vt[0:1, TPP, :], TM[s:s + 1, 0, TM_DV:TM_DV + 128])
            tt(Kt[:], Kt[:], qb_[:, 0, :].unsqueeze(1).broadcast_to([128, NSL, 128]), ALU.mult)
            red(scs[:].rearrange("p s c -> p (s c)"), Kt[:].rearrange("p s (c d) -> p (s c) d", d=64))
            tt(scs[:], scs[:], sbias[:, 0:NSL].unsqueeze(2).broadcast_to([128, NSL, 2]), ALU.add)
            act(Es[:], scs[:], AF.Exp)
            red(sst[:, 0:2], Es[:].rearrange("p s c -> p c s"))
            bk = P.bank()
            mm(bk[:, 0:2], onesf[:, :], sst[:, 0:2])
            P.recip(sst[:, 2:4], bk[:, 0:2])
            ts(sst[:, 3:4], sst[:, 3:4], pbd[:, 1:2], ALU.mult)
            ts(wg[:, 0:NSL], Es[:, :, 0], sst[:, 2:3], ALU.mult)
            stt(wg[:, 0:NSL], Es[:, :, 1], sst[:, 3:4], wg[:, 0:NSL], ALU.mult, ALU.add)
            cp(Vb[:], Vt[:], eng="act")
            cp(wgb[:, 0:NSL], wg[:, 0:NSL])
            for sl in range(NSL):
                mm(oA[:, s:s + 1], Vb[:, sl, :], wgb[:, sl:sl + 1], start=(sl == 0), stop=(sl == NSL - 1))
            P.ld(Kx[:], d["cmk"][s].rearrange("(m p) d -> p m d", p=128))
            P.ld(Vx[:], d["cmv"][s].rearrange("(m p) d -> p m d", p=128))
            tt(Kx[:], Kx[:], qb_[:, 1, :].unsqueeze(1).broadcast_to([128, 2, 128]), ALU.mult)
            red(sst[:, 4:6], Kx[:])
            act(sst[:, 4:6], sst[:, 4:6], AF.Exp)
            red(sst[:, 6:7], sst[:, 4:6])
            bk = P.bank()
            mm(bk[:, 0:1], onesf[:, :], sst[:, 6:7])
            P.recip(sst[:, 7:8], bk[:, 0:1])
            ts(wg[:, NSL:NSL + 2], sst[:, 4:6], sst[:, 7:8], ALU.mult)
            cp(Vxb[:], Vx[:], eng="act")
            cp(wgb[:, NSL:NSL + 2], wg[:, NSL:NSL + 2])
            for m in range(2):
                mm(oX[:, s:s + 1], Vxb[:, m, :], wgb[:, NSL + m:NSL + m + 1], start=(m == 0), stop=(m == 1))
        cp(fin[:, 5, :NSH], oA[:, :NSH])
        bk = P.bank()
        tr(bk[:NSH, 0:128], fin[:, 5, :NSH], identf[:, :])
        cp(fin[:NSH, 1, :], bk[:NSH, 0:128])
        attn_post(fin[:NSH, 1, :], NSH, TM[:NSH, 0, TM_DG:TM_DG + 128], mixt[:NSH, 0, 0:128])
        cp(fin[:, 5, :NSH], oX[:, :NSH])
        bk = P.bank()
        tr(bk[:NSH, 0:128], fin[:, 5, :NSH], identf[:, :])
        act(fin[:NSH, 4, :], TM[:NSH, 0, TM_XG:TM_XG + 128], AF.Silu)
        tt(mixt[:NSH, 0, 384:512], bk[:NSH, 0:128], fin[:NSH, 4, :], ALU.mult)

    def to_fm(src_tok, ncols, dst):
        for c in range(ncols):
            bk = P.bank()
            tr(bk[:, 0:NSH], src_tok[:NSH, c * 128:(c + 1) * 128], identf[:NSH, :NSH])
            cp(dst[:, c, :NSH], bk[:, 0:NSH], eng=("act" if c % 2 else "dve"))

    def to_tm(src_fm, dst):
        bk = P.bank()
        tr(bk[:NSH, 0:128], src_fm, identf[:, :])
        cp(dst, bk[:NSH, 0:128])

    SG = min(16, NSH)
    big = [wbf[:, i * 2048:i * 2048 + SG * 128].rearrange("p (s n) -> p s n", n=128) for i in range(4)]
    stok = xt[:, 0:1024]
    prevT = sb("prevT", [128, 9, NSH])
    curT = sb("curT", [128, 8, NSH])
    curM = sb("curM", [128, 3, NSH])

    def sample_proj_bc(l):
        proj_tm(0, NSH, C_XM, 384, xbcT[:, :])
        for grp, c0 in enumerate((C_XM, C_BM, C_CM)):
            bk = proj_fm(c0, NSH)
            cp(curM[:, grp, :], bk[:, :NSH])
        ngrp = 8 if l == 1 else 4
        cols = [C_R, C_K, C_V, C_WA] + ([C_VF + i * 128 for i in range(4)] if l == 1 else [])
        for grp in range(ngrp):
            bk = proj_fm(cols[grp], NSH)
            cp(curT[:, grp, :], bk[:, :NSH])
    ysT = sb("ysT", [128, NSH])
    dtS = sb("dtS", [128, 1, 2])
    dtT2 = sb("dtT2", [2, 2, NSH])
    dtF = sb("dtF", [128, 2, NSH])
    xdtS = sb("xdtS", [128, NSH])
    cvS = sb("cvS", [128, 3, NSH])
    wS = sb("wS", [128, NSH])
    bS = sb("bS", [128, NSH])
    aS = sb("aS", [128, NSH])

    def sample_mamba(l):
        d = L[l]
        for j in range(3):
            P.ld(stok[:NSH, 0:384], d["scv"][:, j, :])
            if j >= 1:
                P.stq(d["o_cv"][1:1 + NSH, j - 1, :], stok[:NSH, 0:384])
            for grp in range(3):
                bk = P.bank()
                tr(bk[:, 0:NSH], stok[:NSH, grp * 128:(grp + 1) * 128], identf[:NSH, :NSH])
                cp(prevT[:, grp * 3 + j, :], bk[:, 0:NSH])
        P.stq(d["o_cv"][1:1 + NSH, 2, :], xbcT[:NSH, :])
        for grp in range(3):
            conv_silu([prevT[:, grp * 3 + 0, :], prevT[:, grp * 3 + 1, :], prevT[:, grp * 3 + 2, :], curM[:, grp, :]],
                      grp, NSH, cvS[:, grp, :])
        softplus_dt(dtS[:NSH], TM[:NSH, 0:1, TM_DT:TM_DT + 2], NSH)
        tt(sst[:NSH, 0:2], dtS[:NSH, 0, :], pbd[:NSH, 2:4], ALU.mult)
        act(sst[:NSH, 0:2], sst[:NSH, 0:2], AF.Exp)
        for q_, src in enumerate((dtS[:NSH, 0, :], sst[:NSH, 0:2])):
            bk = P.bank()
            tr(bk[:2, 0:NSH], src, identf[:NSH, :NSH])
            cp(dtT2[:, q_, :], bk[:2, 0:NSH])
            bk = P.bank()
            mm(bk[:, 0:NSH], ind2[:, :], dtT2[:, q_, :])
            cp(dtF[:, q_, :], bk[:, 0:NSH])
        tt(xdtS[:], cvS[:, 0, :], dtF[:, 0, :], ALU.mult)
        to_tm(cvS[:, 1, :], tokS[:NSH, 0, :])
        to_tm(cvS[:, 2, :], tokS[:NSH, 1, :])
        P.stq(scr[0:2].rearrange("q s n -> s q n"), tokS[:NSH, 0:2, :])
        for s0 in range(0, NSH, SG):
            Sm, BB, CB, tmp = big
            P.ld(Sm[:], d["ssm"][s0:s0 + SG].rearrange("s h p n -> (h p) s n"))
            P.ld(BB[:], scr[0:1, s0:s0 + SG, :].broadcast_to([128, SG, 128]))
            P.ld(CB[:], scr[1:2, s0:s0 + SG, :].broadcast_to([128, SG, 128]))
            tt(Sm[:], Sm[:], dtF[:, 1, s0:s0 + SG].unsqueeze(2).broadcast_to([128, SG, 128]), ALU.mult)
            tt(tmp[:], BB[:], xdtS[:, s0:s0 + SG].unsqueeze(2).broadcast_to([128, SG, 128]), ALU.mult)
            tt(Sm[:], Sm[:], tmp[:], ALU.add)
            P.stq(d["o_ss"][1 + s0:1 + s0 + SG].rearrange("s h p n -> (h p) s n"), Sm[:])
            tt(tmp[:], Sm[:], CB[:], ALU.mult)
            red(ysT[:, s0:s0 + SG], tmp[:])
        to_tm(cvS[:, 0, :], tokS[:NSH, 2, :])
        bk = P.bank()
        tr(bk[:NSH, 0:128], ysT[:, :], identf[:, :])
        mamba_post(bk, tokS[:, 2, :], NSH, TM[:NSH, 0, TM_MZ:TM_MZ + 128], mixt[:NSH, 0, 256:384], mixt[:NSH, 0, 512:514])

    def sample_rwkv(l):
        d = L[l]
        ngrp = 8 if l == 1 else 4
        P.ld(stok[:NSH, 0:ngrp * 128], d["srs"][:, :])
        for grp in range(ngrp):
            bk = P.bank()
            tr(bk[:, 0:NSH], stok[:NSH, grp * 128:(grp + 1) * 128], identf[:NSH, :NSH])
            cp(prevT[:, grp, :], bk[:, 0:NSH])
        P.stq(d["o_rs"][:, :, 1:1 + NSH].rearrange("g p s -> p g s"), curT[:, 0:4, :])
        rwkv_prep(l, NSH, lambda g_: curT[:, g_, :], lambda g_: prevT[:, g_, :], slice(T, T + NSH))
        act(wS[:], sgw[:, :NSH], AF.Exp, scale=-DECAY)
        ts(aS[:], kkn[:, :NSH], -1.0, ALU.mult)
        tt(bS[:], kkn[:, :NSH], asg[:, :NSH], ALU.mult)
        for q_, src in enumerate((aS[:], wS[:], bS[:], kp[:, :NSH], xs_[:, 0, :NSH], xs_[:, 2, :NSH])):
            to_tm(src, tokS[:NSH, q_, :])
        P.stq(scr[2:8].rearrange("q s n -> s q n"), tokS[:NSH, 0:6, :])
        Sv = big[0][:].rearrange("p s (a k) -> p (s a) k", k=64)
        XB = big[1][:].rearrange("p s (a k) -> p (s a) k", k=64)
        tmp = big[2][:].rearrange("p s (a k) -> p (s a) k", k=64)
        SG2 = 2 * SG
        for s0 in range(0, NSH, SG2):
            ns = min(SG2, NSH - s0)
            P.ld(Sv[:, :ns, :], d["srw"][s0:s0 + ns].rearrange("s h v k -> (h v) s k"))

            def bcast(q_):
                for h in range(2):
                    P.ld(XB[h * 64:(h + 1) * 64, :ns, :],
                         scr[2 + q_:3 + q_, s0:s0 + ns, h * 64:(h + 1) * 64].broadcast_to([64, ns, 64]))
            bcast(0)
            tt(tmp[:, :ns, :], Sv[:, :ns, :], XB[:, :ns, :], ALU.mult)
            red(ysT[:, s0:s0 + ns], tmp[:, :ns, :])
            bcast(1)
            tt(Sv[:, :ns, :], Sv[:, :ns, :], XB[:, :ns, :], ALU.mult)
            bcast(2)
            tt(tmp[:, :ns, :], XB[:, :ns, :], ysT[:, s0:s0 + ns].unsqueeze(2).broadcast_to([128, ns, 64]), ALU.mult)
            tt(Sv[:, :ns, :], Sv[:, :ns, :], tmp[:, :ns, :], ALU.add)
            bcast(3)
            tt(tmp[:, :ns, :], XB[:, :ns, :], xs_[:, 2, s0:s0 + ns].unsqueeze(2).broadcast_to([128, ns, 64]), ALU.mult)
            tt(Sv[:, :ns, :], Sv[:, :ns, :], tmp[:, :ns, :], ALU.add)
            P.stq(d["o_rw"][1 + s0:1 + s0 + ns].rearrange("s h v k -> (h v) s k"), Sv[:, :ns, :])
            bcast(4)
            tt(tmp[:, :ns, :], Sv[:, :ns, :], XB[:, :ns, :], ALU.mult)
            red(ysT[:, s0:s0 + ns], tmp[:, :ns, :])
        bk = P.bank()
        tr(bk[:NSH, 0:128], ysT[:, :], identf[:, :])
        bkR = P.bank()
        mm(bkR[:NSH, 0:2], rkf[:, :NSH], bind[:, :])
        cp(gst[:NSH, 10:12], bkR[:NSH, 0:2])
        rwkv_post(bk[:NSH, 0:128], tokS[:NSH, 5, :], gst[:, 10:12], NSH, TM[:NSH, 0, TM_RG:TM_RG + 128], mixt[:NSH, 0, 128:256])
    rsbb = rsb[:].rearrange("p a b -> p (a b)").bitcast(BF16)
    mT = rsbb[:, 0:2048].rearrange("p (k t) -> p k t", t=128)
    mg = rsbb[:, 2048:2048 + 4 * NMIX].rearrange("p (r c) -> p r c", c=NMIX)
    sq8 = sb("sq8", [128, 4, 2])
    xr = xs_[:, 0:2, :].rearrange("p a b -> p (a b)")
    x1t = xs_[:, 2:4, :].rearrange("p a b -> p (a b)")
    x1tb = xsvfb[:, 0:2, :].rearrange("p a b -> p (a b)")
    sq4 = sb("sq4", [128, 4, 2])

    def tiles():
        out = [(i * 128, 128) for i in range(NT)]
        out.append((T, NSH))
        return out

    def phase_b(l):
        d = L[l]
        for ci, (c0, cr) in enumerate(chunks):
            P.allgather(d["mixg"][ci], d["mixb"][c0:c0 + cr, :], groups)
        for (r0, nr) in tiles():
            ci = r0 // CHR
            lr = r0 - ci * CHR
            P.ld(mg[:nr], d["mixg"][ci].rearrange("(r t) c -> t r c", r=4)[lr:lr + nr])
            cp(sq8[:nr], mg[:nr, :, 512:514])
            red(st1[:nr, 0:1], sq8[:nr].rearrange("p a b -> p (a b)"))
            P.rsqrt(st1[:nr, 1:2], st1[:nr, 0:1], 1.0 / 512, RMS_EPS)
            ts(mg[:nr, :, 256:384], mg[:nr, :, 256:384], st1[:nr, 1:2], ALU.mult)
            for half in range(2):
                bkb = P.bank()
                bkbb = bkb[:].bitcast(BF16)
                for j in range(8):
                    kc = half * 8 + j
                    r_, br_ = kc // 4, kc % 4
                    tr(bkbb[:, j * 128:j * 128 + nr], mg[:nr, r_, br_ * 128:(br_ + 1) * 128], identb[:nr, :nr])
                cp(mT[:, half * 8:(half + 1) * 8, :nr], bkbb[:, :].rearrange("p (j t) -> p j t", t=128)[:, :, :nr],
                   eng=("act" if half else "dve"))
            bk = P.bank()
            for kc in range(16):
                mm(bk[:nr, :], mT[:, kc, :nr], woutb[:, kc, :], start=(kc == 0), stop=(kc == 15))
            P.ld(xr[:nr], (xres if l == 0 else x1s)[r0:r0 + nr, :])
            tt(x1t[:nr], bk[:nr, :], xr[:nr], ALU.add)
            P.stq(x1s[r0:r0 + nr, :], x1t[:nr])
            if l == 0:
                cp(x1tb[:nr], x1t[:nr], eng="act")
                P.stq(x1b[r0:r0 + nr, :], x1tb[:nr])
            else:
                act(xr[:nr], x1t[:nr], AF.Square, accum_out=st1[:nr, 2:3])
                cp(st1[:nr, 3:4], st1[:nr, 2:3])
                P.stq(ssqb[r0:r0 + nr, :], st1[:nr, 2:4])
        if l == 0:
            for ci, (c0, cr) in enumerate(chunks):
                P.allgather(x1g[ci], x1b[c0:c0 + cr, :], groups)
        else:
            P.allgather(ssqg, ssqb, groups)
            sgv = ssqg.rearrange("(r t) c -> t r c", r=4)
            for (r0, nr) in tiles():
                P.ld(sq4[:nr], sgv[r0:r0 + nr])
                red(st1[:nr, 0:1], sq4[:nr, :, 0])
                P.rsqrt(st1[:nr, 1:2], st1[:nr, 0:1], 1.0 / D, RMS_EPS)
                P.ld(xr[:nr], x1s[r0:r0 + nr, :])
                stt(x1t[:nr], xr[:nr], st1[:nr, 1:2], pb[:nr, PB_FG:PB_FG + 512], ALU.mult, ALU.mult)
                P.stq(o_y[r0:r0 + nr, :], x1t[:nr])

    def load_x_tile(l, r0, nr):
        if l == 0:
            src = xp[r0:r0 + nr, :] if r0 < T else xs[r0 - T:r0 - T + nr, :]
            P.ld(xt[:nr, :], src)
            return None
        ci = r0 // CHR
        lr = r0 - ci * CHR
        P.ld(xtb[:nr, :].rearrange("p (r c) -> p r c", r=4), x1g[ci].rearrange("(r t) c -> t r c", r=4)[lr:lr + nr])
        return xtb[:nr, :]

    for l in range(2):
        d = L[l]
        load_params(l)
        load_w(d["wmk"], 128, 0)
        load_w(d["wmv"], 128, 128)
        memory_kv(l)
        load_w(d["winax"], NCOL_AX)
        for blk in range(NBLK):
            for ti in range(2):
                r0 = blk * 256 + ti * 128
                fb = load_x_tile(l, r0, 128)
                norm_transpose(xt, 128, PP_NG, hT, ti * 128, from_bf16=fb)
            P.stq(hTs[blk], hT[:].rearrange("p a b -> p (a b)"))
            for ti in range(2):
                r0 = blk * 256 + ti * 128
                proj_tm(ti, 128, C_TM1, 512, TM[:, ti, 0:512])
                P.stq(d["o_k"][r0:r0 + 128, :], TM[:, ti, TM_DK:TM_DK + 128])
                P.stq(d["o_v"][r0:r0 + 128, :], TM[:, ti, TM_DV:TM_DV + 128])
                cp(Vaug[:, 2 * blk + ti, 0:128], TM[:, ti, TM_DV:TM_DV + 128])
            bk = proj_fm(C_DQ, 256)
            act(qT[:], bk[:, :256], AF.Copy, scale=0.125)
            bk = proj_fm(C_TM1 + TM_DK, 256)
            cp(kT_all[:, blk * 256:(blk + 1) * 256], bk[:, :256])
            bk = proj_fm(C_XQ, 256)
            cp(xqT[:], bk[:, :256], eng="act")
            if stage >= 1:
                diff_attn_block(blk)
            if stage >= 2:
                cross_attn_block(256)
            mb = d["mixb"][blk * 256:(blk + 1) * 256, :].rearrange("(t p) c -> p t c", p=128)
            P.stq(mb[:, :, 0:128], mixt[:, :, 0:128])
            P.stq(mb[:, :, 384:512], mixt[:, :, 384:512])
        fb = load_x_tile(l, T, NSH)
        norm_transpose(xt, NSH, PP_NG, hT, 0, from_bf16=fb)
        P.stq(hTs[NBLK], hT[:].rearrange("p a b -> p (a b)"))
        proj_tm(0, NSH, C_TM1, 512, TM[:, 0, 0:512])
        P.stq(d["o_k"][T:T + NSH, :], TM[:NSH, 0, TM_DK:TM_DK + 128])
        P.stq(d["o_v"][T:T + NSH, :], TM[:NSH, 0, TM_DV:TM_DV + 128])
        if stage >= 5:
            sample_attn(l)
        P.stq(d["mixb"][T:T + NSH, 0:128], mixt[:NSH, 0, 0:128])
        P.stq(d["mixb"][T:T + NSH, 384:512], mixt[:NSH, 0, 384:512])
        load_w(d["winbc"], d["ncol_bc"])
        for t_ in (SsT, SsTb, SrT):
            P.memset(t_[:], 0.0)
        P.memset(xbc_pre[:, :, 0:3], 0.0)
        P.memset(rsb[:, :, 0:1], 0.0)
        for blk in range(NBLK):
            P.ld(hT[:].rearrange("p a b -> p (a b)"), hTs[blk])
            for ti in range(2):
                proj_tm(ti, 128, C_TM2, 258, TM[:, ti, 512:770])
            if stage >= 3:
                mamba_block(blk, 256)
            if stage > 3.1:
                rwkv_block(l, blk, 256)
            mb = d["mixb"][blk * 256:(blk + 1) * 256, :].rearrange("(t p) c -> p t c", p=128)
            P.stq(mb[:, :, 128:384], mixt[:, :, 128:384])
            P.stq(mb[:, :, 512:514], mixt[:, :, 512:514])
            if blk == NBLK - 1:
                proj_tm(1, 128, C_XM, 384, xbcT[:, :])
                P.stq(d["o_cv"][0, :, :], xbcT[125:128, :])
        if stage >= 3:
            mamba_state_out(d["o_ss"][0])
        if stage >= 4:
            rwkv_state_out(d["o_rw"][0])
            P.stq(d["o_rs"][:, :, 0:1].rearrange("g p o -> p g o"), rsb[:, 0:4, 0:1], slow=True)
        P.ld(hT[:].rearrange("p a b -> p (a b)"), hTs[NBLK])
        proj_tm(0, NSH, C_TM2, 258, TM[:, 0, 512:770])
        if stage >= 6:
            sample_proj_bc(l)
            sample_mamba(l)
        if stage >= 7:
            sample_rwkv(l)
        P.stq(d["mixb"][T:T + NSH, 128:384], mixt[:NSH, 0, 128:384])
        P.stq(d["mixb"][T:T + NSH, 512:514], mixt[:NSH, 0, 512:514])
        if P.dbg:
            for (r0, nr) in tiles():
                P.ld(mg[:nr, 0, :], d["mixb"][r0:r0 + nr, :])
                P.stq(dbg_mix[l][r0:r0 + nr, :], mg[:nr, 0, :])
        load_w(d["wout"], 512)
        phase_b(l)

    P.S.emit()
    return P


def _consts(cfg, g):
    slope = 2.0 ** (-8.0 * (g + 1) / 4.0)
    r = np.arange(128)[:, None]
    c = np.arange(128)[None, :]
    f32 = np.float32
    out = {}
    out["c_identb"] = (r == c).astype(NPBF)
    out["c_identf"] = (r == c).astype(f32)
    tri = np.where(c >= r, 0.0, NEG).astype(f32)
    out["c_negtri"] = tri
    nm = np.zeros((128, 2, 2, 2, 128), f32)
    nm[:, 0, :, 0, :] = tri[:, None, :]
    nm[:, 1, :, 0, :] = NEG
    nm[:, 1, :, 1, :] = tri[:, None, :]
    out["c_negmask"] = nm.reshape(128, 1024).astype(NPBF)
    al = np.ones((3, 128), f32)
    al[2] = np.arange(128)
    out["c_alibl"] = al.astype(NPBF)
    ar = np.zeros((3, 2, 2, 128), f32)
    ar[0] = (-slope * 128.0 * np.arange(2))[None, :, None]
    ar[1] = (-slope * np.arange(128))[None, None, :]
    ar[2] = slope
    out["c_alibr"] = ar.reshape(3, 512).astype(NPBF)
    out["c_abias"] = (slope * (np.arange(128)[:, None] - 128.0 * np.arange(64)[None, :])).astype(f32)
    su = (r < c).astype(f32)
    ui = (r <= c).astype(f32)
    out["c_mask4"] = np.concatenate([su, ui, su, ui], axis=1)
    out["c_maskl"] = (c < r).astype(f32)
    out["c_utri"] = (r <= c).astype(f32)
    out["c_bones"] = ((r // 64) == (c // 64)).astype(f32)
    bi = np.zeros((128, 2), f32)
    bi[:64, 0] = 1
    bi[64:, 1] = 1
    out["c_bind"] = bi
    out["c_ind2"] = np.ascontiguousarray(bi.T)
    TPP = cfg.NPG
    BPP = 128 // TPP
    sbias = np.full((128, 17), NEG, f32)
    p = np.arange(128)
    for ti in range(TPP):
        kpos = (p // BPP) * 128 + (p % BPP) * TPP + ti
        sbias[:, ti] = -slope * (cfg.PAST - kpos)
    sbias[0, TPP] = 0.0
    out["c_sbias"] = sbias
    out["c_oiota"] = (p % BPP).astype(f32)[:, None]
    ij = np.zeros((cfg.NPG, 128), f32)
    ij[p // BPP, p] = 1.0
    out["c_indj"] = ij
    return out


def _fm(v):
    return np.ascontiguousarray(v.reshape(-1, 128).T)


def prepare_core(inp, cfg, core):
    b, g = core // 4, core % 4
    gg = g // 2
    NSH, T = cfg.NSH, cfg.T
    sl = slice(b * NSH, (b + 1) * NSH)
    gs = slice(g * 128, (g + 1) * 128)
    f32 = np.float32
    m = dict(_consts(cfg, g))
    xp = inp["x_prompt"][b]
    xs = inp["x_sample"][sl, 0, :]
    m["xp"] = xp
    m["xs"] = xs
    m["xres"] = np.concatenate([xp[:, g * 512:(g + 1) * 512], xs[:, g * 512:(g + 1) * 512]], axis=0)
    m["mem"] = inp["mem_prompt"][b]
    m["pt"] = inp["page_table"][sl].astype(np.int32)
    A = np.arange
    o_dq, o_dk, o_dv, o_dg, o_rs, o_rg, o_mz, o_xbc, o_dt, o_xq, o_xg = 0, 512, 1024, 1536, 2048, 3712, 4224, 4736, 5760, 5768, 6280
    gi = g * 128 + A(128)
    ggi = gg * 128 + A(128)
    cols_ax = np.concatenate([o_dq + gi, o_dk + gi, o_dv + gi, o_dg + gi, o_xg + gi, o_xq + gi])
    for l in range(2):
        cols_bc = [o_rg + gi, o_mz + gi, o_dt + 2 * g + A(2), o_rs + gi, o_rs + 512 + gi, o_rs + 1024 + gi,
                   o_rs + 1536 + A(128), o_xbc + gi, o_xbc + 512 + ggi, o_xbc + 768 + ggi]
        if l == 1:
            cols_bc.append(o_rs + 1024 + A(512))
        cols_bc = np.concatenate(cols_bc)
        w = inp["w_in"][l]
        m[f"winax{l}"] = np.ascontiguousarray(w[:, cols_ax])
        m[f"winbc{l}"] = np.ascontiguousarray(w[:, cols_bc])
        rows = np.concatenate([br * 512 + r_ * 128 + A(128) for r_ in range(4) for br in range(4)])
        m[f"wout{l}"] = np.ascontiguousarray(inp["w_out"][l][rows][:, g * 512:(g + 1) * 512])
        m[f"wmk{l}"] = np.ascontiguousarray(inp["w_mk"][l][:, gs])
        m[f"wmv{l}"] = np.ascontiguousarray(inp["w_mv"][l][:, gs])
        m[f"wua{l}"] = np.concatenate([inp["w_up"][l][:, gs], inp["a_up"][l][:, gs]], axis=0)
        pp = np.zeros((128, NPP), f32)
        pp[:, PP_NG:PP_NG + 16] = _fm(inp["norm_g"][l])
        pp[:, PP_MG:PP_MG + 16] = _fm(inp["mem_norm_g"][l])
        mu = inp["shift_mu"][l]
        pp[:, PP_MU + 0] = mu[gi]
        pp[:, PP_MU + 1] = mu[512 + gi]
        pp[:, PP_MU + 2] = mu[1024 + gi]
        pp[:, PP_MU + 3] = mu[1536 + A(128)]
        pp[:, PP_MUVF:PP_MUVF + 4] = _fm(mu[1024:1536])
        pp[:, PP_W0] = inp["w0"][l][gi]
        pp[:, PP_A0] = inp["a0"][l][gi]
        pp[:, PP_KK] = inp["k_k"][l][gi]
        pp[:, PP_KA] = inp["k_a"][l][gi]
        pp[:, PP_RK] = inp["r_k"][l].reshape(-1)[gi]
        if l == 1:
            pp[:, PP_VR0] = inp["vres_0"][0][gi]
            m["vra"] = inp["vres_a"][0]
            m["vrb"] = np.ascontiguousarray(inp["vres_b"][0][:, gs])
        cch = [gi, 512 + ggi, 768 + ggi]
        for grp in range(3):
            for tap in range(4):
                pp[:, PP_CW + grp * 4 + tap] = inp["conv_w"][l][tap][cch[grp]]
            pp[:, PP_CB + grp] = inp["conv_b"][l][cch[grp]]
        m[f"pp{l}"] = pp
        pb = np.zeros((1, NPB), f32)
        pb[0, PB_SUB:PB_SUB + 128] = inp["subln_g"][l]
        pb[0, PB_LNG:PB_LNG + 128] = inp["lnx_g"][l][gi]
        pb[0, PB_LNB:PB_LNB + 128] = inp["lnx_b"][l][gi]
        pb[0, PB_GN:PB_GN + 128] = inp["gnorm_g"][l][gi]
        pb[0, PB_DTB:PB_DTB + 2] = inp["dt_bias"][l][2 * g:2 * g + 2]
        pb[0, PB_ALOG:PB_ALOG + 2] = inp["a_log"][l][2 * g:2 * g + 2]
        pb[0, PB_DSK:PB_DSK + 2] = inp["d_skip"][l][2 * g:2 * g + 2]
        pb[0, PB_LAM:PB_LAM + 64] = inp["lam_q1"][l]
        pb[0, PB_LAM + 64:PB_LAM + 128] = inp["lam_k1"][l]
        pb[0, PB_LAM + 128:PB_LAM + 192] = inp["lam_q2"][l]
        pb[0, PB_LAM + 192:PB_LAM + 256] = inp["lam_k2"][l]
        pb[0, PB_FG:PB_FG + 512] = inp["final_g"][g * 512:(g + 1) * 512]
        m[f"pb{l}"] = pb
        TPP = cfg.NPG
        m[f"ck{l}"] = np.ascontiguousarray(inp["cache_diff_k"][l][:, :, g]).reshape(cfg.NPOOL * (128 // TPP), TPP * 128)
        m[f"cv{l}"] = np.ascontiguousarray(inp["cache_diff_v"][l][:, :, g]).reshape(cfg.NPOOL * (128 // TPP), TPP * 128)
        m[f"cmk{l}"] = np.ascontiguousarray(inp["cache_mem_k"][l][sl, :, g, :])
        m[f"cmv{l}"] = np.ascontiguousarray(inp["cache_mem_v"][l][sl, :, g, :])
        srs = inp["state_rwkv_shift"][l][sl]
        sc = [gi, 512 + gi, 1024 + gi, 1536 + A(128)] + ([1024 + A(512)] if l == 1 else [])
        m[f"srs{l}"] = np.ascontiguousarray(srs[:, np.concatenate(sc)])
        m[f"srw{l}"] = np.ascontiguousarray(inp["state_rwkv"][l][sl, 2 * g:2 * g + 2])
        m[f"scv{l}"] = np.ascontiguousarray(inp["state_conv"][l][sl][:, :, np.concatenate(cch)])
        m[f"ssm{l}"] = np.ascontiguousarray(inp["state_ssm"][l][sl, 2 * g:2 * g + 2])
    return m


def assemble(res, cfg, B, DEC):
    T, NSH = cfg.T, cfg.NSH
    f32 = np.float32
    y_p = np.zeros((B, T, D), f32)
    y_s = np.zeros((DEC, 1, D), f32)
    kp = np.zeros((2, B, T, 4, 2, 64), f32)
    vp = np.zeros((2, B, T, 4, 128), f32)
    mk = np.zeros((2, B, NMEM, 4, 128), f32)
    mv = np.zeros((2, B, NMEM, 4, 128), f32)
    rsp = np.zeros((2, B, 1664), f32)
    rwp = np.zeros((2, B, 8, 64, 64), f32)
    cvp = np.zeros((2, B, 3, 1024), f32)
    ssp = np.zeros((2, B, 8, 64, 128), f32)
    ks = np.zeros((2, DEC, 1, 4, 2, 64), f32)
    vs = np.zeros((2, DEC, 1, 4, 128), f32)
    rss = np.zeros((2, DEC, 1664), f32)
    rws = np.zeros((2, DEC, 8, 64, 64), f32)
    cvs = np.zeros((2, DEC, 3, 1024), f32)
    sss = np.zeros((2, DEC, 8, 64, 128), f32)
    for core in range(8):
        b, g = core // 4, core % 4
        gg = g // 2
        r = res[core]
        sl = slice(b * NSH, (b + 1) * NSH)
        gs = slice(g * 128, (g + 1) * 128)
        y_p[b, :, g * 512:(g + 1) * 512] = r["o_y"][:T]
        y_s[sl, 0, g * 512:(g + 1) * 512] = r["o_y"][T:]
        for l in range(2):
            kp[l, b, :, g] = r[f"o_k{l}"][:T].reshape(T, 2, 64)
            vp[l, b, :, g] = r[f"o_v{l}"][:T]
            ks[l, sl, 0, g] = r[f"o_k{l}"][T:].reshape(NSH, 2, 64)
            vs[l, sl, 0, g] = r[f"o_v{l}"][T:]
            mk[l, b, :, g] = r[f"o_mk{l}"]
            mv[l, b, :, g] = r[f"o_mv{l}"]
            ors = r[f"o_rs{l}"]
            for grp in range(3):
                rsp[l, b, grp * 512 + g * 128:grp * 512 + (g + 1) * 128] = ors[grp, :, 0]
                rss[l, sl, grp * 512 + g * 128:grp * 512 + (g + 1) * 128] = ors[grp, :, 1:].T
            rsp[l, b, 1536:1664] = ors[3, :, 0]
            rss[l, sl, 1536:1664] = ors[3, :, 1:].T
            rwp[l, b, 2 * g:2 * g + 2] = r[f"o_rw{l}"][0]
            rws[l, sl, 2 * g:2 * g + 2] = r[f"o_rw{l}"][1:]
            ocv = r[f"o_cv{l}"]
            for (dst, src) in ((cvp[l, b], ocv[0]), (cvs[l, sl], ocv[1:])):
                dst[..., g * 128:(g + 1) * 128] = src[..., 0:128]
                dst[..., 512 + gg * 128:512 + (gg + 1) * 128] = src[..., 128:256]
                dst[..., 768 + gg * 128:768 + (gg + 1) * 128] = src[..., 256:384]
            ssp[l, b, 2 * g:2 * g + 2] = r[f"o_ss{l}"][0]
            sss[l, sl, 2 * g:2 * g + 2] = r[f"o_ss{l}"][1:]
    return (y_p, y_s, kp, vp, mk, mv, rsp, rwp, cvp, ssp, ks, vs, rss, rws, cvs, sss)


def run(inputs, stage=99, dbg=False):
    inputs = {k: np.asarray(v) for k, v in inputs.items()}
    B, T = inputs["x_prompt"].shape[:2]
    DEC = inputs["x_sample"].shape[0]
    NPG = inputs["page_table"].shape[1]
    NPOOL = inputs["cache_diff_k"].shape[1]
    cfg = Cfg(T, DEC // 2, NPG, NPOOL)
    P = build_program(cfg, stage=stage, dbg=dbg)
    in_maps = []
    for core in range(8):
        m = prepare_core(inputs, cfg, core)
        mm_ = {}
        for name, (shape, dt) in P.ins.items():
            a = np.ascontiguousarray(m[name])
            assert tuple(a.shape) == tuple(shape), (name, a.shape, shape)
            mm_[name] = a
        in_maps.append(mm_)
    res = run_bass_kernel_spmd(P.nc, in_maps, core_ids=list(range(8)))
    return assemble(res.results, cfg, B, DEC), res.results


def kernel(**inputs):
    outs, _ = run(inputs)
    return outs
```

```python
import contextlib
import numpy as np
import ml_dtypes
import concourse.bass as bass
import concourse.mybir as mybir
from concourse.bass_utils import run_bass_kernel_spmd

F32 = mybir.dt.float32
BF16 = mybir.dt.bfloat16
I32 = mybir.dt.int32
AF = mybir.ActivationFunctionType
ALU = mybir.AluOpType
AX = mybir.AxisListType
NPBF = ml_dtypes.bfloat16

D = 2048
NDC = 16
BR = 512
N_IN = 6792
NMEM = 256
RMS_EPS = 1e-5
LNX_EPS = 64e-5
DECAY = 0.6065306597
NEG = -30000.0

C_DQ, C_TM1, C_XQ, NCOL_AX = 0, 128, 640, 768
C_TM2, C_R, C_K, C_V, C_WA, C_XM, C_BM, C_CM, C_VF = 0, 258, 386, 514, 642, 770, 898, 1026, 1154
NCOL_BC0, NCOL_BC1 = 1154, 1666
TM_DK, TM_DV, TM_DG, TM_XG, TM_RG, TM_MZ, TM_DT = 0, 128, 256, 384, 512, 640, 768
NTM = 770
NMIX = 514


class Cfg:
    def __init__(self, T, NSH, NPG, NPOOL):
        self.T, self.NSH, self.NPG, self.NPOOL = T, NSH, NPG, NPOOL
        self.TB = 256
        self.NBLK = T // 256
        self.NTOK = T + NSH
        self.PAST = NPG * 128
        self.CHR = 640 if T >= 1024 else 256


ENG = ("pe", "act", "dve", "pool", "sp")
SEM_EPOCH = 20000
N_DMA_SLOTS = 32


class _Buf:
    __slots__ = ("last_w", "readers")

    def __init__(self):
        self.last_w = None
        self.readers = []


class _Op:
    __slots__ = ("eng", "idx", "fn", "waits", "is_dma", "slot", "slot_cnt", "inc")


class Sched:
    def __init__(self, nc):
        self.nc = nc
        self.ops = {e: [] for e in ENG}
        self.seen = {e: {} for e in ENG}
        self.ccount = {e: 0 for e in ENG}
        self.dma_cnt = [0] * N_DMA_SLOTS
        self.dma_rr = 0
        self.dma_rr_pool = 0
        self.bufs = {}

    def _b(self, name):
        b = self.bufs.get(name)
        if b is None:
            b = self.bufs[name] = _Buf()
        return b

    def _deps(self, reads, writes):
        deps = []
        for n in reads:
            b = self._b(n)
            if b.last_w is not None:
                deps.append(b.last_w)
        for n in writes:
            b = self._b(n)
            if b.last_w is not None:
                deps.append(b.last_w)
            deps.extend(b.readers)
        return deps

    def _commit(self, op, reads, writes):
        for n in reads:
            self._b(n).readers.append(op)
        for n in writes:
            b = self._b(n)
            b.last_w = op
            b.readers = []

    def _waits(self, eng, deps):
        need = {}
        for d in deps:
            if d.is_dma:
                key = ("d", d.slot)
                val = d.inc * d.slot_cnt
            else:
                if d.eng == "pe" and eng == "pe":
                    continue
                ep = d.idx // SEM_EPOCH
                key = ("e", d.eng, ep)
                val = d.idx - ep * SEM_EPOCH + 1
            if need.get(key, 0) < val:
                need[key] = val
        out = []
        seen = self.seen[eng]
        for key, val in need.items():
            if seen.get(key, 0) >= val:
                continue
            seen[key] = val
            out.append((key, val))
        return out

    def op(self, eng, fn, reads=(), writes=()):
        o = _Op()
        o.eng = eng
        o.idx = self.ccount[eng]
        self.ccount[eng] += 1
        o.fn = fn
        o.is_dma = False
        o.waits = self._waits(eng, self._deps(reads, writes))
        self.ops[eng].append(o)
        self._commit(o, reads, writes)
        return o

    def dma(self, q, fn, reads=(), writes=(), inc=16):
        o = _Op()
        o.eng = q
        o.fn = fn
        o.is_dma = True
        o.inc = inc
        o.idx = -1
        if inc == 16:
            if q == "pool":
                s = 20 + self.dma_rr_pool
                self.dma_rr_pool = (self.dma_rr_pool + 1) % 10
            else:
                s = self.dma_rr
                self.dma_rr = (self.dma_rr + 1) % 20
        else:
            s = N_DMA_SLOTS - 1
        o.slot = s
        waits = self._waits(q, self._deps(reads, writes))
        if self.dma_cnt[s] > 0:
            key = ("d", s)
            val = inc * self.dma_cnt[s]
            if self.seen[q].get(key, 0) < val:
                self.seen[q][key] = val
                waits.append((key, val))
        self.dma_cnt[s] += 1
        o.slot_cnt = self.dma_cnt[s]
        o.waits = waits
        self.ops[q].append(o)
        self._commit(o, reads, writes)
        return o

    def emit(self):
        nc = self.nc
        n_ep = {e: (self.ccount[e] // SEM_EPOCH + 1) for e in ENG}
        with contextlib.ExitStack() as st:
            esem = {}
            for e in ENG:
                for ep in range(n_ep[e]):
                    esem[(e, ep)] = st.enter_context(nc.semaphore(f"s_{e}_{ep}"))
            dsem = [st.enter_context(nc.semaphore(f"d_{i}")) for i in range(N_DMA_SLOTS)]
            block = st.enter_context(nc.Block())

            def semof(key):
                return dsem[key[1]] if key[0] == "d" else esem[(key[1], key[2])]

            def run(e, final=False):
                def body(engobj):
                    for o in self.ops[e]:
                        for key, val in o.waits:
                            engobj.wait_ge(semof(key), val)
                        ins = o.fn(engobj)
                        if o.is_dma:
                            if o.inc == 16:
                                ins.then_inc(dsem[o.slot], 16)
                            else:
                                ins.then_inc(dsem[o.slot])
                        else:
                            ins.then_inc(esem[(e, o.idx // SEM_EPOCH)], 1)
                    if final:
                        for s in range(N_DMA_SLOTS):
                            if self.dma_cnt[s] > 0:
                                inc = 16 if s != N_DMA_SLOTS - 1 else 1
                                engobj.wait_ge(dsem[s], inc * self.dma_cnt[s])
                return body

            block.tensor(run("pe"))
            block.scalar(run("act"))
            block.vector(run("dve"))
            block.gpsimd(run("pool"))
            block.sync(run("sp", final=True))


_REGIONS = {}
_ESZ = {F32: 4, BF16: 2, I32: 4}


def _names(*aps):
    out = []
    for a in aps:
        if a is None or isinstance(a, (int, float)):
            continue
        nm = a.tensor.name
        regs = _REGIONS.get(nm)
        if regs is None:
            out.append(nm)
            continue
        apl = a.ap
        esz = _ESZ[a.dtype]
        pstride = apl[0][0]
        off = int(a.offset) % pstride if pstride else int(a.offset)
        ext = sum(st_ * (cn_ - 1) for st_, cn_ in apl[1:]) + 1
        b0, b1 = off * esz, (off + ext) * esz
        inside = False
        for key, r0, r1 in regs:
            if b0 < r1 and b1 > r0:
                out.append(key)
            if r0 <= b0 and b1 <= r1:
                inside = True
        if not inside:
            out.append(nm)
    return out


class Prog:
    def __init__(self, cfg, dbg=False):
        self.cfg = cfg
        self.dbg = dbg
        self.nc = bass.Bass("TRN2", target_bir_lowering=False)
        self.S = Sched(self.nc)
        self.st = contextlib.ExitStack()
        self.ins = {}
        self.outs = {}
        self.ps_rr = 0
        self.act_rr = 0

    def inp(self, name, shape, dt=F32):
        t = self.nc.dram_tensor(name, list(shape), dt, kind="ExternalInput").ap()
        self.ins[name] = (tuple(shape), dt)
        return t

    def outp(self, name, shape, dt=F32):
        t = self.nc.dram_tensor(name, list(shape), dt, kind="ExternalOutput").ap()
        self.outs[name] = (tuple(shape), dt)
        return t

    def dram(self, name, shape, dt):
        return self.nc.dram_tensor(name, list(shape), dt).ap()

    def sb(self, name, shape, dt=F32):
        return self.st.enter_context(self.nc.sbuf_tensor(name, list(shape), dt))

    def mm(self, out, lhsT, rhs, start=True, stop=True):
        rd = _names(lhsT, rhs) + ([] if start else _names(out))
        self.S.op("pe", lambda e: e.matmul(out, lhsT=lhsT, rhs=rhs, start=start, stop=stop),
                  reads=rd, writes=_names(out))

    def tr(self, out, in_, ident):
        self.S.op("pe", lambda e: e.transpose(out=out, in_=in_, identity=ident),
                  reads=_names(in_, ident), writes=_names(out))

    def act(self, out, in_, func, bias=0.0, scale=1.0, accum_out=None):
        kw = {}
        if accum_out is not None:
            kw["accum_out"] = accum_out
        self.S.op("act", lambda e: e.activation(out=out, in_=in_, func=func, bias=bias, scale=scale, **kw),
                  reads=_names(in_, bias, scale), writes=_names(out, accum_out))

    def tt(self, out, in0, in1, op, eng="dve"):
        self.S.op(eng, lambda e: e.tensor_tensor(out=out, in0=in0, in1=in1, op=op),
                  reads=_names(in0, in1), writes=_names(out))

    def ts(self, out, in0, s1, op0, s2=None, op1=None, eng="dve", accum_out=None):
        kw = {}
        if accum_out is not None:
            kw["accum_out"] = accum_out
        if op1 is None:
            self.S.op(eng, lambda e: e.tensor_scalar(out=out, in0=in0, scalar1=s1, scalar2=None, op0=op0, **kw),
                      reads=_names(in0, s1), writes=_names(out, accum_out))
        else:
            self.S.op(eng, lambda e: e.tensor_scalar(out=out, in0=in0, scalar1=s1, scalar2=s2, op0=op0, op1=op1, **kw),
                      reads=_names(in0, s1, s2), writes=_names(out, accum_out))

    def stt(self, out, in0, scalar, in1, op0, op1):
        self.S.op("dve", lambda e: e.scalar_tensor_tensor(out=out, in0=in0, scalar=scalar, in1=in1, op0=op0, op1=op1),
                  reads=_names(in0, scalar, in1), writes=_names(out))

    def cp(self, out, in_, eng="dve"):
        if eng == "act":
            self.act(out, in_, AF.Copy)
        else:
            self.S.op(eng, lambda e: e.tensor_copy(out=out, in_=in_), reads=_names(in_), writes=_names(out))

    def red(self, out, in_, op=ALU.add, axis=AX.X):
        self.S.op("dve", lambda e: e.tensor_reduce(out=out, in_=in_, axis=axis, op=op),
                  reads=_names(in_), writes=_names(out))

    def recip(self, out, in_):
        self.S.op("dve", lambda e: e.reciprocal(out=out, in_=in_), reads=_names(in_), writes=_names(out))

    def memset(self, ap, val, eng="pool"):
        self.S.op(eng, lambda e: e.memset(ap, val), writes=_names(ap))

    def scan(self, out, d0, d1, init, op0, op1):
        self.S.op("dve", lambda e: e.tensor_tensor_scan(out=out, data0=d0, data1=d1, initial=init, op0=op0, op1=op1),
                  reads=_names(d0, d1, init), writes=_names(out))

    def ld(self, out, in_, q="sp", slow=False):
        kw = {"allow_slow_non_contiguous": True} if slow else {}
        self.S.dma(q, lambda e: e.dma_start(out=out, in_=in_, **kw), reads=_names(in_), writes=_names(out))

    def stq(self, out, in_, slow=False):
        self.ld(out, in_, q="pool", slow=slow)

    def gather(self, out, in_, idx_ap):
        self.S.dma("pool", lambda e: e.indirect_dma_start(
            out=out, out_offset=None, in_=in_, in_offset=bass.IndirectOffsetOnAxis(ap=idx_ap, axis=0)),
            reads=_names(in_, idx_ap), writes=_names(out))

    def allgather(self, out, in_, groups):
        self.S.dma("pool", lambda e: e.collective_compute(
            "AllGather", ALU.bypass, replica_groups=groups, ins=[in_.opt()], outs=[out.opt()]),
            reads=_names(in_), writes=_names(out), inc=1)

    def bank(self):
        b = self.PS[self.ps_rr]
        self.ps_rr = (self.ps_rr + 1) % 4
        return b

    def rsqrt(self, out, in_, mul, add):
        self.ts(out, in_, mul, ALU.mult, add, ALU.add)
        self.act(out, out, AF.Sqrt)
        self.recip(out, out)


PP_NG, PP_MG, PP_MU, PP_MUVF, PP_W0, PP_A0, PP_KK, PP_KA, PP_RK, PP_VR0, PP_CW, PP_CB, NPP = 0, 16, 32, 36, 40, 41, 42, 43, 44, 45, 46, 58, 64
PB_SUB, PB_LNG, PB_LNB, PB_GN, PB_DTB, PB_ALOG, PB_DSK, PB_LAM, PB_FG, NPB = 0, 128, 256, 384, 512, 514, 516, 518, 774, 1286


def build_program(cfg, stage=99, dbg=False):
    P = Prog(cfg, dbg)
    T, NSH, NTOK, NBLK, NPG = cfg.T, cfg.NSH, cfg.NTOK, cfg.NBLK, cfg.NPG
    NT = T // 128
    groups = [[0, 1, 2, 3], [4, 5, 6, 7]]
    sb, mm, tr, act, tt, ts, stt, cp, red = P.sb, P.mm, P.tr, P.act, P.tt, P.ts, P.stt, P.cp, P.red

    xp = P.inp("xp", [T, D])
    xs = P.inp("xs", [NSH, D])
    xres = P.inp("xres", [NTOK, 512])
    mem = P.inp("mem", [NMEM, D])
    pt = P.inp("pt", [NSH, NPG], I32)
    c_identb = P.inp("c_identb", [128, 128], BF16)
    c_identf = P.inp("c_identf", [128, 128])
    c_negtri = P.inp("c_negtri", [128, 128])
    c_negmask = P.inp("c_negmask", [128, 1024], BF16)
    c_alibl = P.inp("c_alibl", [3, 128], BF16)
    c_alibr = P.inp("c_alibr", [3, 512], BF16)
    c_mask4 = P.inp("c_mask4", [128, 512])
    c_maskl = P.inp("c_maskl", [128, 128])
    c_utri = P.inp("c_utri", [128, 128])
    c_bones = P.inp("c_bones", [128, 128])
    c_bind = P.inp("c_bind", [128, 2])
    c_ind2 = P.inp("c_ind2", [2, 128])
    c_sbias = P.inp("c_sbias", [128, 17])
    c_oiota = P.inp("c_oiota", [128, 1])
    c_abias = P.inp("c_abias", [128, 64])
    CHR = cfg.CHR
    chunks = [(c0, min(CHR, NTOK - c0)) for c0 in range(0, NTOK, CHR)]
    L = []
    for l in range(2):
        d = {}
        d["ncol_bc"] = NCOL_BC0 if l == 0 else NCOL_BC1
        d["winax"] = P.inp(f"winax{l}", [D, NCOL_AX])
        d["winbc"] = P.inp(f"winbc{l}", [D, d["ncol_bc"]])
        d["wout"] = P.inp(f"wout{l}", [D, 512])
        d["wmk"] = P.inp(f"wmk{l}", [D, 128])
        d["wmv"] = P.inp(f"wmv{l}", [D, 128])
        d["wua"] = P.inp(f"wua{l}", [128, 128])
        if l == 1:
            d["vra"] = P.inp("vra", [512, 32])
            d["vrb"] = P.inp("vrb", [32, 128])
        d["pp"] = P.inp(f"pp{l}", [128, NPP])
        d["pb"] = P.inp(f"pb{l}", [1, NPB])
        d["ck"] = P.inp(f"ck{l}", [cfg.NPOOL * (128 // NPG), NPG * 128])
        d["cv"] = P.inp(f"cv{l}", [cfg.NPOOL * (128 // NPG), NPG * 128])
        d["cmk"] = P.inp(f"cmk{l}", [NSH, 256, 128])
        d["cmv"] = P.inp(f"cmv{l}", [NSH, 256, 128])
        d["srs"] = P.inp(f"srs{l}", [NSH, 1024 if l == 1 else 512])
        d["srw"] = P.inp(f"srw{l}", [NSH, 2, 64, 64])
        d["scv"] = P.inp(f"scv{l}", [NSH, 3, 384])
        d["ssm"] = P.inp(f"ssm{l}", [NSH, 2, 64, 128])
        d["o_k"] = P.outp(f"o_k{l}", [NTOK, 128])
        d["o_v"] = P.outp(f"o_v{l}", [NTOK, 128])
        d["o_mk"] = P.outp(f"o_mk{l}", [NMEM, 128])
        d["o_mv"] = P.outp(f"o_mv{l}", [NMEM, 128])
        d["o_rs"] = P.outp(f"o_rs{l}", [4, 128, 1 + NSH])
        d["o_rw"] = P.outp(f"o_rw{l}", [1 + NSH, 2, 64, 64])
        d["o_cv"] = P.outp(f"o_cv{l}", [1 + NSH, 3, 384])
        d["o_ss"] = P.outp(f"o_ss{l}", [1 + NSH, 2, 64, 128])
        d["mixb"] = P.dram(f"mixb{l}", [NTOK, NMIX], BF16)
        d["mixg"] = [P.dram(f"mixg{l}_{ci}", [4 * cr, NMIX], BF16) for ci, (c0, cr) in enumerate(chunks)]
        L.append(d)
    o_y = P.outp("o_y", [NTOK, 512])
    x1s = P.dram("x1s", [NTOK, 512], F32)
    x1b = P.dram("x1b", [NTOK, 512], BF16)
    x1g = [P.dram(f"x1g_{ci}", [4 * cr, 512], BF16) for ci, (c0, cr) in enumerate(chunks)]
    ssqb = P.dram("ssqb", [NTOK, 2], F32)
    ssqg = P.dram("ssqg", [4 * NTOK, 2], F32)
    vfirst = P.dram("vfirst", [128, NTOK], F32)
    hTs = P.dram("hTs", [NBLK + 1, 128, NDC * 256], BF16)
    scr = P.dram("scr", [8, NSH, 128], F32)
    scrq = P.dram("scrq", [2, NSH, 128], F32)
    if P.dbg:
        dbg_mix = [P.outp(f"dbg_mix{l}", [NTOK, NMIX], BF16) for l in range(2)]

    P.PS = [P.st.enter_context(P.nc.psum_tensor(f"ps{i}", [128, 512], F32)) for i in range(8)]
    PS = P.PS
    identb = sb("identb", [128, 128], BF16)
    identf = sb("identf", [128, 128])
    negtri = sb("negtri", [128, 128])
    negmask = sb("negmask", [128, 1024], BF16)
    alibl = sb("alibl", [3, 128], BF16)
    alibr = sb("alibr", [3, 512], BF16)
    mask4 = sb("mask4", [128, 512])
    maskl = sb("maskl", [128, 128])
    utri = sb("utri", [128, 128])
    bones = sb("bones", [128, 128])
    bind = sb("bind", [128, 2])
    ind2 = sb("ind2", [2, 128])
    sbias = sb("sbias", [128, 17])
    oiota = sb("oiota", [128, 1])
    abias = sb("abias", [128, 64])
    onesf = sb("onesf", [128, 128])
    onesb = sb("onesb", [128, 128], BF16)
    for dst, src in ((identb, c_identb), (identf, c_identf), (negtri, c_negtri), (negmask, c_negmask),
                     (alibl, c_alibl), (alibr, c_alibr), (mask4, c_mask4), (maskl, c_maskl), (utri, c_utri),
                     (bones, c_bones), (bind, c_bind), (ind2, c_ind2), (sbias, c_sbias), (oiota, c_oiota), (abias, c_abias)):
        P.ld(dst[:], src[:, :])
    P.memset(onesf[:], 1.0)
    P.memset(onesb[:], 1.0)

    Wb = sb("Wb", [128, NDC, NCOL_BC1], BF16)
    wst2 = sb("wst2", [128, 1024])
    woutb = Wb[:, :, 0:512]
    wmkb = Wb[:, :, 0:128]
    wmvb = Wb[:, :, 128:256]
    wuab = sb("wuab", [128, 128], BF16)
    vrab = sb("vrab", [128, 4, 32], BF16)
    vrbb = sb("vrbb", [32, 128], BF16)
    pp = sb("pp", [128, NPP])
    pb = sb("pb", [128, NPB])
    ppd = sb("ppd", [128, 16])
    pbd = sb("pbd", [128, 8])
    gsub = sb("gsub", [128, 128])
    hT = sb("hT", [128, NDC, 256], BF16)
    xt = sb("xt", [128, D])
    xtb = sb("xtb", [128, D], BF16)
    xn = xtb
    st1 = sb("st1", [128, 8])
    TM = sb("TM", [128, 2, NTM])
    mixt = sb("mixt", [128, 2, NMIX], BF16)
    P.memset(mixt[:], 0.0)

    def load_w(src, ncol, dst_c0=0):
        k = 0
        for dc in range(NDC):
            c = 0
            while c < ncol:
                n = min(1024, ncol - c)
                stg = xt[:, 0:n] if k % 2 == 0 else wst2[:, 0:n]
                P.ld(stg, src[dc * 128:(dc + 1) * 128, c:c + n])
                cp(Wb[:, dc, dst_c0 + c:dst_c0 + c + n], stg, eng=("dve", "act", "pool")[k % 3])
                k += 1
                c += n

    def load_params(l):
        d = L[l]
        P.ld(wst2[:, :128], d["wua"][:, :])
        cp(wuab[:], wst2[:, :128])
        if l == 1:
            P.ld(wst2[:, 128:256].rearrange("p (c w) -> p c w", w=32), d["vra"].rearrange("(c p) w -> p c w", p=128))
            cp(vrab[:], wst2[:, 128:256].rearrange("p (c w) -> p c w", w=32))
            P.ld(wst2[:32, 256:384], d["vrb"][:, :])
            cp(vrbb[:], wst2[:32, 256:384])
        P.ld(pp[:], d["pp"][:, :])
        P.ld(pb[:], d["pb"][0:1, :].broadcast_to([128, NPB]))
        ts(ppd[:, 0:8], pp[:, PP_MU:PP_MU + 8], -1.0, ALU.mult, 1.0, ALU.add)
        ts(ppd[:, 8:9], pp[:, PP_KA:PP_KA + 1], -1.0, ALU.mult, 1.0, ALU.add)
        import math
        lam_init = 0.8 - 0.6 * math.exp(-0.3 * l)
        tt(xt[:, 0:64], pb[:, PB_LAM:PB_LAM + 64], pb[:, PB_LAM + 64:PB_LAM + 128], ALU.mult)
        tt(xt[:, 64:128], pb[:, PB_LAM + 128:PB_LAM + 192], pb[:, PB_LAM + 192:PB_LAM + 256], ALU.mult)
        red(st1[:, 0:2], xt[:, 0:128].rearrange("p (a b) -> p a b", b=64))
        act(st1[:, 0:2], st1[:, 0:2], AF.Exp)
        tt(pbd[:, 0:1], st1[:, 0:1], st1[:, 1:2], ALU.subtract)
        ts(pbd[:, 0:1], pbd[:, 0:1], lam_init, ALU.add)
        ts(pbd[:, 1:2], pbd[:, 0:1], -1.0, ALU.mult)
        ts(gsub[:], pb[:, PB_SUB:PB_SUB + 128], 1.0 - lam_init, ALU.mult)
        act(pbd[:, 2:4], pb[:, PB_ALOG:PB_ALOG + 2], AF.Exp)
        ts(pbd[:, 2:4], pbd[:, 2:4], -1.0, ALU.mult)

    def norm_transpose(src_f32, nrows, gcol, dstT, c0, from_bf16=None):
        if from_bf16 is not None:
            cp(xt[:nrows, :], from_bf16)
            src = xt
        else:
            src = src_f32
        act(xn[:nrows, :], src[:nrows, :], AF.Square, accum_out=st1[:nrows, 4:5])
        P.rsqrt(st1[:nrows, 5:6], st1[:nrows, 4:5], 1.0 / D, RMS_EPS)
        ts(xn[:nrows, :], src[:nrows, :], st1[:nrows, 5:6], ALU.mult)
        for half in range(2):
            bk = P.bank()
            bkb = bk[:].bitcast(BF16)
            for j in range(8):
                dc = half * 8 + j
                tr(bkb[:, j * 128:j * 128 + nrows], xn[:nrows, dc * 128:(dc + 1) * 128], identb[:nrows, :nrows])
            tt(dstT[:, half * 8:(half + 1) * 8, c0:c0 + nrows],
               bkb[:, :].rearrange("p (j t) -> p j t", t=128)[:, :, :nrows],
               pp[:, gcol + half * 8:gcol + half * 8 + 8].unsqueeze(2).broadcast_to([128, 8, nrows]), ALU.mult)

    kT_all = sb("kT_all", [128, T], BF16)
    Vaug = sb("Vaug", [128, NT, 132], BF16)
    qT = sb("qT", [128, 256], BF16)
    ET = [sb(f"ET{i}", [128, 512], BF16) for i in range(2)]
    mkT = sb("mkT", [128, 256], BF16)
    mvaug = sb("mvaug", [128, 2, 132], BF16)
    xqT = sb("xqT", [128, 256], BF16)
    EX = sb("EX", [128, 512], BF16)
    fin = sb("fin", [128, 6, 128])
    r2 = sb("r2", [128, 8])
    P.memset(Vaug[:], 1.0)
    P.memset(mvaug[:], 1.0)

    def proj_fm(c0, NB, ncols=128):
        bk = P.bank()
        for dc in range(NDC):
            mm(bk[:ncols, :NB], Wb[:, dc, c0:c0 + ncols], hT[:, dc, :NB], start=(dc == 0), stop=(dc == NDC - 1))
        return bk

    def proj_tm(ti, nr, c0, ncols, dst):
        done = 0
        while done < ncols:
            n = min(512, ncols - done)
            bk = P.bank()
            for dc in range(NDC):
                mm(bk[:nr, :n], hT[:, dc, ti * 128:ti * 128 + nr], Wb[:, dc, c0 + done:c0 + done + n],
                   start=(dc == 0), stop=(dc == NDC - 1))
            cp(dst[:nr, done:done + n], bk[:nr, :n], eng=("act" if (done // 512) % 2 == 0 else "dve"))
            done += n

    def attn_finalize(O0, O1, nr, gate, out):
        P.recip(r2[:nr, 0:1], O0[:nr, 128:129])
        P.recip(r2[:nr, 1:2], O1[:nr, 128:129])
        ts(r2[:nr, 1:2], r2[:nr, 1:2], pbd[:nr, 1:2], ALU.mult)
        ts(fin[:nr, 0, :], O0[:nr, 0:128], r2[:nr, 0:1], ALU.mult)
        stt(fin[:nr, 1, :], O1[:nr, 0:128], r2[:nr, 1:2], fin[:nr, 0, :], ALU.mult, ALU.add)
        attn_post(fin[:nr, 1, :], nr, gate, out)

    def attn_post(o, nr, gate, out):
        act(fin[:nr, 2, :], o, AF.Square, accum_out=r2[:nr, 2:3])
        P.rsqrt(r2[:nr, 3:4], r2[:nr, 2:3], 1.0 / 128, RMS_EPS)
        act(fin[:nr, 3, :], gate, AF.Silu)
        stt(fin[:nr, 2, :], o, r2[:nr, 3:4], gsub[:nr, :], ALU.mult, ALU.mult)
        tt(out, fin[:nr, 2, :], fin[:nr, 3, :], ALU.mult)

    def diff_attn_block(qb):
        for kt in range(2 * qb + 2):
            j = kt - 2 * qb
            bks = [P.bank(), P.bank()]
            for c in range(2):
                o_ = bks[c][:, 0:256]
                mm(o_, kT_all[c * 64:(c + 1) * 64, kt * 128:(kt + 1) * 128], qT[c * 64:(c + 1) * 64, :], start=True, stop=(j < 0))
                if j >= 0:
                    mm(o_, identb[:, :], negmask[:, j * 512 + c * 256:j * 512 + (c + 1) * 256], start=False, stop=True)
            E = ET[kt % 2]
            for qt in range(2):
                if j == 1 and qt == 0:
                    continue
                m = 2 * qb + qt - kt
                for c in range(2):
                    act(E[:, c * 256 + qt * 128:c * 256 + (qt + 1) * 128], bks[c][:, qt * 128:(qt + 1) * 128],
                        AF.Exp, bias=abias[:, m:m + 1])
            for qt in range(2):
                if j == 1 and qt == 0:
                    continue
                for c in range(2):
                    mm(PS[4 + 2 * qt + c][:, 0:129], E[:, c * 256 + qt * 128:c * 256 + (qt + 1) * 128],
                       Vaug[:, kt, 0:129], start=(kt == 0), stop=(kt == 2 * qb + qt))
        for qt in range(2):
            attn_finalize(PS[4 + 2 * qt], PS[5 + 2 * qt], 128, TM[:, qt, TM_DG:TM_DG + 128], mixt[:, qt, 0:128])

    def cross_attn_block(NB):
        bk = P.bank()
        for m in range(2):
            mm(bk[:, m * 256:m * 256 + NB], mkT[:, m * 128:(m + 1) * 128], xqT[:, :NB])
        act(EX[:], bk[:], AF.Exp)
        bk2 = P.bank()
        for qt in range(NB // 128):
            for m in range(2):
                mm(bk2[:, qt * 132:qt * 132 + 129], EX[:, m * 256 + qt * 128:m * 256 + (qt + 1) * 128],
                   mvaug[:, m, 0:129], start=(m == 0), stop=(m == 1))
        for qt in range(NB // 128):
            P.recip(r2[:, 4:5], bk2[:, qt * 132 + 128:qt * 132 + 129])
            act(fin[:, 4, :], TM[:, qt, TM_XG:TM_XG + 128], AF.Silu)
            stt(mixt[:, qt, 384:512], bk2[:, qt * 132:qt * 132 + 128], r2[:, 4:5], fin[:, 4, :], ALU.mult, ALU.mult)

    def memory_kv(l):
        d = L[l]
        for ti in range(2):
            P.ld(xt[:, :], mem[ti * 128:(ti + 1) * 128, :])
            norm_transpose(xt, 128, PP_MG, hT, ti * 128)
        for (wb_, o_dram, isk) in ((wmkb, d["o_mk"], True), (wmvb, d["o_mv"], False)):
            for ti in range(2):
                bk = P.bank()
                for dc in range(NDC):
                    mm(bk[:, :128], hT[:, dc, ti * 128:(ti + 1) * 128], wb_[:, dc, :], start=(dc == 0), stop=(dc == NDC - 1))
                cp(fin[:, 0, :], bk[:, :128], eng="act")
                P.stq(o_dram[ti * 128:(ti + 1) * 128, :], fin[:, 0, :])
                if not isk:
                    cp(mvaug[:, ti, 0:128], bk[:, :128])
            if isk:
                bk = P.bank()
                for dc in range(NDC):
                    mm(bk[:, :256], wb_[:, dc, :], hT[:, dc, :256], start=(dc == 0), stop=(dc == NDC - 1))
                act(mkT[:], bk[:, :256], AF.Copy, scale=1.0 / np.sqrt(128.0))
    xbc_pre = sb("xbc_pre", [128, 3, 3 + 256])
    cacc = sb("cacc", [128, 3, 256])
    xcT = sb("xcT", [128, 256])
    BcT = sb("BcT", [128, 256], BF16)
    CcT = sb("CcT", [128, 256], BF16)
    dtt = sb("dtt", [128, 2, 2])
    adt = sb("adt", [128, 2, 2])
    acs = sb("acs", [128, 2])
    adtb = sb("adtb", [128, 2, 128])
    Xtok = sb("Xtok", [128, 128])
    Xdtf = sb("Xdtf", [128, 128])
    Xdtb = sb("Xdtb", [128, 128], BF16)
    Xdec = sb("Xdec", [128, 128], BF16)
    Btok = sb("Btok", [128, 128], BF16)
    Dm = sb("Dm", [128, 2, 128])
    LT = sb("LT", [128, 2, 128])
    EA = sb("EA", [128, 2, 128])
    CE = sb("CE", [128, 2, 128], BF16)
    MT = sb("MT", [128, 2, 128], BF16)
    dcol = sb("dcol", [128, 4])
    ym = sb("ym", [128, 128])
    vv = sb("vv", [128, 128])
    szs = sb("szs", [128, 128])
    SsT = sb("SsT", [128, 2, 64])
    SsTb = sb("SsTb", [128, 2, 64], BF16)
    sso = sb("sso", [64, 2, 128])

    def conv_silu(taps, grp, NB, out):
        a = cacc[:, grp, :NB]
        ts(a, taps[3], pp[:, PP_CW + grp * 4 + 3:PP_CW + grp * 4 + 4], ALU.mult)
        for j in (2, 1, 0):
            stt(a, taps[j], pp[:, PP_CW + grp * 4 + j:PP_CW + grp * 4 + j + 1], a, ALU.mult, ALU.add)
        act(out, a, AF.Silu, bias=pp[:, PP_CB + grp:PP_CB + grp + 1])

    def softplus_dt(dst, src, nr):
        k = src.shape[1]
        tt(dst, src, pb[:nr, PB_DTB:PB_DTB + 2].unsqueeze(1).broadcast_to([nr, k, 2]), ALU.add)
        act(dst, dst, AF.Exp)
        act(dst, dst, AF.Ln, bias=1.0)

    def mamba_post(y_ps, xtok, nr, zgate, out, ssq_out):
        for h in range(2):
            stt(ym[:nr, h * 64:(h + 1) * 64], xtok[:nr, h * 64:(h + 1) * 64], pb[:nr, PB_DSK + h:PB_DSK + h + 1],
                y_ps[:nr, h * 64:(h + 1) * 64], ALU.mult, ALU.add)
        act(szs[:nr, :], zgate, AF.Silu)
        tt(vv[:nr, :], ym[:nr, :], szs[:nr, :], ALU.mult)
        act(szs[:nr, :], vv[:nr, :], AF.Square, accum_out=dcol[:nr, 2:3])
        tt(out, vv[:nr, :], pb[:nr, PB_GN:PB_GN + 128], ALU.mult)
        cp(ssq_out[:, 0:1], dcol[:nr, 2:3])
        cp(dcol[:nr, 3:4], ssq_out[:, 0:1])
        tt(dcol[:nr, 3:4], dcol[:nr, 2:3], dcol[:nr, 3:4], ALU.subtract)
        cp(ssq_out[:, 1:2], dcol[:nr, 3:4])

    def mamba_block(blk, NB):
        for grp, c0 in enumerate((C_XM, C_BM, C_CM)):
            bk = proj_fm(c0, NB)
            cp(xbc_pre[:, grp, 3:3 + NB], bk[:, :NB], eng=("act" if grp == 1 else "dve"))
        for grp, out in enumerate((xcT, BcT, CcT)):
            conv_silu([xbc_pre[:, grp, j:j + NB] for j in range(4)], grp, NB, out[:, :NB])
        cp(xbc_pre[:, :, 0:3], xbc_pre[:, :, NB:NB + 3])
        nch = NB // 128
        softplus_dt(dtt[:, :nch, :], TM[:, :nch, TM_DT:TM_DT + 2], 128)
        tt(adt[:, :nch, :], dtt[:, :nch, :], pbd[:, 2:4].unsqueeze(1).broadcast_to([128, nch, 2]), ALU.mult)
        for ci in range(nch):
            cs_ = slice(ci * 128, (ci + 1) * 128)
            bk = P.bank()
            tr(bk[:, 0:128], xcT[:, cs_], identf[:, :])
            cp(Xtok[:], bk[:, 0:128], eng="act")
            bkb = P.bank()
            bkbb = bkb[:].bitcast(BF16)
            tr(bkbb[:, 0:128], BcT[:, cs_], identb[:, :])
            cp(Btok[:], bkbb[:, 0:128])
            for h in range(2):
                ts(Xdtf[:, h * 64:(h + 1) * 64], Xtok[:, h * 64:(h + 1) * 64], dtt[:, ci, h:h + 1], ALU.mult)
            cp(Xdtb[:], Xdtf[:])
            bk1 = P.bank()
            mm(bk1[:, 0:2], utri[:, :], adt[:, ci, :])
            cp(acs[:], bk1[:, 0:2])
            bkA = P.bank()
            for h in range(2):
                cp(adtb[:, h, :], adt[:, ci, h:h + 1].broadcast_to([128, 128]))
                mm(bkA[:, h * 128:(h + 1) * 128], adtb[:, h, :], utri[:, :])
            bkG = P.bank()
            mm(bkG[:, 0:128], BcT[:, cs_], CcT[:, cs_])
            for h in range(2):
                ar = bkA[:, h * 128:(h + 1) * 128]
                stt(Dm[:, h, :], ar, acs[:, h:h + 1], negtri[:, :], ALU.subtract, ALU.add)
                act(LT[:, h, :], Dm[:, h, :], AF.Exp)
                act(EA[:, h, :], ar, AF.Exp)
                tt(CE[:, h, :], CcT[:, cs_], EA[:, h, :], ALU.mult)
                ts(dcol[:, h:h + 1], bkA[:, h * 128 + 127:h * 128 + 128], acs[:, h:h + 1], ALU.subtract)
                tt(MT[:, h, :], bkG[:, 0:128], LT[:, h, :], ALU.mult)
            act(dcol[:, 0:2], dcol[:, 0:2], AF.Exp)
            for h in range(2):
                ts(Xdec[:, h * 64:(h + 1) * 64], Xdtf[:, h * 64:(h + 1) * 64], dcol[:, h:h + 1], ALU.mult)
            bkY = P.bank()
            for h in range(2):
                mm(bkY[:, h * 64:(h + 1) * 64], MT[:, h, :], Xdtb[:, h * 64:(h + 1) * 64], start=True, stop=False)
                mm(bkY[:, h * 64:(h + 1) * 64], CE[:, h, :], SsTb[:, h, :], start=False, stop=True)
            mamba_post(bkY, Xtok, 128, TM[:, ci, TM_MZ:TM_MZ + 128], mixt[:, ci, 256:384], mixt[:, ci, 512:514])
            bkS = P.bank()
            for h in range(2):
                mm(bkS[:, h * 64:(h + 1) * 64], Btok[:, :], Xdec[:, h * 64:(h + 1) * 64])
            for h in range(2):
                act(dcol[:, 2:3], bkA[:, h * 128 + 127:h * 128 + 128], AF.Exp)
                stt(SsT[:, h, :], SsT[:, h, :], dcol[:, 2:3], bkS[:, h * 64:(h + 1) * 64], ALU.mult, ALU.add)
            cp(SsTb[:], SsT[:])

    def mamba_state_out(dst):
        for h in range(2):
            bk = P.bank()
            tr(bk[:64, 0:128], SsT[:, h, :], identf[:, :])
            cp(sso[:, h, :], bk[:64, 0:128])
        P.stq(dst.rearrange("h p n -> p h n"), sso[:])
    rsb = sb("rsb", [128, 8, 1 + 256])
    xs_ = sb("xs_", [128, 8, 256])
    wab = sb("wab", [128, 256], BF16)
    xsvfb = sb("xsvfb", [128, 4, 256], BF16)
    lo32 = sb("lo32", [32, 256], BF16)
    sgw = sb("sgw", [128, 256])
    asg = sb("asg", [128, 256])
    vsg = sb("vsg", [128, 256])
    vfb = sb("vfb", [128, 256])
    kkn = sb("kkn", [128, 256])
    kp = sb("kp", [128, 256])
    rtmp = sb("rtmp", [128, 256])
    rtmp2 = sb("rtmp2", [128, 256])
    rkf = sb("rkf", [128, 256])
    cs = sb("cs", [128, 256])
    csr = sb("csr", [128, 256])
    Pin = sb("Pin", [128, 256])
    Pex = sb("Pex", [128, 256])
    Pinv = sb("Pinv", [128, 256])
    AR = sb("AR", [128, 2 * 256], BF16)
    btl = sb("btl", [128, 256], BF16)
    ktl = sb("ktl", [128, 256], BF16)
    vbl = sb("vbl", [128, 256], BF16)
    rhoinv = sb("rhoinv", [128, 2])
    BKV = sb("BKV", [128, 3, 128], BF16)
    A4 = sb("A4", [128, 2, 512], BF16)
    PQ = [sb(f"PQ{i}", [128, 2, 2, 128], BF16) for i in range(2)]
    TTt = [sb(f"TTt{i}", [128, 2, 128], BF16) for i in range(2)]
    SrT = sb("SrT", [128, 64])
    S0s = sb("S0s", [128, 64])
    S0sb = sb("S0sb", [128, 64], BF16)
    tmpS = sb("tmpS", [128, 64])
    Wt = sb("Wt", [128, 128], BF16)
    Ut = sb("Ut", [128, 128], BF16)
    Yt = sb("Yt", [128, 128])
    yn = sb("yn", [128, 128])
    sqy = sb("sqy", [128, 128])
    sgr = sb("sgr", [128, 128])
    gst = sb("gst", [128, 12])
    rwo = sb("rwo", [64, 128])

    def rwkv_prep(l, NB, cur, prev, vf_cols):
        ngrp = 8 if l == 1 else 4
        for grp in range(ngrp):
            ts(xs_[:, grp, :NB], cur(grp), ppd[:, grp:grp + 1], ALU.mult)
            stt(xs_[:, grp, :NB], prev(grp), pp[:, PP_MU + grp:PP_MU + grp + 1], xs_[:, grp, :NB], ALU.mult, ALU.add)
        act(wab[0:64, :NB], xs_[0:64, 3, :NB], AF.Tanh)
        cp(wab[64:128, :NB], xs_[64:128, 3, :NB])
        bk = P.bank()
        mm(bk[:, :NB], wuab[0:64, :], wab[0:64, :NB])
        act(sgw[:, :NB], bk[:, :NB], AF.Sigmoid, bias=pp[:, PP_W0:PP_W0 + 1])
        bk = P.bank()
        mm(bk[:, :NB], wuab[64:128, :], wab[64:128, :NB])
        act(asg[:, :NB], bk[:, :NB], AF.Sigmoid, bias=pp[:, PP_A0:PP_A0 + 1])
        xv = xs_[:, 2, :NB]
        if l == 1:
            cp(xsvfb[:, :, :NB], xs_[:, 4:8, :NB])
            bk = P.bank()
            for c4 in range(4):
                mm(bk[:32, :NB], vrab[:, c4, :], xsvfb[:, c4, :NB], start=(c4 == 0), stop=(c4 == 3))
            cp(lo32[:, :NB], bk[:32, :NB])
            bk = P.bank()
            mm(bk[:, :NB], vrbb[:, :], lo32[:, :NB])
            act(vsg[:, :NB], bk[:, :NB], AF.Sigmoid, bias=pp[:, PP_VR0:PP_VR0 + 1])
            P.ld(vfb[:, :NB], vfirst[:, vf_cols])
            tt(rtmp[:, :NB], vfb[:, :NB], xv, ALU.subtract)
            tt(rtmp[:, :NB], rtmp[:, :NB], vsg[:, :NB], ALU.mult)
            tt(xv, xv, rtmp[:, :NB], ALU.add)
        else:
            P.stq(vfirst[:, vf_cols], xv)
        ts(rtmp[:, :NB], xs_[:, 1, :NB], pp[:, PP_KK:PP_KK + 1], ALU.mult)
        tt(rtmp2[:, :NB], rtmp[:, :NB], rtmp[:, :NB], ALU.mult)
        bk = P.bank()
        mm(bk[:, :NB], bones[:, :], rtmp2[:, :NB])
        P.rsqrt(rtmp2[:, :NB], bk[:, :NB], 1.0, 1e-24)
        tt(kkn[:, :NB], rtmp[:, :NB], rtmp2[:, :NB], ALU.mult)
        ts(rtmp[:, :NB], asg[:, :NB], pp[:, PP_KA:PP_KA + 1], ALU.mult, ppd[:, 8:9], ALU.add)
        tt(kp[:, :NB], xs_[:, 1, :NB], rtmp[:, :NB], ALU.mult)
        stt(rkf[:, :NB], xs_[:, 0, :NB], pp[:, PP_RK:PP_RK + 1], kp[:, :NB], ALU.mult, ALU.mult)

    def rwkv_post(y_ps, vtok, rk2, nr, gate, out):
        if y_ps is not None:
            cp(Yt[:nr, :], y_ps, eng="act")
        red(gst[:nr, 0:2], Yt[:nr, :].rearrange("p (h n) -> p h n", n=64))
        tt(sqy[:nr, :], Yt[:nr, :], Yt[:nr, :], ALU.mult)
        red(gst[:nr, 2:4], sqy[:nr, :].rearrange("p (h n) -> p h n", n=64))
        ts(gst[:nr, 4:6], gst[:nr, 0:2], 1.0 / 64, ALU.mult)
        tt(gst[:nr, 6:8], gst[:nr, 4:6], gst[:nr, 4:6], ALU.mult)
        stt(gst[:nr, 8:10], gst[:nr, 2:4], 1.0 / 64, gst[:nr, 6:8], ALU.mult, ALU.subtract)
        P.rsqrt(gst[:nr, 8:10], gst[:nr, 8:10], 1.0, LNX_EPS)
        for h in range(2):
            ts(yn[:nr, h * 64:(h + 1) * 64], Yt[:nr, h * 64:(h + 1) * 64], gst[:nr, 4 + h:5 + h], ALU.subtract,
               gst[:nr, 8 + h:9 + h], ALU.mult)
        tt(yn[:nr, :], yn[:nr, :], pb[:nr, PB_LNG:PB_LNG + 128], ALU.mult)
        tt(yn[:nr, :], yn[:nr, :], pb[:nr, PB_LNB:PB_LNB + 128], ALU.add)
        for h in range(2):
            stt(yn[:nr, h * 64:(h + 1) * 64], vtok[:nr, h * 64:(h + 1) * 64], rk2[:nr, h:h + 1],
                yn[:nr, h * 64:(h + 1) * 64], ALU.mult, ALU.add)
        act(sgr[:nr, :], gate, AF.Silu)
        tt(out, yn[:nr, :], sgr[:nr, :], ALU.mult)

    def rwkv_block(l, blk, NB):
        ngrp = 8 if l == 1 else 4
        cols = [C_R, C_K, C_V, C_WA] + ([C_VF + i * 128 for i in range(4)] if l == 1 else [])
        for grp in range(ngrp):
            bk = proj_fm(cols[grp], NB)
            cp(rsb[:, grp, 1:1 + NB], bk[:, :NB], eng=("act" if grp % 2 == 0 else "dve"))
        rwkv_prep(l, NB, lambda g_: rsb[:, g_, 1:1 + NB], lambda g_: rsb[:, g_, 0:NB], slice(blk * 256, blk * 256 + NB))
        cp(rsb[:, :ngrp, 0:1], rsb[:, :ngrp, NB:NB + 1])
        nch = NB // 128
        if stage < 3.3:
            return
        for ci in range(nch):
            c_ = slice(ci * 128, (ci + 1) * 128)
            P.scan(cs[:, c_], onesf[:, :128], sgw[:, c_], 0.0, ALU.mult, ALU.add)
            ts(csr[:, c_], cs[:, c_], cs[:, ci * 128 + 63:ci * 128 + 64], ALU.subtract)
        act(Pin[:, :NB], csr[:, :NB], AF.Exp, scale=-DECAY)
        act(Pinv[:, :NB], csr[:, :NB], AF.Exp, scale=DECAY)
        tt(rtmp[:, :NB], csr[:, :NB], sgw[:, :NB], ALU.subtract)
        act(Pex[:, :NB], rtmp[:, :NB], AF.Exp, scale=-DECAY)
        for ci in range(nch):
            c_ = slice(ci * 128, (ci + 1) * 128)
            stt(AR[:, ci * 256:ci * 256 + 128], kkn[:, c_], -1.0, Pex[:, c_], ALU.mult, ALU.mult)
            tt(AR[:, ci * 256 + 128:ci * 256 + 256], xs_[:, 0, c_], Pin[:, c_], ALU.mult)
            act(rhoinv[:, ci:ci + 1], cs[:, ci * 128 + 63:ci * 128 + 64], AF.Exp, scale=-DECAY)
        tt(rtmp[:, :NB], kkn[:, :NB], asg[:, :NB], ALU.mult)
        tt(btl[:, :NB], rtmp[:, :NB], Pinv[:, :NB], ALU.mult)
        tt(ktl[:, :NB], kp[:, :NB], Pinv[:, :NB], ALU.mult)
        cp(vbl[:, :NB], xs_[:, 2, :NB])
        if stage < 3.5:
            return
        for ci in range(nch):
            c_ = slice(ci * 128, (ci + 1) * 128)
            ar0 = ci * 256
            bkb = P.bank()
            bkbb = bkb[:].bitcast(BF16)
            tr(bkbb[:, 0:128], btl[:, c_], identb[:, :])
            tr(bkbb[:, 128:256], ktl[:, c_], identb[:, :])
            tr(bkbb[:, 256:384], vbl[:, c_], identb[:, :])
            cp(BKV[:].rearrange("p a b -> p (a b)"), bkbb[:, 0:384], eng="act")
            if stage < 3.52:
                continue
            for h in range(2):
                hs = slice(h * 64, (h + 1) * 64)
                bkA = P.bank()
                mm(bkA[:, 0:256], btl[hs, c_], AR[hs, ar0:ar0 + 256])
                mm(bkA[:, 256:512], ktl[hs, c_], AR[hs, ar0:ar0 + 256])
                tt(A4[:, h, :], bkA[:, :], mask4[:, :], ALU.mult)
            if stage < 3.54:
                continue
            for h in range(2):
                hs = slice(h * 64, (h + 1) * 64)
                bkL = P.bank()
                mm(bkL[:, 0:128], AR[hs, ar0:ar0 + 128], btl[hs, c_])
                tt(PQ[0][:, h, 1, :], bkL[:, 0:128], maskl[:, :], ALU.mult)
            cp(PQ[0][:, :, 0, :], A4[:, :, 0:128])
            tt(TTt[0][:], A4[:, :, 0:128], identb[:, :].unsqueeze(1).broadcast_to([128, 2, 128]), ALU.add)
            cur = 0
            if stage < 3.56:
                continue
            for k in range(6 if stage >= 3.6 else 1):
                nxt = 1 - cur
                bkQ = P.bank()
                for h in range(2):
                    mm(bkQ[:, h * 256:h * 256 + 128], PQ[cur][:, h, 1, :], PQ[cur][:, h, 0, :])
                    mm(bkQ[:, h * 256 + 128:h * 256 + 256], PQ[cur][:, h, 0, :], PQ[cur][:, h, 1, :])
                cp(PQ[nxt][:].rearrange("p h a t -> p (h a t)"), bkQ[:, :], eng="act")
                bkT = P.bank()
                for h in range(2):
                    mm(bkT[:, h * 128:(h + 1) * 128], PQ[nxt][:, h, 1, :], TTt[cur][:, h, :])
                tt(TTt[nxt][:].rearrange("p h t -> p (h t)"), bkT[:, 0:256], TTt[cur][:].rearrange("p h t -> p (h t)"), ALU.add)
                cur = nxt
            TTf = TTt[cur]
            if stage < 3.7:
                continue
            ts(S0s[:], SrT[:], rhoinv[:, ci:ci + 1], ALU.mult)
            cp(S0sb[:], S0s[:])
            for h in range(2):
                hs = slice(h * 64, (h + 1) * 64)
                bkW = P.bank()
                mm(bkW[:, 0:64], AR[hs, ar0:ar0 + 128], S0sb[hs, :], start=True, stop=False)
                mm(bkW[:, 0:64], A4[:, h, 256:384], BKV[:, 2, hs], start=False, stop=True)
                cp(Wt[:, hs], bkW[:, 0:64], eng=("act" if h == 0 else "dve"))
            bkU = P.bank()
            for h in range(2):
                hs = slice(h * 64, (h + 1) * 64)
                mm(bkU[:, hs], TTf[:, h, :], Wt[:, hs])
            cp(Ut[:], bkU[:, 0:128])
            for h in range(2):
                hs = slice(h * 64, (h + 1) * 64)
                bkY = P.bank()
                mm(bkY[:, 0:64], AR[hs, ar0 + 128:ar0 + 256], S0sb[hs, :], start=True, stop=False)
                mm(bkY[:, 0:64], A4[:, h, 128:256], Ut[:, hs], start=False, stop=False)
                mm(bkY[:, 0:64], A4[:, h, 384:512], BKV[:, 2, hs], start=False, stop=True)
                cp(Yt[:, hs], bkY[:, 0:64], eng=("act" if h == 0 else "dve"))
            bkS = P.bank()
            mm(bkS[:, 0:128], BKV[:, 0, :], Ut[:, :], start=True, stop=False)
            mm(bkS[:, 0:128], BKV[:, 1, :], BKV[:, 2, :], start=False, stop=True)
            for h in range(2):
                hs = slice(h * 64, (h + 1) * 64)
                tt(tmpS[hs, :], bkS[hs, hs], S0s[hs, :], ALU.add)
                ts(SrT[hs, :], tmpS[hs, :], Pin[hs, ci * 128 + 127:ci * 128 + 128], ALU.mult)
            if stage < 3.9:
                continue
            bkR = P.bank()
            mm(bkR[:, 0:2], rkf[:, c_], bind[:, :])
            cp(gst[:, 10:12], bkR[:, 0:2])
            rwkv_post(None, BKV[:, 2, :], gst[:, 10:12], 128, TM[:, ci, TM_RG:TM_RG + 128], mixt[:, ci, 128:256])

    def rwkv_state_out(dst):
        bk = P.bank()
        tr(bk[:64, 0:128], SrT[:, :], identf[:, :])
        cp(rwo[:], bk[:64, 0:128])
        P.stq(dst.rearrange("h v k -> v h k"), rwo[:].rearrange("v (h k) -> v h k", k=64))
    TPP = NPG
    NSL = TPP + 1
    BPP = 128 // TPP
    c_indj = P.inp("c_indj", [NPG, 128])
    indj = sb("indj", [NPG, 128])
    P.ld(indj[:], c_indj[:, :])
    pti = sb("pti", [NPG, NSH], I32)
    ptf = sb("ptf", [NPG, NSH])
    idxf = sb("idxf", [128, NSH])
    idx = sb("idx", [128, NSH], I32)
    P.ld(pti[:], pt.rearrange("s j -> j s"), slow=True)
    cp(ptf[:], pti[:])
    bk = P.bank()
    mm(bk[:, :NSH], indj[:, :], ptf[:, :])
    ts(idxf[:], bk[:, :NSH], float(BPP), ALU.mult, oiota[:, 0:1], ALU.add)
    cp(idx[:], idxf[:])

    wbf = Wb[:].rearrange("p a b -> p (a b)").bitcast(F32)
    _REGIONS.clear()
    _REGIONS["Wb"] = [("Wb#K", 8192 * 4, (8192 + NSL * 128) * 4), ("Wb#V", 10496 * 4, (10496 + NSL * 128) * 4),
                      ("Wb#K1", 0, NSL * 128 * 4), ("Wb#V1", 2304 * 4, (2304 + NSL * 128) * 4)]
    Kt1 = wbf[:, 0:NSL * 128].rearrange("p (s d) -> p s d", d=128)
    Vt1 = wbf[:, 2304:2304 + NSL * 128].rearrange("p (s d) -> p s d", d=128)
    Kt = wbf[:, 8192:8192 + NSL * 128].rearrange("p (s d) -> p s d", d=128)
    Vt = wbf[:, 10496:10496 + NSL * 128].rearrange("p (s d) -> p s d", d=128)
    Kx = sb("Kx", [128, 2, 128])
    Vx = sb("Vx", [128, 2, 128])
    qb_ = sb("qb_", [128, 2, 128])
    scs = sb("scs", [128, NSL, 2])
    Es = sb("Es", [128, NSL, 2])
    wg = sb("wg", [128, NSL + 2])
    sst = sb("sst", [128, 8])
    qtok = sb("qtok", [128, 2, 128])
    xbcT = sb("xbcT", [128, 384])
    tokS = xt[:, 1024:2048].rearrange("p (a b) -> p a b", b=128)

    def sample_attn(l):
        d = L[l]
        proj_tm(0, NSH, C_DQ, 128, qtok[:, 0, :])
        proj_tm(0, NSH, C_XQ, 128, qtok[:, 1, :])
        for t_ in (Kt, Vt, Kt1, Vt1):
            P.memset(t_[:, TPP, :], 0.0)
        ts(qtok[:NSH, 0, :], qtok[:NSH, 0, :], 0.125, ALU.mult)
        ts(qtok[:NSH, 1, :], qtok[:NSH, 1, :], float(1.0 / np.sqrt(128.0)), ALU.mult)
        P.stq(scrq.rearrange("q s d -> s q d"), qtok[:NSH, :, :])
        oA, oX = PS[6], PS[7]
        for s in range(NSH):
            Kc, Vc = (Kt, Vt) if s % 2 == 0 else (Kt1, Vt1)
            P.ld(qb_[:], scrq[:, s:s + 1, :].rearrange("q o d -> o q d").broadcast_to([128, 2, 128]))
            P.gather(Kc[:, 0:TPP, :].rearrange("p t d -> p (t d)"), d["ck"][:, :], idx[:, s:s + 1])
            P.ld(Kc[0:1, TPP, :], TM[s:s + 1, 0, TM_DK:TM_DK + 128])
            P.gather(Vc[:, 0:TPP, :].rearrange("p t d -> p (t d)"), d["cv"][:, :], idx[:, s:s + 1])
            P.ld(Vc[0:1, TPP, :], TM[s:s + 1, 0, TM_DV:TM_DV + 128])
            tt(Kc[:], Kc[:], qb_[:, 0, :].unsqueeze(1).broadcast_to([128, NSL, 128]), ALU.mult)
            red(scs[:].rearrange("p s c -> p (s c)"), Kc[:].rearrange("p s (c d) -> p (s c) d", d=64))
            tt(scs[:], scs[:], sbias[:, 0:NSL].unsqueeze(2).broadcast_to([128, NSL, 2]), ALU.add)
            act(Es[:], scs[:], AF.Exp)
            red(sst[:, 0:2], Es[:].rearrange("p s c -> p c s"))
            bk = P.bank()
            mm(bk[:, 0:2], onesf[:, :], sst[:, 0:2])
            P.recip(sst[:, 2:4], bk[:, 0:2])
            ts(sst[:, 3:4], sst[:, 3:4], pbd[:, 1:2], ALU.mult)
            ts(wg[:, 0:NSL], Es[:, :, 0], sst[:, 2:3], ALU.mult)
            stt(wg[:, 0:NSL], Es[:, :, 1], sst[:, 3:4], wg[:, 0:NSL], ALU.mult, ALU.add)
            for sl in range(NSL):
                mm(oA[:, s:s + 1], Vc[:, sl, :], wg[:, sl:sl + 1], start=(sl == 0), stop=(sl == NSL - 1))
            P.ld(Kx[:], d["cmk"][s].rearrange("(m p) d -> p m d", p=128))
            P.ld(Vx[:], d["cmv"][s].rearrange("(m p) d -> p m d", p=128))
            tt(Kx[:], Kx[:], qb_[:, 1, :].unsqueeze(1).broadcast_to([128, 2, 128]), ALU.mult)
            red(sst[:, 4:6], Kx[:])
            act(sst[:, 4:6], sst[:, 4:6], AF.Exp)
            red(sst[:, 6:7], sst[:, 4:6])
            bk = P.bank()
            mm(bk[:, 0:1], onesf[:, :], sst[:, 6:7])
            P.recip(sst[:, 7:8], bk[:, 0:1])
            ts(wg[:, NSL:NSL + 2], sst[:, 4:6], sst[:, 7:8], ALU.mult)
            for m in range(2):
                mm(oX[:, s:s + 1], Vx[:, m, :], wg[:, NSL + m:NSL + m + 1], start=(m == 0), stop=(m == 1))
        cp(fin[:, 5, :NSH], oA[:, :NSH])
        bk = P.bank()
        tr(bk[:NSH, 0:128], fin[:, 5, :NSH], identf[:, :])
        cp(fin[:NSH, 1, :], bk[:NSH, 0:128])
        attn_post(fin[:NSH, 1, :], NSH, TM[:NSH, 0, TM_DG:TM_DG + 128], mixt[:NSH, 0, 0:128])
        cp(fin[:, 5, :NSH], oX[:, :NSH])
        bk = P.bank()
        tr(bk[:NSH, 0:128], fin[:, 5, :NSH], identf[:, :])
        act(fin[:NSH, 4, :], TM[:NSH, 0, TM_XG:TM_XG + 128], AF.Silu)
        tt(mixt[:NSH, 0, 384:512], bk[:NSH, 0:128], fin[:NSH, 4, :], ALU.mult)

    def to_fm(src_tok, ncols, dst):
        for c in range(ncols):
            bk = P.bank()
            tr(bk[:, 0:NSH], src_tok[:NSH, c * 128:(c + 1) * 128], identf[:NSH, :NSH])
            cp(dst[:, c, :NSH], bk[:, 0:NSH], eng=("act" if c % 2 else "dve"))

    def to_tm(src_fm, dst):
        bk = P.bank()
        tr(bk[:NSH, 0:128], src_fm, identf[:, :])
        cp(dst, bk[:NSH, 0:128])

    SG = min(16, NSH)
    big = [wbf[:, i * 2048:i * 2048 + SG * 128].rearrange("p (s n) -> p s n", n=128) for i in range(4)]
    stok = xt[:, 0:1024]
    prevT = sb("prevT", [128, 9, NSH])
    curT = sb("curT", [128, 8, NSH])
    curM = sb("curM", [128, 3, NSH])

    def sample_proj_bc(l):
        proj_tm(0, NSH, C_XM, 384, xbcT[:, :])
        for grp, c0 in enumerate((C_XM, C_BM, C_CM)):
            bk = proj_fm(c0, NSH)
            cp(curM[:, grp, :], bk[:, :NSH])
        ngrp = 8 if l == 1 else 4
        cols = [C_R, C_K, C_V, C_WA] + ([C_VF + i * 128 for i in range(4)] if l == 1 else [])
        for grp in range(ngrp):
            bk = proj_fm(cols[grp], NSH)
            cp(curT[:, grp, :], bk[:, :NSH])
    ysT = sb("ysT", [128, NSH])
    dtS = sb("dtS", [128, 1, 2])
    dtT2 = sb("dtT2", [2, 2, NSH])
    dtF = sb("dtF", [128, 2, NSH])
    xdtS = sb("xdtS", [128, NSH])
    cvS = sb("cvS", [128, 3, NSH])
    wS = sb("wS", [128, NSH])
    bS = sb("bS", [128, NSH])
    aS = sb("aS", [128, NSH])

    def sample_mamba(l):
        d = L[l]
        for j in range(3):
            P.ld(stok[:NSH, 0:384], d["scv"][:, j, :])
            if j >= 1:
                P.stq(d["o_cv"][1:1 + NSH, j - 1, :], stok[:NSH, 0:384])
            for grp in range(3):
                bk = P.bank()
                tr(bk[:, 0:NSH], stok[:NSH, grp * 128:(grp + 1) * 128], identf[:NSH, :NSH])
                cp(prevT[:, grp * 3 + j, :], bk[:, 0:NSH])
        P.stq(d["o_cv"][1:1 + NSH, 2, :], xbcT[:NSH, :])
        for grp in range(3):
            conv_silu([prevT[:, grp * 3 + 0, :], prevT[:, grp * 3 + 1, :], prevT[:, grp * 3 + 2, :], curM[:, grp, :]],
                      grp, NSH, cvS[:, grp, :])
        softplus_dt(dtS[:NSH], TM[:NSH, 0:1, TM_DT:TM_DT + 2], NSH)
        tt(sst[:NSH, 0:2], dtS[:NSH, 0, :], pbd[:NSH, 2:4], ALU.mult)
        act(sst[:NSH, 0:2], sst[:NSH, 0:2], AF.Exp)
        for q_, src in enumerate((dtS[:NSH, 0, :], sst[:NSH, 0:2])):
            bk = P.bank()
            tr(bk[:2, 0:NSH], src, identf[:NSH, :NSH])
            cp(dtT2[:, q_, :], bk[:2, 0:NSH])
            bk = P.bank()
            mm(bk[:, 0:NSH], ind2[:, :], dtT2[:, q_, :])
            cp(dtF[:, q_, :], bk[:, 0:NSH])
        tt(xdtS[:], cvS[:, 0, :], dtF[:, 0, :], ALU.mult)
        to_tm(cvS[:, 1, :], tokS[:NSH, 0, :])
        to_tm(cvS[:, 2, :], tokS[:NSH, 1, :])
        P.stq(scr[0:2].rearrange("q s n -> s q n"), tokS[:NSH, 0:2, :])
        for s0 in range(0, NSH, SG):
            Sm, BB, CB, tmp = big
            P.ld(Sm[:], d["ssm"][s0:s0 + SG].rearrange("s h p n -> (h p) s n"))
            P.ld(BB[:], scr[0:1, s0:s0 + SG, :].broadcast_to([128, SG, 128]))
            P.ld(CB[:], scr[1:2, s0:s0 + SG, :].broadcast_to([128, SG, 128]))
            tt(Sm[:], Sm[:], dtF[:, 1, s0:s0 + SG].unsqueeze(2).broadcast_to([128, SG, 128]), ALU.mult)
            tt(tmp[:], BB[:], xdtS[:, s0:s0 + SG].unsqueeze(2).broadcast_to([128, SG, 128]), ALU.mult)
            tt(Sm[:], Sm[:], tmp[:], ALU.add)
            P.stq(d["o_ss"][1 + s0:1 + s0 + SG].rearrange("s h p n -> (h p) s n"), Sm[:])
            tt(tmp[:], Sm[:], CB[:], ALU.mult)
            red(ysT[:, s0:s0 + SG], tmp[:])
        to_tm(cvS[:, 0, :], tokS[:NSH, 2, :])
        bk = P.bank()
        tr(bk[:NSH, 0:128], ysT[:, :], identf[:, :])
        mamba_post(bk, tokS[:, 2, :], NSH, TM[:NSH, 0, TM_MZ:TM_MZ + 128], mixt[:NSH, 0, 256:384], mixt[:NSH, 0, 512:514])

    def sample_rwkv(l):
        d = L[l]
        ngrp = 8 if l == 1 else 4
        P.ld(stok[:NSH, 0:ngrp * 128], d["srs"][:, :])
        for grp in range(ngrp):
            bk = P.bank()
            tr(bk[:, 0:NSH], stok[:NSH, grp * 128:(grp + 1) * 128], identf[:NSH, :NSH])
            cp(prevT[:, grp, :], bk[:, 0:NSH])
        P.stq(d["o_rs"][:, :, 1:1 + NSH].rearrange("g p s -> p g s"), curT[:, 0:4, :])
        rwkv_prep(l, NSH, lambda g_: curT[:, g_, :], lambda g_: prevT[:, g_, :], slice(T, T + NSH))
        act(wS[:], sgw[:, :NSH], AF.Exp, scale=-DECAY)
        ts(aS[:], kkn[:, :NSH], -1.0, ALU.mult)
        tt(bS[:], kkn[:, :NSH], asg[:, :NSH], ALU.mult)
        for q_, src in enumerate((aS[:], wS[:], bS[:], kp[:, :NSH], xs_[:, 0, :NSH], xs_[:, 2, :NSH])):
            to_tm(src, tokS[:NSH, q_, :])
        P.stq(scr[2:8].rearrange("q s n -> s q n"), tokS[:NSH, 0:6, :])
        Sv = big[0][:].rearrange("p s (a k) -> p (s a) k", k=64)
        XB = big[1][:].rearrange("p s (a k) -> p (s a) k", k=64)
        tmp = big[2][:].rearrange("p s (a k) -> p (s a) k", k=64)
        SG2 = 2 * SG
        for s0 in range(0, NSH, SG2):
            ns = min(SG2, NSH - s0)
            P.ld(Sv[:, :ns, :], d["srw"][s0:s0 + ns].rearrange("s h v k -> (h v) s k"))

            def bcast(q_):
                for h in range(2):
                    P.ld(XB[h * 64:(h + 1) * 64, :ns, :],
                         scr[2 + q_:3 + q_, s0:s0 + ns, h * 64:(h + 1) * 64].broadcast_to([64, ns, 64]))
            bcast(0)
            tt(tmp[:, :ns, :], Sv[:, :ns, :], XB[:, :ns, :], ALU.mult)
            red(ysT[:, s0:s0 + ns], tmp[:, :ns, :])
            bcast(1)
            tt(Sv[:, :ns, :], Sv[:, :ns, :], XB[:, :ns, :], ALU.mult)
            bcast(2)
            tt(tmp[:, :ns, :], XB[:, :ns, :], ysT[:, s0:s0 + ns].unsqueeze(2).broadcast_to([128, ns, 64]), ALU.mult)
            tt(Sv[:, :ns, :], Sv[:, :ns, :], tmp[:, :ns, :], ALU.add)
            bcast(3)
            tt(tmp[:, :ns, :], XB[:, :ns, :], xs_[:, 2, s0:s0 + ns].unsqueeze(2).broadcast_to([128, ns, 64]), ALU.mult)
            tt(Sv[:, :ns, :], Sv[:, :ns, :], tmp[:, :ns, :], ALU.add)
            P.stq(d["o_rw"][1 + s0:1 + s0 + ns].rearrange("s h v k -> (h v) s k"), Sv[:, :ns, :])
            bcast(4)
            tt(tmp[:, :ns, :], Sv[:, :ns, :], XB[:, :ns, :], ALU.mult)
            red(ysT[:, s0:s0 + ns], tmp[:, :ns, :])
        bk = P.bank()
        tr(bk[:NSH, 0:128], ysT[:, :], identf[:, :])
        bkR = P.bank()
        mm(bkR[:NSH, 0:2], rkf[:, :NSH], bind[:, :])
        cp(gst[:NSH, 10:12], bkR[:NSH, 0:2])
        rwkv_post(bk[:NSH, 0:128], tokS[:NSH, 5, :], gst[:, 10:12], NSH, TM[:NSH, 0, TM_RG:TM_RG + 128], mixt[:NSH, 0, 128:256])
    rsbb = rsb[:].rearrange("p a b -> p (a b)").bitcast(BF16)
    mT = rsbb[:, 0:2048].rearrange("p (k t) -> p k t", t=128)
    mg = rsbb[:, 2048:2048 + 4 * NMIX].rearrange("p (r c) -> p r c", c=NMIX)
    sq8 = sb("sq8", [128, 4, 2])
    xr = xs_[:, 0:2, :].rearrange("p a b -> p (a b)")
    x1t = xs_[:, 2:4, :].rearrange("p a b -> p (a b)")
    x1tb = xsvfb[:, 0:2, :].rearrange("p a b -> p (a b)")
    sq4 = sb("sq4", [128, 4, 2])

    def tiles():
        out = [(i * 128, 128) for i in range(NT)]
        out.append((T, NSH))
        return out

    def phase_b(l):
        d = L[l]
        for ci, (c0, cr) in enumerate(chunks):
            P.allgather(d["mixg"][ci], d["mixb"][c0:c0 + cr, :], groups)
        for (r0, nr) in tiles():
            ci = r0 // CHR
            lr = r0 - ci * CHR
            P.ld(mg[:nr], d["mixg"][ci].rearrange("(r t) c -> t r c", r=4)[lr:lr + nr])
            cp(sq8[:nr], mg[:nr, :, 512:514])
            red(st1[:nr, 0:1], sq8[:nr].rearrange("p a b -> p (a b)"))
            P.rsqrt(st1[:nr, 1:2], st1[:nr, 0:1], 1.0 / 512, RMS_EPS)
            ts(mg[:nr, :, 256:384], mg[:nr, :, 256:384], st1[:nr, 1:2], ALU.mult)
            for half in range(2):
                bkb = P.bank()
                bkbb = bkb[:].bitcast(BF16)
                for j in range(8):
                    kc = half * 8 + j
                    r_, br_ = kc // 4, kc % 4
                    tr(bkbb[:, j * 128:j * 128 + nr], mg[:nr, r_, br_ * 128:(br_ + 1) * 128], identb[:nr, :nr])
                cp(mT[:, half * 8:(half + 1) * 8, :nr], bkbb[:, :].rearrange("p (j t) -> p j t", t=128)[:, :, :nr],
                   eng=("act" if half else "dve"))
            bk = P.bank()
            for kc in range(16):
                mm(bk[:nr, :], mT[:, kc, :nr], woutb[:, kc, :], start=(kc == 0), stop=(kc == 15))
            P.ld(xr[:nr], (xres if l == 0 else x1s)[r0:r0 + nr, :])
            tt(x1t[:nr], bk[:nr, :], xr[:nr], ALU.add)
            P.stq(x1s[r0:r0 + nr, :], x1t[:nr])
            if l == 0:
                cp(x1tb[:nr], x1t[:nr], eng="act")
                P.stq(x1b[r0:r0 + nr, :], x1tb[:nr])
            else:
                act(xr[:nr], x1t[:nr], AF.Square, accum_out=st1[:nr, 2:3])
                cp(st1[:nr, 3:4], st1[:nr, 2:3])
                P.stq(ssqb[r0:r0 + nr, :], st1[:nr, 2:4])
        if l == 0:
            for ci, (c0, cr) in enumerate(chunks):
                P.allgather(x1g[ci], x1b[c0:c0 + cr, :], groups)
        else:
            P.allgather(ssqg, ssqb, groups)
            sgv = ssqg.rearrange("(r t) c -> t r c", r=4)
            for (r0, nr) in tiles():
                P.ld(sq4[:nr], sgv[r0:r0 + nr])
                red(st1[:nr, 0:1], sq4[:nr, :, 0])
                P.rsqrt(st1[:nr, 1:2], st1[:nr, 0:1], 1.0 / D, RMS_EPS)
                P.ld(xr[:nr], x1s[r0:r0 + nr, :])
                stt(x1t[:nr], xr[:nr], st1[:nr, 1:2], pb[:nr, PB_FG:PB_FG + 512], ALU.mult, ALU.mult)
                P.stq(o_y[r0:r0 + nr, :], x1t[:nr])

    def load_x_tile(l, r0, nr):
        if l == 0:
            src = xp[r0:r0 + nr, :] if r0 < T else xs[r0 - T:r0 - T + nr, :]
            P.ld(xt[:nr, :], src)
            return None
        ci = r0 // CHR
        lr = r0 - ci * CHR
        P.ld(xtb[:nr, :].rearrange("p (r c) -> p r c", r=4), x1g[ci].rearrange("(r t) c -> t r c", r=4)[lr:lr + nr])
        return xtb[:nr, :]

    for l in range(2):
        d = L[l]
        load_params(l)
        load_w(d["wmk"], 128, 0)
        load_w(d["wmv"], 128, 128)
        memory_kv(l)
        load_w(d["winax"], NCOL_AX)
        for blk in range(NBLK):
            for ti in range(2):
                r0 = blk * 256 + ti * 128
                fb = load_x_tile(l, r0, 128)
                norm_transpose(xt, 128, PP_NG, hT, ti * 128, from_bf16=fb)
            P.stq(hTs[blk], hT[:].rearrange("p a b -> p (a b)"))
            for ti in range(2):
                r0 = blk * 256 + ti * 128
                proj_tm(ti, 128, C_TM1, 512, TM[:, ti, 0:512])
                P.stq(d["o_k"][r0:r0 + 128, :], TM[:, ti, TM_DK:TM_DK + 128])
                P.stq(d["o_v"][r0:r0 + 128, :], TM[:, ti, TM_DV:TM_DV + 128])
                cp(Vaug[:, 2 * blk + ti, 0:128], TM[:, ti, TM_DV:TM_DV + 128])
            bk = proj_fm(C_DQ, 256)
            act(qT[:], bk[:, :256], AF.Copy, scale=0.125)
            bk = proj_fm(C_TM1 + TM_DK, 256)
            cp(kT_all[:, blk * 256:(blk + 1) * 256], bk[:, :256])
            bk = proj_fm(C_XQ, 256)
            cp(xqT[:], bk[:, :256], eng="act")
            if stage >= 1:
                diff_attn_block(blk)
            if stage >= 2:
                cross_attn_block(256)
            mb = d["mixb"][blk * 256:(blk + 1) * 256, :].rearrange("(t p) c -> p t c", p=128)
            P.stq(mb[:, :, 0:128], mixt[:, :, 0:128])
            P.stq(mb[:, :, 384:512], mixt[:, :, 384:512])
        fb = load_x_tile(l, T, NSH)
        norm_transpose(xt, NSH, PP_NG, hT, 0, from_bf16=fb)
        P.stq(hTs[NBLK], hT[:].rearrange("p a b -> p (a b)"))
        proj_tm(0, NSH, C_TM1, 512, TM[:, 0, 0:512])
        P.stq(d["o_k"][T:T + NSH, :], TM[:NSH, 0, TM_DK:TM_DK + 128])
        P.stq(d["o_v"][T:T + NSH, :], TM[:NSH, 0, TM_DV:TM_DV + 128])
        if stage >= 5:
            sample_attn(l)
        P.stq(d["mixb"][T:T + NSH, 0:128], mixt[:NSH, 0, 0:128])
        P.stq(d["mixb"][T:T + NSH, 384:512], mixt[:NSH, 0, 384:512])
        load_w(d["winbc"], d["ncol_bc"])
        for t_ in (SsT, SsTb, SrT):
            P.memset(t_[:], 0.0)
        P.memset(xbc_pre[:, :, 0:3], 0.0)
        P.memset(rsb[:, :, 0:1], 0.0)
        for blk in range(NBLK):
            P.ld(hT[:].rearrange("p a b -> p (a b)"), hTs[blk])
            for ti in range(2):
                proj_tm(ti, 128, C_TM2, 258, TM[:, ti, 512:770])
            if stage >= 3:
                mamba_block(blk, 256)
            if stage > 3.1:
                rwkv_block(l, blk, 256)
            mb = d["mixb"][blk * 256:(blk + 1) * 256, :].rearrange("(t p) c -> p t c", p=128)
            P.stq(mb[:, :, 128:384], mixt[:, :, 128:384])
            P.stq(mb[:, :, 512:514], mixt[:, :, 512:514])
            if blk == NBLK - 1:
                proj_tm(1, 128, C_XM, 384, xbcT[:, :])
                P.stq(d["o_cv"][0, :, :], xbcT[125:128, :])
        if stage >= 3:
            mamba_state_out(d["o_ss"][0])
        if stage >= 4:
            rwkv_state_out(d["o_rw"][0])
            P.stq(d["o_rs"][:, :, 0:1].rearrange("g p o -> p g o"), rsb[:, 0:4, 0:1], slow=True)
        P.ld(hT[:].rearrange("p a b -> p (a b)"), hTs[NBLK])
        proj_tm(0, NSH, C_TM2, 258, TM[:, 0, 512:770])
        if stage >= 6:
            sample_proj_bc(l)
            sample_mamba(l)
        if stage >= 7:
            sample_rwkv(l)
        P.stq(d["mixb"][T:T + NSH, 128:384], mixt[:NSH, 0, 128:384])
        P.stq(d["mixb"][T:T + NSH, 512:514], mixt[:NSH, 0, 512:514])
        if P.dbg:
            for (r0, nr) in tiles():
                P.ld(mg[:nr, 0, :], d["mixb"][r0:r0 + nr, :])
                P.stq(dbg_mix[l][r0:r0 + nr, :], mg[:nr, 0, :])
        load_w(d["wout"], 512)
        phase_b(l)

    P.S.emit()
    return P


def _consts(cfg, g):
    slope = 2.0 ** (-8.0 * (g + 1) / 4.0)
    r = np.arange(128)[:, None]
    c = np.arange(128)[None, :]
    f32 = np.float32
    out = {}
    out["c_identb"] = (r == c).astype(NPBF)
    out["c_identf"] = (r == c).astype(f32)
    tri = np.where(c >= r, 0.0, NEG).astype(f32)
    out["c_negtri"] = tri
    nm = np.zeros((128, 2, 2, 2, 128), f32)
    nm[:, 0, :, 0, :] = tri[:, None, :]
    nm[:, 1, :, 0, :] = NEG
    nm[:, 1, :, 1, :] = tri[:, None, :]
    out["c_negmask"] = nm.reshape(128, 1024).astype(NPBF)
    al = np.ones((3, 128), f32)
    al[2] = np.arange(128)
    out["c_alibl"] = al.astype(NPBF)
    ar = np.zeros((3, 2, 2, 128), f32)
    ar[0] = (-slope * 128.0 * np.arange(2))[None, :, None]
    ar[1] = (-slope * np.arange(128))[None, None, :]
    ar[2] = slope
    out["c_alibr"] = ar.reshape(3, 512).astype(NPBF)
    out["c_abias"] = (slope * (np.arange(128)[:, None] - 128.0 * np.arange(64)[None, :])).astype(f32)
    su = (r < c).astype(f32)
    ui = (r <= c).astype(f32)
    out["c_mask4"] = np.concatenate([su, ui, su, ui], axis=1)
    out["c_maskl"] = (c < r).astype(f32)
    out["c_utri"] = (r <= c).astype(f32)
    out["c_bones"] = ((r // 64) == (c // 64)).astype(f32)
    bi = np.zeros((128, 2), f32)
    bi[:64, 0] = 1
    bi[64:, 1] = 1
    out["c_bind"] = bi
    out["c_ind2"] = np.ascontiguousarray(bi.T)
    TPP = cfg.NPG
    BPP = 128 // TPP
    sbias = np.full((128, 17), NEG, f32)
    p = np.arange(128)
    for ti in range(TPP):
        kpos = (p // BPP) * 128 + (p % BPP) * TPP + ti
        sbias[:, ti] = -slope * (cfg.PAST - kpos)
    sbias[0, TPP] = 0.0
    out["c_sbias"] = sbias
    out["c_oiota"] = (p % BPP).astype(f32)[:, None]
    ij = np.zeros((cfg.NPG, 128), f32)
    ij[p // BPP, p] = 1.0
    out["c_indj"] = ij
    return out


def _fm(v):
    return np.ascontiguousarray(v.reshape(-1, 128).T)


def prepare_core(inp, cfg, core):
    b, g = core // 4, core % 4
    gg = g // 2
    NSH, T = cfg.NSH, cfg.T
    sl = slice(b * NSH, (b + 1) * NSH)
    gs = slice(g * 128, (g + 1) * 128)
    f32 = np.float32
    m = dict(_consts(cfg, g))
    xp = inp["x_prompt"][b]
    xs = inp["x_sample"][sl, 0, :]
    m["xp"] = xp
    m["xs"] = xs
    m["xres"] = np.concatenate([xp[:, g * 512:(g + 1) * 512], xs[:, g * 512:(g + 1) * 512]], axis=0)
    m["mem"] = inp["mem_prompt"][b]
    m["pt"] = inp["page_table"][sl].astype(np.int32)
    A = np.arange
    o_dq, o_dk, o_dv, o_dg, o_rs, o_rg, o_mz, o_xbc, o_dt, o_xq, o_xg = 0, 512, 1024, 1536, 2048, 3712, 4224, 4736, 5760, 5768, 6280
    gi = g * 128 + A(128)
    ggi = gg * 128 + A(128)
    cols_ax = np.concatenate([o_dq + gi, o_dk + gi, o_dv + gi, o_dg + gi, o_xg + gi, o_xq + gi])
    for l in range(2):
        cols_bc = [o_rg + gi, o_mz + gi, o_dt + 2 * g + A(2), o_rs + gi, o_rs + 512 + gi, o_rs + 1024 + gi,
                   o_rs + 1536 + A(128), o_xbc + gi, o_xbc + 512 + ggi, o_xbc + 768 + ggi]
        if l == 1:
            cols_bc.append(o_rs + 1024 + A(512))
        cols_bc = np.concatenate(cols_bc)
        w = inp["w_in"][l]
        m[f"winax{l}"] = np.ascontiguousarray(w[:, cols_ax])
        m[f"winbc{l}"] = np.ascontiguousarray(w[:, cols_bc])
        rows = np.concatenate([br * 512 + r_ * 128 + A(128) for r_ in range(4) for br in range(4)])
        m[f"wout{l}"] = np.ascontiguousarray(inp["w_out"][l][rows][:, g * 512:(g + 1) * 512])
        m[f"wmk{l}"] = np.ascontiguousarray(inp["w_mk"][l][:, gs])
        m[f"wmv{l}"] = np.ascontiguousarray(inp["w_mv"][l][:, gs])
        m[f"wua{l}"] = np.concatenate([inp["w_up"][l][:, gs], inp["a_up"][l][:, gs]], axis=0)
        pp = np.zeros((128, NPP), f32)
        pp[:, PP_NG:PP_NG + 16] = _fm(inp["norm_g"][l])
        pp[:, PP_MG:PP_MG + 16] = _fm(inp["mem_norm_g"][l])
        mu = inp["shift_mu"][l]
        pp[:, PP_MU + 0] = mu[gi]
        pp[:, PP_MU + 1] = mu[512 + gi]
        pp[:, PP_MU + 2] = mu[1024 + gi]
        pp[:, PP_MU + 3] = mu[1536 + A(128)]
        pp[:, PP_MUVF:PP_MUVF + 4] = _fm(mu[1024:1536])
        pp[:, PP_W0] = inp["w0"][l][gi]
        pp[:, PP_A0] = inp["a0"][l][gi]
        pp[:, PP_KK] = inp["k_k"][l][gi]
        pp[:, PP_KA] = inp["k_a"][l][gi]
        pp[:, PP_RK] = inp["r_k"][l].reshape(-1)[gi]
        if l == 1:
            pp[:, PP_VR0] = inp["vres_0"][0][gi]
            m["vra"] = inp["vres_a"][0]
            m["vrb"] = np.ascontiguousarray(inp["vres_b"][0][:, gs])
        cch = [gi, 512 + ggi, 768 + ggi]
        for grp in range(3):
            for tap in range(4):
                pp[:, PP_CW + grp * 4 + tap] = inp["conv_w"][l][tap][cch[grp]]
            pp[:, PP_CB + grp] = inp["conv_b"][l][cch[grp]]
        m[f"pp{l}"] = pp
        pb = np.zeros((1, NPB), f32)
        pb[0, PB_SUB:PB_SUB + 128] = inp["subln_g"][l]
        pb[0, PB_LNG:PB_LNG + 128] = inp["lnx_g"][l][gi]
        pb[0, PB_LNB:PB_LNB + 128] = inp["lnx_b"][l][gi]
        pb[0, PB_GN:PB_GN + 128] = inp["gnorm_g"][l][gi]
        pb[0, PB_DTB:PB_DTB + 2] = inp["dt_bias"][l][2 * g:2 * g + 2]
        pb[0, PB_ALOG:PB_ALOG + 2] = inp["a_log"][l][2 * g:2 * g + 2]
        pb[0, PB_DSK:PB_DSK + 2] = inp["d_skip"][l][2 * g:2 * g + 2]
        pb[0, PB_LAM:PB_LAM + 64] = inp["lam_q1"][l]
        pb[0, PB_LAM + 64:PB_LAM + 128] = inp["lam_k1"][l]
        pb[0, PB_LAM + 128:PB_LAM + 192] = inp["lam_q2"][l]
        pb[0, PB_LAM + 192:PB_LAM + 256] = inp["lam_k2"][l]
        pb[0, PB_FG:PB_FG + 512] = inp["final_g"][g * 512:(g + 1) * 512]
        m[f"pb{l}"] = pb
        TPP = cfg.NPG
        m[f"ck{l}"] = np.ascontiguousarray(inp["cache_diff_k"][l][:, :, g]).reshape(cfg.NPOOL * (128 // TPP), TPP * 128)
        m[f"cv{l}"] = np.ascontiguousarray(inp["cache_diff_v"][l][:, :, g]).reshape(cfg.NPOOL * (128 // TPP), TPP * 128)
        m[f"cmk{l}"] = np.ascontiguousarray(inp["cache_mem_k"][l][sl, :, g, :])
        m[f"cmv{l}"] = np.ascontiguousarray(inp["cache_mem_v"][l][sl, :, g, :])
        srs = inp["state_rwkv_shift"][l][sl]
        sc = [gi, 512 + gi, 1024 + gi, 1536 + A(128)] + ([1024 + A(512)] if l == 1 else [])
        m[f"srs{l}"] = np.ascontiguousarray(srs[:, np.concatenate(sc)])
        m[f"srw{l}"] = np.ascontiguousarray(inp["state_rwkv"][l][sl, 2 * g:2 * g + 2])
        m[f"scv{l}"] = np.ascontiguousarray(inp["state_conv"][l][sl][:, :, np.concatenate(cch)])
        m[f"ssm{l}"] = np.ascontiguousarray(inp["state_ssm"][l][sl, 2 * g:2 * g + 2])
    return m


def assemble(res, cfg, B, DEC):
    T, NSH = cfg.T, cfg.NSH
    f32 = np.float32
    y_p = np.zeros((B, T, D), f32)
    y_s = np.zeros((DEC, 1, D), f32)
    kp = np.zeros((2, B, T, 4, 2, 64), f32)
    vp = np.zeros((2, B, T, 4, 128), f32)
    mk = np.zeros((2, B, NMEM, 4, 128), f32)
    mv = np.zeros((2, B, NMEM, 4, 128), f32)
    rsp = np.zeros((2, B, 1664), f32)
    rwp = np.zeros((2, B, 8, 64, 64), f32)
    cvp = np.zeros((2, B, 3, 1024), f32)
    ssp = np.zeros((2, B, 8, 64, 128), f32)
    ks = np.zeros((2, DEC, 1, 4, 2, 64), f32)
    vs = np.zeros((2, DEC, 1, 4, 128), f32)
    rss = np.zeros((2, DEC, 1664), f32)
    rws = np.zeros((2, DEC, 8, 64, 64), f32)
    cvs = np.zeros((2, DEC, 3, 1024), f32)
    sss = np.zeros((2, DEC, 8, 64, 128), f32)
    for core in range(8):
        b, g = core // 4, core % 4
        gg = g // 2
        r = res[core]
        sl = slice(b * NSH, (b + 1) * NSH)
        gs = slice(g * 128, (g + 1) * 128)
        y_p[b, :, g * 512:(g + 1) * 512] = r["o_y"][:T]
        y_s[sl, 0, g * 512:(g + 1) * 512] = r["o_y"][T:]
        for l in range(2):
            kp[l, b, :, g] = r[f"o_k{l}"][:T].reshape(T, 2, 64)
            vp[l, b, :, g] = r[f"o_v{l}"][:T]
            ks[l, sl, 0, g] = r[f"o_k{l}"][T:].reshape(NSH, 2, 64)
            vs[l, sl, 0, g] = r[f"o_v{l}"][T:]
            mk[l, b, :, g] = r[f"o_mk{l}"]
            mv[l, b, :, g] = r[f"o_mv{l}"]
            ors = r[f"o_rs{l}"]
            for grp in range(3):
                rsp[l, b, grp * 512 + g * 128:grp * 512 + (g + 1) * 128] = ors[grp, :, 0]
                rss[l, sl, grp * 512 + g * 128:grp * 512 + (g + 1) * 128] = ors[grp, :, 1:].T
            rsp[l, b, 1536:1664] = ors[3, :, 0]
            rss[l, sl, 1536:1664] = ors[3, :, 1:].T
            rwp[l, b, 2 * g:2 * g + 2] = r[f"o_rw{l}"][0]
            rws[l, sl, 2 * g:2 * g + 2] = r[f"o_rw{l}"][1:]
            ocv = r[f"o_cv{l}"]
            for (dst, src) in ((cvp[l, b], ocv[0]), (cvs[l, sl], ocv[1:])):
                dst[..., g * 128:(g + 1) * 128] = src[..., 0:128]
                dst[..., 512 + gg * 128:512 + (gg + 1) * 128] = src[..., 128:256]
                dst[..., 768 + gg * 128:768 + (gg + 1) * 128] = src[..., 256:384]
            ssp[l, b, 2 * g:2 * g + 2] = r[f"o_ss{l}"][0]
            sss[l, sl, 2 * g:2 * g + 2] = r[f"o_ss{l}"][1:]
    return (y_p, y_s, kp, vp, mk, mv, rsp, rwp, cvp, ssp, ks, vs, rss, rws, cvs, sss)


def run(inputs, stage=99, dbg=False):
    inputs = {k: np.asarray(v) for k, v in inputs.items()}
    B, T = inputs["x_prompt"].shape[:2]
    DEC = inputs["x_sample"].shape[0]
    NPG = inputs["page_table"].shape[1]
    NPOOL = inputs["cache_diff_k"].shape[1]
    cfg = Cfg(T, DEC // 2, NPG, NPOOL)
    P = build_program(cfg, stage=stage, dbg=dbg)
    in_maps = []
    for core in range(8):
        m = prepare_core(inputs, cfg, core)
        mm_ = {}
        for name, (shape, dt) in P.ins.items():
            a = np.ascontiguousarray(m[name])
            assert tuple(a.shape) == tuple(shape), (name, a.shape, shape)
            mm_[name] = a
        in_maps.append(mm_)
    res = run_bass_kernel_spmd(P.nc, in_maps, core_ids=list(range(8)))
    return assemble(res.results, cfg, B, DEC), res.results


def kernel(**inputs):
    outs, _ = run(inputs)
    return outs
```
